# Optimizing a Trainium2 kernel written in Bass

```python
import jax, jax.numpy as jnp
from jax import lax
import numpy as np

D_MODEL = 1024
BATCH = 8
SEQ = 4096
DEPTH = 2
DEC_BATCH = 16
DEC_SEQ = 2048
PAST_LEN = 128

GRID_W = 64
POOL_WIDTH = 512
POOL_GROUPS = 4
POOL_GROUP_DIM = POOL_WIDTH // POOL_GROUPS
POOL_WINDOWS = (2, 4, 8, 16)
MLSTM_HEADS = 4
MLSTM_HEAD_DIM = 128
MLSTM_WIDTH = MLSTM_HEADS * MLSTM_HEAD_DIM
MLSTM_CHUNK = 128
CONV_WIDTH = 3
N_DIRS = 2
IN0_SIZES = (POOL_WIDTH, POOL_WIDTH, MLSTM_WIDTH, MLSTM_WIDTH, MLSTM_WIDTH, MLSTM_WIDTH, N_DIRS * MLSTM_HEADS, N_DIRS * MLSTM_HEADS)
IN0_WIDTH = sum(IN0_SIZES)
IN0_SPLIT_POINTS = tuple(int(s) for s in np.cumsum(IN0_SIZES)[:-1])
OUT0_WIDTH = POOL_WIDTH + MLSTM_WIDTH
ATTN_HEADS = 8
KV_HEADS = 2
HEAD_DIM = 128
ATTN_WIDTH = ATTN_HEADS * HEAD_DIM
KV_WIDTH = KV_HEADS * HEAD_DIM
Q_BLOCK = 128
ROPE_THETA = 10000.0
AXIS_DIM = HEAD_DIM // 2
IN1_SIZES = (ATTN_WIDTH, KV_WIDTH, KV_WIDTH, ATTN_WIDTH)
IN1_WIDTH = sum(IN1_SIZES)
IN1_SPLIT_POINTS = tuple(int(s) for s in np.cumsum(IN1_SIZES)[:-1])
ALPHA = (2 * DEPTH) ** 0.25
BETA = (8 * DEPTH) ** -0.25
LN_EPS = 1e-5
RMS_EPS = 1e-6
M_INIT = -1e30
N_EVEN = (DEPTH + 1) // 2
N_ODD = DEPTH // 2

kernel_name = 'hybrid_pool_mlstm_axial_gqa_encoder'


def layer_norm(x, g, b):
    xf = x.astype(jnp.float32)
    mu = xf.mean(-1, keepdims=True)
    var = jnp.square(xf - mu).mean(-1, keepdims=True)
    return ((xf - mu) * lax.rsqrt(var + LN_EPS) * g + b).astype(x.dtype)


def rms_norm(x, g):
    xf = x.astype(jnp.float32)
    return (xf * lax.rsqrt(jnp.square(xf).mean(-1, keepdims=True) + RMS_EPS) * g).astype(x.dtype)


def centred_pool_mixer(xa, w_pool, pool_scale):
    B, T, _ = xa.shape
    xf = xa.astype(jnp.float32)
    cs = jnp.concatenate([jnp.zeros((B, 1, POOL_WIDTH), jnp.float32), jnp.cumsum(xf, axis=1)], axis=1)
    t = jnp.arange(T)
    groups = []
    for g, w in enumerate(POOL_WINDOWS):
        left = w // 2
        right = w - 1 - left
        lo = jnp.clip(t - left, 0, T)
        hi = jnp.clip(t + right + 1, 0, T)
        sl = slice(g * POOL_GROUP_DIM, (g + 1) * POOL_GROUP_DIM)
        seg = cs[:, :, sl]
        cnt = (hi - lo).astype(jnp.float32)
        mean = (seg[:, hi] - seg[:, lo]) / cnt[None, :, None]
        groups.append(mean - xf[:, :, sl])
    pooled = jnp.stack(groups, axis=2).astype(xa.dtype)
    mixed = jnp.einsum('btgc,gcd->btgd', pooled, w_pool)
    return mixed.reshape(B, T, POOL_WIDTH) * pool_scale


def centred_depthwise_conv(x, w, b):
    T = x.shape[1]
    pad = CONV_WIDTH // 2
    xp = jnp.pad(x, ((0, 0), (pad, CONV_WIDTH - 1 - pad), (0, 0)))
    return sum(xp[:, j:j + T] * w[j] for j in range(CONV_WIDTH)) + b


def mlstm_chunkwise(q, k, v, ig, lf):
    B, T, H, dh = q.shape
    L = MLSTM_CHUNK
    nc = T // L

    def chunks(a):
        return jnp.moveaxis(a.reshape((B, nc, L) + a.shape[2:]), 1, 0).swapaxes(2, 3)

    qc = chunks(q.astype(jnp.float32))
    kc = chunks(k.astype(jnp.float32) * (dh ** -0.5))
    vc = chunks(v.astype(jnp.float32))
    ic = chunks(ig)
    fc = chunks(lf)
    causal = jnp.tril(jnp.ones((L, L), dtype=bool))

    def step(carry, inp):
        C, n, m = carry
        qj, kj, vj, ij, fj = inp
        g = jnp.cumsum(fj, axis=-1)
        G = g[..., -1]
        dmat = jnp.where(causal, g[..., :, None] - g[..., None, :] + ij[..., None, :], -jnp.inf)
        inter = g + m[..., None]
        m_row = jnp.maximum(inter, dmat.max(-1))
        s = jnp.einsum('bhld,bhsd->bhls', qj, kj) * jnp.exp(dmat - m_row[..., None])
        ex = jnp.exp(inter - m_row)
        num = jnp.einsum('bhls,bhsd->bhld', s, vj) + ex[..., None] * jnp.einsum('bhvk,bhlk->bhlv', C, qj)
        den = s.sum(-1) + ex * jnp.einsum('bhk,bhlk->bhl', n, qj)
        h = num / jnp.maximum(jnp.abs(den), jnp.exp(-m_row))[..., None]
        w_s = G[..., None] - g + ij
        m_new = jnp.maximum(G + m, w_s.max(-1))
        decay = jnp.exp(G + m - m_new)
        ws = jnp.exp(w_s - m_new[..., None])
        C_new = decay[..., None, None] * C + jnp.einsum('bhs,bhsv,bhsk->bhvk', ws, vj, kj)
        n_new = decay[..., None] * n + jnp.einsum('bhs,bhsk->bhk', ws, kj)
        return (C_new, n_new, m_new), h

    init = (jnp.zeros((B, H, dh, dh), jnp.float32), jnp.zeros((B, H, dh), jnp.float32), jnp.full((B, H), M_INIT, jnp.float32))
    _, h = lax.scan(step, init, (qc, kc, vc, ic, fc))
    return jnp.moveaxis(h.swapaxes(2, 3), 0, 1).reshape(B, T, H, dh)


def even_mixer(x, w_in, w_pool, pool_scale, conv_w, conv_b, w_q, w_k, b_gate_i, b_gate_f, mh_norm_g, skip, w_out):
    B, T, _ = x.shape
    H, dh = MLSTM_HEADS, MLSTM_HEAD_DIM
    proj = x @ w_in
    xa, za, xb, vb, ob, zb, gi, gf = jnp.split(proj, IN0_SPLIT_POINTS, axis=-1)
    out_a = centred_pool_mixer(xa, w_pool, pool_scale) * jax.nn.silu(za)
    xc = jax.nn.silu(centred_depthwise_conv(xb, conv_w, conv_b))
    xch = xc.reshape(B, T, H, dh)
    q = jnp.einsum('bthd,hde->bthe', xch, w_q)
    k = jnp.einsum('bthd,hde->bthe', xch, w_k)
    v = vb.reshape(B, T, H, dh)
    ig = (gi.reshape(B, T, N_DIRS, H) + b_gate_i).astype(jnp.float32)
    lf = jax.nn.log_sigmoid((gf.reshape(B, T, N_DIRS, H) + b_gate_f).astype(jnp.float32))
    rev = lambda a: jnp.flip(a, axis=1)
    h_fwd = mlstm_chunkwise(q, k, v, ig[:, :, 0], lf[:, :, 0])
    h_bwd = rev(mlstm_chunkwise(rev(q), rev(k), rev(v), rev(ig[:, :, 1]), rev(lf[:, :, 1])))
    h = (h_fwd + h_bwd) * jax.nn.sigmoid(ob.astype(jnp.float32)).reshape(B, T, H, dh)
    mu = h.mean(-1, keepdims=True)
    var = jnp.square(h - mu).mean(-1, keepdims=True)
    hn = ((h - mu) * lax.rsqrt(var + LN_EPS)).reshape(B, T, MLSTM_WIDTH) * mh_norm_g
    out_b = (hn.astype(x.dtype) + skip * xc) * jax.nn.silu(zb)
    return jnp.concatenate([out_a, out_b], axis=-1) @ w_out


def axial_rope_tables(T):
    rows = T // GRID_W
    t_row = jnp.repeat(jnp.arange(rows, dtype=jnp.float32), GRID_W)
    t_col = jnp.tile(jnp.arange(GRID_W, dtype=jnp.float32), rows)
    n_freq = AXIS_DIM // 2
    inv = ROPE_THETA ** (-jnp.arange(n_freq, dtype=jnp.float32) / n_freq)
    ang = jnp.stack([t_row[:, None] * inv, t_col[:, None] * inv], axis=1)
    return jnp.cos(ang), jnp.sin(ang)


def apply_axial_rope(x, cos, sin):
    B, T, H, _ = x.shape
    n_freq = AXIS_DIM // 2
    xr = x.reshape(B, T, H, 2, 2, n_freq)
    x1, x2 = xr[..., 0, :], xr[..., 1, :]
    c = cos.astype(x.dtype)[None, :, None]
    s = sin.astype(x.dtype)[None, :, None]
    out = jnp.stack([x1 * c - x2 * s, x2 * c + x1 * s], axis=-2)
    return out.reshape(B, T, H, HEAD_DIM)


def blocked_attention(q, k, v):
    B, T = q.shape[:2]
    nblk = T // Q_BLOCK
    grp = ATTN_HEADS // KV_HEADS
    qb = q.reshape(B, nblk, Q_BLOCK, KV_HEADS, grp, HEAD_DIM).transpose(1, 0, 2, 3, 4, 5)
    scale = HEAD_DIM ** -0.5

    def one_block(qblk):
        s = jnp.einsum('bqkgd,bskd->bkgqs', qblk, k).astype(jnp.float32) * scale
        p = jax.nn.softmax(s, axis=-1).astype(v.dtype)
        return jnp.einsum('bkgqs,bskd->bqkgd', p, v)

    o = lax.map(one_block, qb)
    return o.transpose(1, 0, 2, 3, 4, 5).reshape(B, T, ATTN_WIDTH)


def odd_mixer(x, w_in, q_norm_g, k_norm_g, w_out):
    B, T, _ = x.shape
    proj = x @ w_in
    q, k, v, z = jnp.split(proj, IN1_SPLIT_POINTS, axis=-1)
    q = rms_norm(q.reshape(B, T, ATTN_HEADS, HEAD_DIM), q_norm_g)
    k = rms_norm(k.reshape(B, T, KV_HEADS, HEAD_DIM), k_norm_g)
    v = v.reshape(B, T, KV_HEADS, HEAD_DIM)
    cos, sin = axial_rope_tables(T)
    q = apply_axial_rope(q, cos, sin)
    k = apply_axial_rope(k, cos, sin)
    o = blocked_attention(q, k, v)
    return (o * jax.nn.silu(z)) @ w_out


def trunk(x, w_in_even, w_pool, pool_scale, conv_w, conv_b, w_q_m, w_k_m, b_gate_i, b_gate_f, mh_norm_g, skip, w_out_even, w_in_odd, q_norm_g, k_norm_g, w_out_odd, ln_g, ln_b):
    for layer in range(DEPTH):
        j = layer // 2
        if layer % 2 == 0:
            mix = even_mixer(x, w_in_even[j], w_pool[j], pool_scale[j], conv_w[j], conv_b[j], w_q_m[j], w_k_m[j], b_gate_i[j], b_gate_f[j], mh_norm_g[j], skip[j], w_out_even[j])
        else:
            mix = odd_mixer(x, w_in_odd[j], q_norm_g[j], k_norm_g[j], w_out_odd[j])
        x = layer_norm(ALPHA * x + mix, ln_g[layer], ln_b[layer])
    return x


def setup_inputs(seed: int = 0) -> dict:
    key = jax.random.key(seed)
    ks = jax.random.split(key, 24)
    f32 = jnp.float32
    nrm = lambda k, shape: jax.random.normal(k, shape, f32)
    return {
        'x_prompt': nrm(ks[0], (BATCH, SEQ, D_MODEL)),
        'x_sample': nrm(ks[1], (DEC_BATCH, DEC_SEQ, D_MODEL)),
        'w_in_even': nrm(ks[2], (N_EVEN, D_MODEL, IN0_WIDTH)) * D_MODEL ** -0.5,
        'w_pool': nrm(ks[3], (N_EVEN, POOL_GROUPS, POOL_GROUP_DIM, POOL_GROUP_DIM)) * POOL_GROUP_DIM ** -0.5,
        'pool_scale': 1.0 + 0.1 * nrm(ks[4], (N_EVEN, POOL_WIDTH)),
        'conv_w': nrm(ks[5], (N_EVEN, CONV_WIDTH, MLSTM_WIDTH)) * CONV_WIDTH ** -0.5,
        'conv_b': 0.02 * nrm(ks[6], (N_EVEN, MLSTM_WIDTH)),
        'w_q_m': nrm(ks[7], (N_EVEN, MLSTM_HEADS, MLSTM_HEAD_DIM, MLSTM_HEAD_DIM)) * MLSTM_HEAD_DIM ** -0.5,
        'w_k_m': nrm(ks[8], (N_EVEN, MLSTM_HEADS, MLSTM_HEAD_DIM, MLSTM_HEAD_DIM)) * MLSTM_HEAD_DIM ** -0.5,
        'b_gate_i': 0.1 * nrm(ks[9], (N_EVEN, N_DIRS, MLSTM_HEADS)),
        'b_gate_f': 3.0 + 3.0 * jax.random.uniform(ks[10], (N_EVEN, N_DIRS, MLSTM_HEADS), f32),
        'mh_norm_g': 1.0 + 0.02 * nrm(ks[11], (N_EVEN, MLSTM_WIDTH)),
        'skip': 1.0 + 0.1 * nrm(ks[12], (N_EVEN, MLSTM_WIDTH)),
        'w_out_even': nrm(ks[13], (N_EVEN, OUT0_WIDTH, D_MODEL)) * (OUT0_WIDTH ** -0.5 * BETA),
        'w_in_odd': nrm(ks[14], (N_ODD, D_MODEL, IN1_WIDTH)) * D_MODEL ** -0.5,
        'q_norm_g': 1.0 + 0.02 * nrm(ks[15], (N_ODD, HEAD_DIM)),
        'k_norm_g': 1.0 + 0.02 * nrm(ks[16], (N_ODD, HEAD_DIM)),
        'w_out_odd': nrm(ks[17], (N_ODD, ATTN_WIDTH, D_MODEL)) * (ATTN_WIDTH ** -0.5 * BETA),
        'ln_g': 1.0 + 0.02 * nrm(ks[18], (DEPTH, D_MODEL)),
        'ln_b': 0.02 * nrm(ks[19], (DEPTH, D_MODEL)),
    }


def reference(x_prompt, x_sample, w_in_even, w_pool, pool_scale, conv_w, conv_b, w_q_m, w_k_m, b_gate_i, b_gate_f, mh_norm_g, skip, w_out_even, w_in_odd, q_norm_g, k_norm_g, w_out_odd, ln_g, ln_b):
    y_prompt = trunk(x_prompt, w_in_even, w_pool, pool_scale, conv_w, conv_b, w_q_m, w_k_m, b_gate_i, b_gate_f, mh_norm_g, skip, w_out_even, w_in_odd, q_norm_g, k_norm_g, w_out_odd, ln_g, ln_b)
    y_sample = trunk(x_sample, w_in_even, w_pool, pool_scale, conv_w, conv_b, w_q_m, w_k_m, b_gate_i, b_gate_f, mh_norm_g, skip, w_out_even, w_in_odd, q_norm_g, k_norm_g, w_out_odd, ln_g, ln_b)
    return (y_prompt, y_sample)
```

```python
import math
from contextlib import ExitStack

import numpy as np
import concourse.bass as bass
import concourse.mybir as mybir
from concourse.bass_utils import run_bass_kernel_spmd

F32 = mybir.dt.float32
BF16 = mybir.dt.bfloat16
I32 = mybir.dt.int32
AF = mybir.ActivationFunctionType
ALU = mybir.AluOpType
AX = mybir.AxisListType

D = 1024
IN0 = 3088
IN1 = 2560
ALPHA = 4.0 ** 0.25
LN_EPS = 1e-5
RMS_EPS = 1e-6
POOL_WINDOWS = (2, 4, 8, 16)
DH = 128
NV = 129


class Buf:
    __slots__ = ("name", "w", "r")

    def __init__(self, name=""):
        self.name = name
        self.w = None
        self.r = {}


class Sched:
    def __init__(self, nc, stack):
        self.nc = nc
        self.stack = stack
        self.sem = {}
        self.cnt = {}
        self.known = {}
        self.prog = {}
        for e in ("pe", "dve", "act", "pool", "sp"):
            self.sem[e] = stack.enter_context(nc.semaphore("s_" + e))
            self.cnt[e] = 0
            self.known[e] = {}
            self.prog[e] = []

    def _waits(self, eng, reads, writes):
        need = {}
        for b in reads:
            if b.w is not None and need.get(b.w[0], 0) < b.w[1]:
                need[b.w[0]] = b.w[1]
        for b in writes:
            if b.w is not None and need.get(b.w[0], 0) < b.w[1]:
                need[b.w[0]] = b.w[1]
            for s, c in b.r.items():
                if need.get(s, 0) < c:
                    need[s] = c
        waits = []
        kn = self.known[eng]
        for s, c in need.items():
            if s == "pe" and eng == "pe":
                continue
            if kn.get(s, 0) < c:
                kn[s] = c
                waits.append((self.sem[s], c * (16 if s.startswith("dq_") else 1)))
        return waits

    def _mark(self, src, my, reads, writes):
        for b in reads:
            if b.r.get(src, 0) < my:
                b.r[src] = my
        for b in writes:
            b.w = (src, my)
            b.r = {}

    def op(self, eng, fn, reads=(), writes=()):
        waits = self._waits(eng, reads, writes)
        self.cnt[eng] += 1
        my = self.cnt[eng]
        sem = self.sem[eng]

        def emit(e):
            for s, v in waits:
                e.wait_ge(s, v)
            fn(e).then_inc(sem, 1)
        self.prog[eng].append(emit)
        self._mark(eng, my, reads, writes)

    def dma(self, eng, out, in_, key, reads=(), writes=(), slow=False):
        q = "dq_" + key
        if q not in self.sem:
            self.sem[q] = self.stack.enter_context(self.nc.semaphore("s_" + q))
            self.cnt[q] = 0
        waits = self._waits(eng, reads, writes)
        self.cnt[q] += 1
        my = self.cnt[q]
        sem = self.sem[q]

        def emit(e):
            for s, v in waits:
                e.wait_ge(s, v)
            if slow:
                e.dma_start(out=out, in_=in_, allow_slow_non_contiguous=True).then_inc(sem, 16)
            else:
                e.dma_start(out=out, in_=in_).then_inc(sem, 16)
        self.prog[eng].append(emit)
        self._mark(q, my, reads, writes)

    def barrier(self):
        targets = [(q, self.cnt[q]) for q in self.sem if self.cnt.get(q, 0) > 0]
        for eng in ("pe", "dve", "act", "pool", "sp"):
            waits = []
            for q, c in targets:
                if q == eng and eng in ("pe", "sp"):
                    continue
                if self.known[eng].get(q, 0) < c:
                    self.known[eng][q] = c
                    waits.append((self.sem[q], c * (16 if q.startswith("dq_") else 1)))

            def emit(e, waits=waits):
                for s_, v in waits:
                    e.wait_ge(s_, v)
            self.prog[eng].append(emit)

    def finish(self, eng):
        targets = [(self.sem[q], self.cnt[q] * 16) for q in self.sem if q.startswith("dq_") and self.cnt[q] > 0]
        targets += [(self.sem[e], self.cnt[e]) for e in ("pe", "dve", "act", "pool") if self.cnt[e] > 0]

        def emit(e):
            for s, v in targets:
                e.wait_ge(s, v)
        self.prog[eng].append(emit)

    def emit_all(self, block):
        progs = self.prog

        @block.sync
        def _(e):
            for f in progs["sp"]:
                f(e)

        @block.tensor
        def _(e):
            for f in progs["pe"]:
                f(e)

        @block.vector
        def _(e):
            for f in progs["dve"]:
                f(e)

        @block.scalar
        def _(e):
            for f in progs["act"]:
                f(e)

        @block.gpsimd
        def _(e):
            for f in progs["pool"]:
                f(e)


class K:
    def __init__(self, seq_lens, TC=2, layers=(0, 1), dbg=False):
        self.seq_lens = list(seq_lens)
        self.TC = TC
        self.W = 128 * TC
        self.layers = layers
        self.NT = sum(seq_lens)
        self.maxch = max(seq_lens) // 128
        self.nc = bass.Bass("TRN2", target_bir_lowering=False)
        self.st = ExitStack()
        self.cur = self.st
        self.uid = 0

    def sb(self, shape, dt=F32, name=None):
        self.uid += 1
        return self.cur.enter_context(self.nc.sbuf_tensor(name or ("t%d" % self.uid), list(shape), dt))

    def din(self, name, shape, dt=F32):
        return self.nc.dram_tensor(name, list(shape), dt, kind="ExternalInput").ap()

    def ACT(self, out, in_, func, R, Wr, bias=None, scale=None, accum=None):
        kw = {}
        if bias is not None:
            kw["bias"] = bias
        if scale is not None:
            kw["scale"] = scale
        if accum is not None:
            kw["accum_out"] = accum
        self.S.op("act", lambda e: e.activation(out=out, in_=in_, func=func, **kw), R, Wr)

    def TT(self, eng, out, in0, in1, op, R, Wr):
        self.S.op(eng, lambda e: e.tensor_tensor(out=out, in0=in0, in1=in1, op=op), R, Wr)

    def TS(self, eng, out, in0, s1, s2, op0, op1, R, Wr):
        if op1 is None:
            self.S.op(eng, lambda e: e.tensor_scalar(out=out, in0=in0, scalar1=s1, scalar2=None, op0=op0), R, Wr)
        else:
            self.S.op(eng, lambda e: e.tensor_scalar(out=out, in0=in0, scalar1=s1, scalar2=s2, op0=op0, op1=op1), R, Wr)

    def STT(self, out, in0, scalar, in1, op0, op1, R, Wr):
        self.S.op("dve", lambda e: e.scalar_tensor_tensor(out=out, in0=in0, scalar=scalar, in1=in1, op0=op0, op1=op1), R, Wr)

    def CP(self, eng, out, in_, R, Wr):
        if eng == "act":
            self.S.op("act", lambda e: e.activation(out=out, in_=in_, func=AF.Copy), R, Wr)
        else:
            self.S.op(eng, lambda e: e.tensor_copy(out=out, in_=in_), R, Wr)

    def MM(self, out, lhsT, rhs, start, stop, R, Wr, skip=False):
        if skip:
            self.S.op("pe", lambda e: e.matmul(out, lhsT=lhsT, rhs=rhs, start=start, stop=stop, skip_group_check=True), R, Wr)
        else:
            self.S.op("pe", lambda e: e.matmul(out, lhsT=lhsT, rhs=rhs, start=start, stop=stop), R, Wr)

    def TR(self, out, in_, ident, R, Wr):
        self.S.op("pe", lambda e: e.transpose(out=out, in_=in_, identity=ident), R, Wr)

    def MS(self, eng, out, val, R, Wr):
        self.S.op(eng, lambda e: e.memset(out, val), R, Wr)

    def DMA(self, eng, out, in_, key, R, Wr, slow=False):
        self.S.dma(eng, out, in_, key, R, Wr, slow=slow)

    def build(self):
        nc = self.nc
        st = self.st
        TC, W, NT = self.TC, self.W, self.NT
        d = {}
        d["x"] = self.din("x", [NT, D])
        d["w_in_even"] = self.din("w_in_even", [D, IN0])
        d["w_pool"] = self.din("w_pool", [4, 128, 128])
        d["pool_scale"] = self.din("pool_scale", [512])
        d["conv_w"] = self.din("conv_w", [3, 512])
        d["conv_b"] = self.din("conv_b", [512])
        d["w_q_m"] = self.din("w_q_m", [4, 128, 128])
        d["w_k_m"] = self.din("w_k_m", [4, 128, 128])
        d["b_gate_i"] = self.din("b_gate_i", [8])
        d["b_gate_f"] = self.din("b_gate_f", [8])
        d["mh_norm_g"] = self.din("mh_norm_g", [512])
        d["skip"] = self.din("skip", [512])
        d["w_out_even"] = self.din("w_out_even", [D, D])
        d["w_in_odd"] = self.din("w_in_odd", [D, IN1])
        d["q_norm_g"] = self.din("q_norm_g", [128])
        d["k_norm_g"] = self.din("k_norm_g", [128])
        d["w_out_odd"] = self.din("w_out_odd", [D, D])
        d["ln_g"] = self.din("ln_g", [2, D])
        d["ln_b"] = self.din("ln_b", [2, D])
        self.d = d
        self.y_d = nc.dram_tensor("y", [NT, D], F32, kind="ExternalOutput").ap()
        self.x1_d = nc.dram_tensor("x1s", [NT, D], F32, kind="Internal").ap()
        self.bst_d = nc.dram_tensor("bsts", [self.maxch, 128, 4 * NV], BF16, kind="Internal").ap()
        self.x1_bufs = {}
        self.bst_bufs = {}

        self.S = Sched(nc, st)
        self.ps = []
        self.psb = []
        for i in range(8):
            self.ps.append(st.enter_context(nc.psum_tensor("ps%d" % i, [128, 512], F32)))
            self.psb.append(Buf("ps%d" % i))
        self.conv_decl = 0
        self.bconv_decl = 0
        self.xloaded_tok = -1
        self.alloc_common()
        self.setup_consts()
        if 0 in self.layers:
            l0st = ExitStack()
            self.cur = l0st
            self.setup_consts_l0()
            self.load_weights(0)
            off = 0
            for T in self.seq_lens:
                self.layer0_seq(off, T)
                off += T
            self.S.barrier()
            l0st.close()
            self.cur = self.st
        if 1 in self.layers:
            self.load_weights(1)
            off = 0
            for T in self.seq_lens:
                self.layer1_seq(off, T)
                off += T
        self.S.finish("sp")
        block = st.enter_context(nc.Block())
        self.S.emit_all(block)
        st.close()
        return nc

    def setup_consts(self):
        nc = self.nc
        cb = Buf("consts")
        self.cb = cb
        R, Wr = [cb], [cb]
        self.ident_f = self.sb([128, 128], F32, "ident_f")
        self.ident_b = self.sb([128, 128], BF16, "ident_b")
        S = self.S
        self.MS("pool", self.ident_f[:], 0.0, [], Wr)
        S.op("pool", lambda e: e.affine_select(out=self.ident_f[:], in_=self.ident_f[:], pattern=[[-1, 128]],
                                               compare_op=ALU.not_equal, fill=1.0, base=0, channel_multiplier=1), R, Wr)
        self.CP("dve", self.ident_b[:], self.ident_f[:], R, Wr)
        self.lng = self.sb([128, D], F32, "lng")
        self.lnb = self.sb([128, D], F32, "lnb")
        self.lnbuf = Buf("ln")
        self.epsc = self.sb([128, 2], F32, "epsc")
        self.MS("pool", self.epsc[:, 0:1], LN_EPS, [], Wr)
        self.MS("pool", self.epsc[:, 1:2], RMS_EPS, [], Wr)
        self.qng = self.sb([128, 128], F32, "qng")
        self.kng = self.sb([128, 128], F32, "kng")
        self.DMA("sp", self.qng[:], self.d["q_norm_g"].partition_broadcast(128), "c0", [], Wr)
        self.DMA("sp", self.kng[:], self.d["k_norm_g"].partition_broadcast(128), "c0", [], Wr)
        self.w_in = self.sb([128, 8, IN0], BF16, "w_in")
        self.w_out = self.sb([128, 8, D], BF16, "w_out")
        self.wbuf = Buf("weights")
        self.wb_e = Buf("weights_early")

    def setup_consts_l0(self):
        cb = self.cb
        R, Wr = [cb], [cb]
        S = self.S
        self.triF = self.sb([128, 128], F32, "triF")
        self.triB = self.sb([128, 128], F32, "triB")
        self.ones_f = self.sb([128, 128], F32, "ones_f")
        self.maskF = self.sb([128, 128], BF16, "maskF")
        self.maskB = self.sb([128, 128], BF16, "maskB")
        self.MS("pool", self.ones_f[:], 1.0, [], Wr)
        S.op("pool", lambda e: e.affine_select(out=self.triF[:], in_=self.ones_f[:], pattern=[[1, 128]],
                                               compare_op=ALU.is_ge, fill=0.0, base=0, channel_multiplier=-1), R, Wr)
        S.op("pool", lambda e: e.affine_select(out=self.triB[:], in_=self.ones_f[:], pattern=[[-1, 128]],
                                               compare_op=ALU.is_ge, fill=0.0, base=0, channel_multiplier=1), R, Wr)
        self.CP("dve", self.maskF[:], self.triF[:], R, Wr)
        self.CP("dve", self.maskB[:], self.triB[:], R, Wr)
        self.bgi = self.sb([128, 8], F32, "bgi")
        self.bgf = self.sb([128, 8], F32, "bgf")
        self.DMA("sp", self.bgi[:], self.d["b_gate_i"].partition_broadcast(128), "c1", [], Wr)
        self.DMA("sp", self.bgf[:], self.d["b_gate_f"].partition_broadcast(128), "c1", [], Wr)
        self.pscale = self.sb([128, 4], F32, "pscale")
        self.convw = self.sb([128, 3, 4], F32, "convw")
        self.convb = self.sb([128, 4], F32, "convb")
        self.mhg = self.sb([128, 4], F32, "mhg")
        self.skipv = self.sb([128, 4], F32, "skipv")
        self.DMA("sp", self.pscale[:], self.d["pool_scale"].rearrange("(g p) -> p g", p=128), "c1", [], Wr, slow=True)
        self.DMA("sp", self.convb[:], self.d["conv_b"].rearrange("(g p) -> p g", p=128), "c1", [], Wr, slow=True)
        self.DMA("sp", self.mhg[:], self.d["mh_norm_g"].rearrange("(g p) -> p g", p=128), "c1", [], Wr, slow=True)
        self.DMA("sp", self.skipv[:], self.d["skip"].rearrange("(g p) -> p g", p=128), "c1", [], Wr, slow=True)
        self.DMA("sp", self.convw[:], self.d["conv_w"].rearrange("j (g p) -> p j g", p=128), "c1", [], Wr, slow=True)
        self.w_small = self.sb([128, 3, 4, 128], BF16, "w_small")
        self.setup_pool_bands()

    def setup_pool_bands(self):
        S = self.S
        cb = self.cb
        R, Wr = [cb], [cb]
        self.PB = self.sb([128, 4, 5, 128], BF16, "PB")
        R = [cb, self.res_b[0]]
        Wr = [cb, self.res_b[0]]
        ws = self.res[0][:]
        tmp, tmp2, iot, rc, t2 = (ws[:, 0:128], ws[:, 128:256], ws[:, 256:384], ws[:, 384:512], ws[:, 512:640])

        class _V:
            def __init__(self, ap):
                self.ap = ap

            def __getitem__(self, k):
                return self.ap
        tmp, tmp2, iot, rc, t2 = _V(tmp), _V(tmp2), _V(iot), _V(rc), _V(t2)
        S.op("pool", lambda e: e.iota(iot[:], pattern=[[1, 128]], base=0, channel_multiplier=0,
                                      allow_small_or_imprecise_dtypes=True), R, Wr)
        for g, w in enumerate(POOL_WINDOWS):
            left = w // 2
            right = w - 1 - left

            def band(dst, base_lo, base_hi, val):
                self.MS("pool", dst, val, R, Wr)
                S.op("pool", lambda e: e.affine_select(out=dst, in_=dst, pattern=[[-1, 128]], compare_op=ALU.is_ge,
                                                       fill=0.0, base=base_lo, channel_multiplier=1), R, Wr)
                S.op("pool", lambda e: e.affine_select(out=dst, in_=dst, pattern=[[1, 128]], compare_op=ALU.is_ge,
                                                       fill=0.0, base=base_hi, channel_multiplier=-1), R, Wr)
            band(tmp[:], left - 128, 10000, 1.0 / w)
            self.CP("dve", self.PB[:, g, 0, :], tmp[:], R, Wr)
            band(tmp[:], 10000, right - 128, 1.0 / w)
            self.CP("dve", self.PB[:, g, 4, :], tmp[:], R, Wr)
            band(tmp[:], left, right, 1.0)
            self.TS("dve", tmp2[:], tmp[:], 1.0 / w, None, ALU.mult, None, R, Wr)
            self.TT("dve", tmp2[:], tmp2[:], self.ident_f[:], ALU.subtract, R, Wr)
            self.CP("dve", self.PB[:, g, 2, :], tmp2[:], R, Wr)
            self.TS("dve", t2[:], iot[:], float(-left), 0.0, ALU.add, ALU.max, R, Wr)
            self.STT(rc[:], iot[:], float(right + 1), t2[:], ALU.add, ALU.subtract, R, Wr)
            S.op("dve", lambda e: e.reciprocal(out=rc[:], in_=rc[:]), R, Wr)
            self.TT("dve", tmp2[:], tmp[:], rc[:], ALU.mult, R, Wr)
            self.TT("dve", tmp2[:], tmp2[:], self.ident_f[:], ALU.subtract, R, Wr)
            self.CP("dve", self.PB[:, g, 1, :], tmp2[:], R, Wr)
            self.TS("dve", t2[:], iot[:], -1.0, 128.0, ALU.mult, ALU.add, R, Wr)
            self.TS("dve", rc[:], t2[:], float(right + 1), float(left), ALU.min, ALU.add, R, Wr)
            S.op("dve", lambda e: e.reciprocal(out=rc[:], in_=rc[:]), R, Wr)
            self.TT("dve", tmp2[:], tmp[:], rc[:], ALU.mult, R, Wr)
            self.TT("dve", tmp2[:], tmp2[:], self.ident_f[:], ALU.subtract, R, Wr)
            self.CP("dve", self.PB[:, g, 3, :], tmp2[:], R, Wr)

    def load_weights(self, layer):
        d = self.d
        wb = self.wbuf
        if layer == 0:
            win, ncol, wout = d["w_in_even"], IN0, d["w_out_even"]
        else:
            win, ncol, wout = d["w_in_odd"], IN1, d["w_out_odd"]
        we = self.wb_e
        win3 = win.rearrange("(k p) c -> p k c", p=128)
        wout3 = wout.rearrange("(k p) c -> p k c", p=128)

        def cols(c0, c1, key, buf):
            self.DMA("pool", self.w_in[:, :, c0:c1], win3[:, :, c0:c1], key, [], [buf])
        if layer == 0:
            cols(1024, 1536, "wle", we)
            cols(1536, 2048, "wle", we)
            cols(3072, 3088, "wle", we)
            for j, nm in enumerate(("w_pool", "w_q_m", "w_k_m")):
                self.DMA("pool", self.w_small[:, j, :, :], d[nm].rearrange("h d e -> d h e"), "wle", [], [we])
            cols(0, 1024, "wld", wb)
            cols(2048, 3072, "wld", wb)
        else:
            cols(1024, 1536, "wle", we)
            cols(0, 1024, "wld", wb)
            cols(1536, 2560, "wld", wb)
        self.DMA("pool", self.w_out[:], wout3, "wld", [], [wb])
        self.DMA("sp", self.lng[:], d["ln_g"][layer].partition_broadcast(128), "lnp", [], [self.lnbuf])
        self.DMA("sp", self.lnb[:], d["ln_b"][layer].partition_broadcast(128), "lnp", [], [self.lnbuf])

    def alloc_common(self):
        if hasattr(self, "xin"):
            return
        TC, W = self.TC, self.W
        self.xin = True
        self.xbf = [self.sb([128, TC, D], BF16, "xbf%d" % i) for i in range(2)]
        self.xbf_b = Buf("xbf")
        self.xbf_c = [[Buf() for _ in range(TC)] for _ in range(2)]
        self.xT = self.sb([128, 8, W], BF16, "xT")
        self.xT_b = Buf("xT")
        self.res = [self.sb([128, D], F32, "res%d" % i) for i in range(2)]
        self.res_b = [Buf("res0"), Buf("res1")]
        self.stat_l = [self.sb([128, 2, 6], F32, "stat%d" % i) for i in range(2)]
        self.mv_l = [self.sb([128, 8], F32, "mv%d" % i) for i in range(2)]
        self.stat_bl = [Buf("stat0"), Buf("stat1")]
        self.resn = 0

    def load_x(self, src_d, tok0, sl, src_bufs=None):
        TC, W = self.TC, self.W
        R = []
        if src_bufs is not None:
            for c in range(TC):
                R.append(src_bufs[(tok0 // 128) + c])
        self.DMA("pool", self.xbf[sl][:], src_d[tok0:tok0 + W, :].rearrange("(c p) d -> p c d", p=128),
                 "xbf%d" % sl, R, self.xbf_c[sl])

    def xT_chunk(self, c, bank, sl):
        pst = self.ps[bank][:].bitcast(BF16)
        for k in range(8):
            self.TR(pst[:, k * 128:(k + 1) * 128], self.xbf[sl][:, c, k * 128:(k + 1) * 128], self.ident_b[:],
                    [self.xbf_c[sl][c], self.cb], [self.psb[bank]])
        self.CP(("act", "dve")[c % 2], self.xT[:, :, c * 128:(c + 1) * 128],
                pst.rearrange("p (k t) -> p k t", k=8), [self.psb[bank]], [self.xT_b])

    def load_xT(self, src_d, tok0, sl, src_bufs=None):
        self.load_x(src_d, tok0, sl, src_bufs)
        for c in range(self.TC):
            self.xT_chunk(c, c % 2, sl)

    def resid_begin(self, src_d, tok, src_bufs=None):
        rs = self.resn % 2
        self.resn += 1
        R = [src_bufs[tok // 128]] if src_bufs is not None else []
        self.DMA("sp", self.res[rs][:], src_d[tok:tok + 128, :], "resx%d" % rs, R, [self.res_b[rs]])
        return rs

    def resid_half(self, rs, hf, bank):
        res, rb = self.res[rs], self.res_b[rs]
        self.STT(res[:, hf * 512:(hf + 1) * 512], res[:, hf * 512:(hf + 1) * 512], ALPHA,
                 self.ps[bank][:], ALU.mult, ALU.add, [rb, self.psb[bank]], [rb])

    def resid_finish_gen(self, rs, dst_d, tok, dst_bufs=None):
        res, rb = self.res[rs], self.res_b[rs]
        sb_ = self.stat_bl[rs]
        stat, mv = self.stat_l[rs], self.mv_l[rs]
        for hf in range(2):
            self.S.op("dve", lambda e, hf=hf: e.bn_stats(out=stat[:, hf, :], in_=res[:, hf * 512:(hf + 1) * 512]), [rb], [sb_])
        self.S.op("dve", lambda e: e.bn_aggr(out=mv[:, 0:2], in_=stat[:].rearrange("p a b -> p (a b)")), [sb_], [sb_])
        yield
        yield
        yield
        yield
        self.ACT(mv[:, 3:4], mv[:, 1:2], AF.Ln, [sb_], [sb_], bias=self.epsc[:, 0:1])
        self.ACT(mv[:, 4:5], mv[:, 3:4], AF.Exp, [sb_], [sb_], scale=-0.5)
        yield
        yield
        self.STT(res[:], res[:], mv[:, 0:1], self.lng[:], ALU.subtract, ALU.mult, [rb, sb_, self.lnbuf], [rb])
        self.STT(res[:], res[:], mv[:, 4:5], self.lnb[:], ALU.mult, ALU.add, [rb, sb_, self.lnbuf], [rb])
        yield
        yield
        yield
        yield
        yield
        yield
        Wr = []
        if dst_bufs is not None:
            b = dst_bufs.setdefault(tok // 128, Buf("dst%d" % (tok // 128)))
            Wr = [b]
        self.DMA("sp", dst_d[tok:tok + 128, :], res[:], "res%d" % rs, [rb], Wr)

    def resid_finish(self, rs, dst_d, tok, dst_bufs=None):
        for _ in self.resid_finish_gen(rs, dst_d, tok, dst_bufs):
            pass

    def resid_ln_store(self, ybanks, src_d, src_bufs, dst_d, tok, dst_bufs=None):
        rs = self.resid_begin(src_d, tok, src_bufs)
        for hf in range(2):
            self.resid_half(rs, hf, ybanks[hf])
        self.resid_finish(rs, dst_d, tok, dst_bufs)

    def alloc_l0(self):
        if hasattr(self, "xa_tm"):
            return
        TC, W = self.TC, self.W
        self.alloc_common()
        self.xa_tm = [self.sb([128, TC, 512], BF16, "xa%d" % i) for i in range(3)]
        self.xa_b = [Buf() for _ in range(3)]
        self.vaug = [self.sb([128, TC, 4, NV], BF16, "vaug%d" % i) for i in range(2)]
        self.vaug_b = [Buf() for _ in range(2)]
        self.obs = [self.sb([128, TC, 512], BF16, "obs%d" % i) for i in range(2)]
        self.obs_b = [Buf() for _ in range(2)]
        self.zas = [self.sb([128, 4, W], BF16, "zas%d" % i) for i in range(2)]
        self.zas_b = [Buf() for _ in range(2)]
        self.zbs = [self.sb([128, 4, W], BF16, "zbs%d" % i) for i in range(2)]
        self.zbs_b = [Buf() for _ in range(2)]
        self.xbT = [self.sb([128, 4, W + 2], F32, "xbT%d" % i) for i in range(2)]
        self.xbT_b = [Buf() for _ in range(2)]
        self.gig = [self.sb([128, TC, 8], F32, "gig%d" % i) for i in range(2)]
        self.gnf = [self.sb([128, TC, 8], F32, "gnf%d" % i) for i in range(2)]
        self.gate_b = [Buf() for _ in range(2)]
        self.gtmp = self.sb([128, TC, 8], F32, "gtmp")
        self.gtmp_b = Buf()
        self.ctmp = [self.sb([128, W], F32, "ctmp%d" % i) for i in range(2)]
        self.ctmp_b = [Buf(), Buf()]
        self.xc = self.sb([128, 4, W], F32, "xc")
        self.xcb_l = [self.sb([128, 4, W], BF16, "xcb%d" % i) for i in range(2)]
        self.skx_l = [self.sb([128, 4, W], F32, "skx%d" % i) for i in range(2)]
        self.xc_b = Buf()
        self.xcf_b = [Buf() for _ in range(4)]
        self.xcb_bl = [[Buf() for _ in range(4)] for _ in range(2)]
        self.skx_bl = [Buf(), Buf()]
        self.qT_l = [self.sb([128, 4, W], BF16, "qT%d" % i) for i in range(2)]
        self.kT_l = [self.sb([128, 4, W], BF16, "kT%d" % i) for i in range(2)]
        self.qk_bl = [Buf(), Buf()]
        P2 = range(2)
        self.kp = [self.sb([128, 4, 128], BF16, "kp%d" % i) for i in P2]
        self.kp_b = [Buf() for _ in P2]
        self.vp = [[self.sb([128, 4, NV], BF16, "vp%d_%d" % (i, j)) for j in range(2)] for i in P2]
        self.vp_b = [[Buf(), Buf()] for _ in P2]
        self.avec_l = [self.sb([128, TC, 8], F32, "avec%d" % i) for i in range(2)]
        self.bvec_l = [self.sb([128, TC, 8], F32, "bvec%d" % i) for i in range(2)]
        self.egv_l = [self.sb([128, TC, 8], F32, "egv%d" % i) for i in range(2)]
        self.gv_bl = [Buf(), Buf()]
        self.pT = [self.sb([128, 2, 4, 128], BF16, "pT%d" % i) for i in P2]
        self.pT_b = [Buf() for _ in P2]
        self.nd = [self.sb([128, 8, 128], F32, "nd%d" % i) for i in P2]
        self.nd_b = [Buf() for _ in P2]
        self.dd = [self.sb([128, 24], F32, "dd%d" % i) for i in P2]
        self.dd_b = [Buf() for _ in P2]
        self.hs = [self.sb([128, 4, 128], F32, "hs%d" % i) for i in P2]
        self.hs2 = [self.sb([128, 4, 128], F32, "hs2%d" % i) for i in P2]
        self.hs_b = [Buf() for _ in P2]
        self.hs2_b = [Buf() for _ in P2]
        self.hstat = [self.sb([128, 4, 6], F32, "hstat%d" % i) for i in P2]
        self.hmv = [self.sb([128, 4, 2], F32, "hmv%d" % i) for i in P2]
        self.hsc = [self.sb([128, 3, 4], F32, "hsc%d" % i) for i in P2]
        self.hst_b = [Buf() for _ in P2]
        self.obT = self.sb([128, 4, 128], F32, "obT")
        self.obT_b = Buf()
        self.outbT = [self.sb([128, 4, 128], BF16, "outbT%d" % i) for i in P2]
        self.outbT_b = [Buf() for _ in P2]
        self.pooledT = [self.sb([128, 4, 128], BF16, "pooledT%d" % i) for i in P2]
        self.pooledT_b = [Buf() for _ in P2]
        self.outaT = [self.sb([128, 4, 128], BF16, "outaT%d" % i) for i in P2]
        self.outaT_b = [Buf() for _ in P2]
        self.oat = self.sb([128, 4, 128], F32, "oat")
        self.oat_b = Buf()
        self.Fm = self.sb([128, 4, NV], F32, "Fm")
        self.Fbf = self.sb([128, 4, NV], BF16, "Fbf")
        self.F_b = Buf()
        self.Fbf_b = Buf()
        self.Bbf = [self.sb([128, 4, NV], BF16, "Bbf%d" % i) for i in range(2)]
        self.Bbf_b = [Buf() for _ in range(2)]
        self.bstn = 0

    def l0_proj(self, off, T, i, slot, slot3, full):
        TC, W = self.TC, self.W
        nt = T // W
        tok0 = off + i * W
        wb = self.wbuf
        self.load_xT(self.d["x"], tok0, slot)
        pn = [0]

        def bank():
            b = 2 + (pn[0] % 2)
            pn[0] += 1
            return b
        en = [0]

        def eng2():
            en[0] += 1
            return ("act", "dve")[en[0] % 2]

        def proj_tm(col0, ncols, c):
            b = bank()
            for k in range(8):
                self.MM(self.ps[b][:, 0:ncols], self.xT[:, k, c * 128:(c + 1) * 128], self.w_in[:, k, col0:col0 + ncols],
                        k == 0, k == 7, [self.xT_b, wb, self.wb_e], [self.psb[b]])
            return b

        def proj_fm(col0):
            b = bank()
            for k in range(8):
                self.MM(self.ps[b][:, 0:W], self.w_in[:, k, col0:col0 + 128], self.xT[:, k, :],
                        k == 0, k == 7, [self.xT_b, wb, self.wb_e], [self.psb[b]])
            return b
        for c in range(TC):
            if full:
                b = proj_tm(0, 512, c)
                self.CP("act", self.xa_tm[slot3][:, c, :], self.ps[b][:], [self.psb[b]], [self.xa_b[slot3]])
            b = proj_tm(1536, 512, c)
            self.CP(eng2(), self.vaug[slot][:, c, :, 0:128], self.ps[b][:].rearrange("p (h e) -> p h e", h=4),
                    [self.psb[b]], [self.vaug_b[slot]])
            if full:
                b = proj_tm(2048, 512, c)
                self.ACT(self.obs[slot][:, c, :], self.ps[b][:], AF.Sigmoid, [self.psb[b]], [self.obs_b[slot]])
        b = bank()
        for c in range(TC):
            for k in range(8):
                self.MM(self.ps[b][:, c * 16:(c + 1) * 16], self.xT[:, k, c * 128:(c + 1) * 128], self.w_in[:, k, 3072:3088],
                        k == 0, k == 7, [self.xT_b, wb, self.wb_e], [self.psb[b]])
        gps = self.ps[b][:, 0:TC * 16].rearrange("p (c g) -> p c g", c=TC)
        gb = self.gate_b[slot]
        self.TT("dve", self.gig[slot][:], gps[:, :, 0:8], self.bgi[:].unsqueeze(1).broadcast_to([128, TC, 8]), ALU.add,
                [self.psb[b], self.cb], [gb])
        self.TT("dve", self.gtmp[:], gps[:, :, 8:16], self.bgf[:].unsqueeze(1).broadcast_to([128, TC, 8]), ALU.add,
                [self.psb[b], self.cb], [self.gtmp_b])
        self.ACT(self.gtmp[:], self.gtmp[:], AF.Exp, [self.gtmp_b], [self.gtmp_b], scale=-1.0)
        self.ACT(self.gnf[slot][:], self.gtmp[:], AF.Ln, [self.gtmp_b], [gb], bias=1.0)
        if full:
            for g in range(4):
                b = proj_fm(512 + g * 128)
                self.ACT(self.zas[slot][:, g, :], self.ps[b][:, 0:W], AF.Silu, [self.psb[b]], [self.zas_b[slot]])
        for f in range(4):
            b = proj_fm(1024 + f * 128)
            self.CP(eng2(), self.xbT[slot][:, f, 1:W + 1], self.ps[b][:, 0:W], [self.psb[b]], [self.xbT_b[slot]])
        if full:
            for f in range(4):
                b = proj_fm(2560 + f * 128)
                self.ACT(self.zbs[slot][:, f, :], self.ps[b][:, 0:W], AF.Silu, [self.psb[b]], [self.zbs_b[slot]])

    def l0_halo(self, slot_lo, slot_hi, has_lo, has_hi):
        W = self.W
        if has_lo and has_hi:
            self.CP("act", self.xbT[slot_lo][:, :, W + 1:W + 2], self.xbT[slot_hi][:, :, 1:2],
                    [self.xbT_b[slot_hi]], [self.xbT_b[slot_lo]])
            self.CP("act", self.xbT[slot_hi][:, :, 0:1], self.xbT[slot_lo][:, :, W:W + 1],
                    [self.xbT_b[slot_lo]], [self.xbT_b[slot_hi]])
        elif has_hi:
            self.MS("pool", self.xbT[slot_hi][:, :, 0:1], 0.0, [], [self.xbT_b[slot_hi]])
        elif has_lo:
            self.MS("pool", self.xbT[slot_lo][:, :, W + 1:W + 2], 0.0, [], [self.xbT_b[slot_lo]])

    def l0_conv_qk(self, slot, need_q):
        TC, W = self.TC, self.W
        xb, xbb = self.xbT[slot], self.xbT_b[slot]
        for f in range(4):
            ct, ctb = self.ctmp[f % 2], self.ctmp_b[f % 2]
            self.TS("dve", ct[:], xb[:, f, 1:W + 1], self.convw[:, 1, f:f + 1], self.convb[:, f:f + 1], ALU.mult, ALU.add,
                    [xbb, self.cb], [ctb])
            self.STT(ct[:], xb[:, f, 0:W], self.convw[:, 0, f:f + 1], ct[:], ALU.mult, ALU.add, [xbb, self.cb, ctb], [ctb])
            self.STT(ct[:], xb[:, f, 2:W + 2], self.convw[:, 2, f:f + 1], ct[:], ALU.mult, ALU.add, [xbb, self.cb, ctb], [ctb])
            self.ACT(self.xc[:, f, :], ct[:], AF.Silu, [ctb], [self.xcf_b[f]])
            self.ACT(self.xcb_l[slot][:, f, :], ct[:], AF.Silu, [ctb], [self.xcb_bl[slot][f]])
            if need_q:
                self.ACT(self.skx_l[slot][:, f, :], self.xc[:, f, :], AF.Copy, [self.xcf_b[f], self.cb], [self.skx_bl[slot]], scale=self.skipv[:, f:f + 1])
        if need_q:
            n = 0
            for h in range(4):
                for (dst, j) in ((self.qT_l[slot], 1), (self.kT_l[slot], 2)):
                    b = 4 + (n % 2)
                    n += 1
                    self.MM(self.ps[b][:, 0:W], self.w_small[:, j, h, :], self.xcb_l[slot][:, h, :], True, True,
                            [self.wbuf, self.wb_e, self.xcb_bl[slot][h]], [self.psb[b]])
                    self.CP(("act", "dve")[n % 2], dst[:, h, :], self.ps[b][:, 0:W], [self.psb[b]], [self.qk_bl[slot]])

    def l0_gatevecs(self, slot, dirs):
        TC = self.TC
        b = 6
        gb = self.gate_b[slot]
        for c in range(TC):
            o = c * 16
            if 0 in dirs:
                self.MM(self.ps[b][:, o:o + 4], self.triF[:], self.gnf[slot][:, c, 0:4], True, True, [gb, self.cb], [self.psb[b]])
            if 1 in dirs:
                self.MM(self.ps[b][:, o + 4:o + 8], self.triB[:], self.gnf[slot][:, c, 4:8], True, True, [gb, self.cb], [self.psb[b]])
            self.MM(self.ps[b][:, o + 8:o + 16], self.ones_f[:], self.gnf[slot][:, c, 0:8], True, True, [gb, self.cb], [self.psb[b]])
        gps = self.ps[b][:, 0:TC * 16].rearrange("p (c g) -> p c g", c=TC)
        lo, hi = (0 if 0 in dirs else 4), (8 if 1 in dirs else 4)
        self.ACT(self.avec_l[slot][:, :, lo:hi], gps[:, :, lo:hi], AF.Exp, [self.psb[b]], [self.gv_bl[slot]], scale=-1.0)
        self.ACT(self.egv_l[slot][:, :, lo:hi], gps[:, :, 8 + lo:8 + hi], AF.Exp, [self.psb[b]], [self.gv_bl[slot]], scale=-1.0)
        self.TT("dve", self.bvec_l[slot][:, :, lo:hi], self.gig[slot][:, :, lo:hi], gps[:, :, lo:hi], ALU.add, [gb, self.psb[b]], [self.gv_bl[slot]])
        self.ACT(self.bvec_l[slot][:, :, lo:hi], self.bvec_l[slot][:, :, lo:hi], AF.Exp, [self.gv_bl[slot]], [self.gv_bl[slot]], bias=math.log(DH ** -0.5))

    def l0_kprime(self, slot, c, par):
        b = 7
        for h in range(4):
            self.MM(self.ps[b][:, h * 128:(h + 1) * 128], self.xcb_l[slot][:, h, c * 128:(c + 1) * 128], self.w_small[:, 2, h, :], True, True,
                    [self.xcb_bl[slot][h], self.wb_e], [self.psb[b]])
        self.CP("act", self.kp[par][:], self.ps[b][:].rearrange("p (h e) -> p h e", h=4), [self.psb[b]], [self.kp_b[par]])

    def l0_vprime(self, slot, c, dr, par):
        self.TT("dve", self.vp[par][dr][:], self.vaug[slot][:, c, :, :],
                self.bvec_l[slot][:, c, dr * 4:dr * 4 + 4].unsqueeze(2).broadcast_to([128, 4, NV]), ALU.mult,
                [self.vaug_b[slot], self.gv_bl[slot]], [self.vp_b[par][dr]])

    def l0_state_update(self, slot, c, dr, par, Mm, Mb, Mbf, Mbfb):
        for h in range(4):
            b, o = (5, h * NV) if h < 3 else (6, 0)
            self.MM(self.ps[b][:, o:o + NV], self.kp[par][:, h, :], self.vp[par][dr][:, h, :], True, True,
                    [self.kp_b[par], self.vp_b[par][dr]], [self.psb[b]])
        self.TT("dve", Mm[:, 0:3, :], Mm[:, 0:3, :], self.ps[5][:, 0:3 * NV].rearrange("p (h e) -> p h e", h=3), ALU.add,
                [Mb, self.psb[5]], [Mb])
        self.TT("dve", Mm[:, 3, :], Mm[:, 3, :], self.ps[6][:, 0:NV], ALU.add, [Mb, self.psb[6]], [Mb])
        self.TT("dve", Mm[:], Mm[:], self.egv_l[slot][:, c, dr * 4:dr * 4 + 4].unsqueeze(2).broadcast_to([128, 4, NV]), ALU.mult,
                [Mb, self.gv_bl[slot]], [Mb])
        if Mbf is not None:
            self.CP("act", Mbf[:], Mm[:], [Mb], [Mbfb])

    def layer0_seq(self, off, T):
        self.alloc_l0()
        TC, W = self.TC, self.W
        nt = T // W
        nch = T // 128
        for s in range(2):
            self.MS("pool", self.vaug[s][:, :, :, 128:129], 1.0, [], [self.vaug_b[s]])
        Bm, Bb = self.Fm, self.F_b
        self.MS("pool", Bm[:], 0.0, [], [Bb])
        sl = lambda t: (nt - 1 - t) % 2
        bbase = self.bconv_decl
        self.l0_bwd_proj_early(off, T, nt - 1, sl(nt - 1), preloaded=False)
        self.l0_halo(sl(nt - 1), None, True, False)
        g0 = [self.l0_bwd_proj_late_gen(sl(nt - 1))]
        if nt >= 2:
            g0.append(self.l0_xload_gen(off + (nt - 2) * W, sl(nt - 2)))
        self.run_gens(g0)
        if nt >= 2:
            self.l0_bwd_proj_early(off, T, nt - 2, sl(nt - 2), preloaded=True)
            self.l0_halo(sl(nt - 2), sl(nt - 1), True, True)
        for ip in range(nt - 1, 0, -1):
            i = ip - 1
            gens = [self.l0_bwd_tile_gen(off, T, ip, sl(ip), Bm, Bb),
                    self.l0_bwd_sideA_gen(off, i, sl(i), bbase + (nt - 1 - ip) + 1)]
            if i - 2 >= 0:
                pass
            if i - 1 >= 0 and ip < nt - 0:
                if not (i - 1 == nt - 2):
                    gens.append(self.l0_xload_gen(off + (i - 1) * W, sl(i - 1)))
            self.run_gens(gens)
        self.l0_halo(None, sl(0), False, True)
        self.run_gens([self.l0_bwd_tile_gen(off, T, 0, sl(0), Bm, Bb)])
        self.MS("pool", self.Fm[:], 0.0, [], [self.F_b])
        self.MS("pool", self.Fbf[:], 0.0, [], [self.Fbf_b])
        self.fdone = 0
        self.pool_done = 0
        self.conv_base = self.conv_decl
        self.l0_proj_early(off, T, 0, 0, 0)
        self.l0_halo(None, 0, False, True)
        side0 = [self.l0_proj_late_gen(0)]
        if nt > 1:
            side0.append(self.l0_xload_gen(off + W, 1))
        self.run_gens(side0)
        if nt > 1:
            self.l0_proj_early(off, T, 1, 1, 1, preloaded=True)
            self.l0_halo(0, 1, True, True)
        else:
            self.l0_halo(0, None, True, False)
        for i in range(nt):
            slot = i % 2
            side = [self.faster(self.l0_sideA_gen(off, T, i, nt), 2)]
            if i + 2 < nt:
                side.append(self.l0_xload_gen(off + (i + 2) * W, slot))
            self.run_gens([self.l0_tile_gen(off, T, i, slot)] + side)

    def l0_xload_gen(self, tok0, slot):
        self.load_x(self.d["x"], tok0, slot)
        for _ in range(12):
            yield
        self.xloaded_tok = tok0

    def l0_xT(self, tok0, slot, preloaded):
        if not preloaded:
            self.load_x(self.d["x"], tok0, slot)
        for c in range(self.TC):
            self.xT_chunk(c, c % 2, slot)

    def l0_bwd_proj_early(self, off, T, i, slot, preloaded=False):
        TC, W = self.TC, self.W
        wb = self.wb_e
        self.l0_xT(off + i * W, slot, preloaded)
        for f in range(4):
            b = 2 + f % 2
            for k in range(8):
                self.MM(self.ps[b][:, 0:W], self.w_in[:, k, 1024 + f * 128:1024 + (f + 1) * 128], self.xT[:, k, :], k == 0, k == 7,
                        [self.xT_b, wb, self.wb_e], [self.psb[b]])
            self.CP(("act", "dve")[f % 2], self.xbT[slot][:, f, 1:W + 1], self.ps[b][:, 0:W], [self.psb[b]], [self.xbT_b[slot]])

    def l0_bwd_early_gen(self, tok0, slot, need):
        TC, W = self.TC, self.W
        while self.xloaded_tok != tok0 or self.bconv_decl < need:
            yield
        for c in range(TC):
            self.xT_chunk(c, 1, slot)
            yield
            yield
        for f in range(4):
            for k in range(8):
                self.MM(self.ps[1][:, 0:W], self.w_in[:, k, 1024 + f * 128:1024 + (f + 1) * 128], self.xT[:, k, :], k == 0, k == 7,
                        [self.xT_b, self.wb_e], [self.psb[1]])
                if k == 3:
                    yield
            self.CP(("act", "dve")[f % 2], self.xbT[slot][:, f, 1:W + 1], self.ps[1][:, 0:W], [self.psb[1]], [self.xbT_b[slot]])
            yield

    def l0_bwd_sideA_gen(self, off, i, slot_i, need):
        W = self.W
        for _ in self.l0_bwd_proj_late_gen(slot_i):
            yield
        if i - 1 >= 0:
            for _ in self.l0_bwd_early_gen(off + (i - 1) * W, 1 - slot_i, need):
                yield
            self.l0_halo(1 - slot_i, slot_i, True, True)

    def l0_bwd_proj_late_gen(self, slot):
        TC, W = self.TC, self.W
        wb = self.wb_e
        for c in range(TC):
            b = 2 + c % 2
            for k in range(8):
                self.MM(self.ps[b][:], self.xT[:, k, c * 128:(c + 1) * 128], self.w_in[:, k, 1536:2048], k == 0, k == 7,
                        [self.xT_b, wb, self.wb_e], [self.psb[b]])
            self.CP("act", self.vaug[slot][:, c, :, 0:128], self.ps[b][:].rearrange("p (h e) -> p h e", h=4),
                    [self.psb[b]], [self.vaug_b[slot]])
            yield
        b = 2
        for c in range(TC):
            for k in range(8):
                self.MM(self.ps[b][:, c * 16:(c + 1) * 16], self.xT[:, k, c * 128:(c + 1) * 128], self.w_in[:, k, 3072:3088],
                        k == 0, k == 7, [self.xT_b, wb, self.wb_e], [self.psb[b]])
        gps = self.ps[b][:, 0:TC * 16].rearrange("p (c g) -> p c g", c=TC)
        gb = self.gate_b[slot]
        self.TT("dve", self.gig[slot][:], gps[:, :, 0:8], self.bgi[:].unsqueeze(1).broadcast_to([128, TC, 8]), ALU.add,
                [self.psb[b], self.cb], [gb])
        self.TT("dve", self.gtmp[:], gps[:, :, 8:16], self.bgf[:].unsqueeze(1).broadcast_to([128, TC, 8]), ALU.add,
                [self.psb[b], self.cb], [self.gtmp_b])
        yield
        self.ACT(self.gtmp[:], self.gtmp[:], AF.Exp, [self.gtmp_b], [self.gtmp_b], scale=-1.0)
        self.ACT(self.gnf[slot][:], self.gtmp[:], AF.Ln, [self.gtmp_b], [gb], bias=1.0)
        yield

    def l0_bwd_tile_gen(self, off, T, i, slot, Bm, Bb):
        TC, W = self.TC, self.W
        xb, xbb = self.xbT[slot], self.xbT_b[slot]
        for f in range(4):
            ct, ctb = self.ctmp[f % 2], self.ctmp_b[f % 2]
            self.TS("dve", ct[:], xb[:, f, 1:W + 1], self.convw[:, 1, f:f + 1], self.convb[:, f:f + 1], ALU.mult, ALU.add,
                    [xbb, self.cb], [ctb])
            self.STT(ct[:], xb[:, f, 0:W], self.convw[:, 0, f:f + 1], ct[:], ALU.mult, ALU.add, [xbb, self.cb, ctb], [ctb])
            self.STT(ct[:], xb[:, f, 2:W + 2], self.convw[:, 2, f:f + 1], ct[:], ALU.mult, ALU.add, [xbb, self.cb, ctb], [ctb])
            yield
            self.ACT(self.xcb_l[slot][:, f, :], ct[:], AF.Silu, [ctb], [self.xcb_bl[slot][f]])
            yield
        self.bconv_decl += 1
        self.l0_gatevecs(slot, dirs=(1,))
        yield
        yield
        for cc in range(TC):
            c = TC - 1 - cc
            jc = i * TC + c
            par = cc % 2
            bs = self.bstn % 2
            self.bstn += 1
            self.CP("act", self.Bbf[bs][:], Bm[:], [Bb], [self.Bbf_b[bs]])
            bb = self.bst_bufs.setdefault(jc, Buf("bst%d" % jc))
            self.DMA("sp", self.bst_d[jc], self.Bbf[bs][:].rearrange("p h e -> p (h e)"), "bst%d" % bs, [self.Bbf_b[bs]], [bb])
            self.l0_kprime(slot, c, par)
            self.l0_vprime(slot, c, 1, par)
            yield
            yield
            self.l0_state_update(slot, c, 1, par, Bm, Bb, None, None)
            yield

    def l0_proj_early(self, off, T, i, slot, slot3, preloaded=False):
        TC, W = self.TC, self.W
        wb = self.wbuf
        self.l0_xT(off + i * W, slot, preloaded)
        for f in range(4):
            b = 2 + f % 2
            for k in range(8):
                self.MM(self.ps[b][:, 0:W], self.w_in[:, k, 1024 + f * 128:1024 + (f + 1) * 128], self.xT[:, k, :], k == 0, k == 7,
                        [self.xT_b, wb, self.wb_e], [self.psb[b]])
            self.CP(("act", "dve")[f % 2], self.xbT[slot][:, f, 1:W + 1], self.ps[b][:, 0:W], [self.psb[b]], [self.xbT_b[slot]])
        for c in range(TC):
            b = 2 + c % 2
            for k in range(8):
                self.MM(self.ps[b][:], self.xT[:, k, c * 128:(c + 1) * 128], self.w_in[:, k, 0:512], k == 0, k == 7,
                        [self.xT_b, wb, self.wb_e], [self.psb[b]])
            self.CP("act", self.xa_tm[slot3][:, c, :], self.ps[b][:], [self.psb[b]], [self.xa_b[slot3]])

    def l0_proj_early_gen(self, off, T, i, slot, slot3, need_conv):
        TC, W = self.TC, self.W
        wb = self.wbuf
        while self.xloaded_tok != off + i * W or self.conv_decl < need_conv or self.pool_done < (i - 2) * TC + 1:
            yield
        for c in range(TC):
            self.xT_chunk(c, 1, slot)
            yield
            yield
        for f in range(4):
            b = 2 + f % 2
            for k in range(8):
                self.MM(self.ps[b][:, 0:W], self.w_in[:, k, 1024 + f * 128:1024 + (f + 1) * 128], self.xT[:, k, :], k == 0, k == 7,
                        [self.xT_b, wb, self.wb_e], [self.psb[b]])
            self.CP(("act", "dve")[f % 2], self.xbT[slot][:, f, 1:W + 1], self.ps[b][:, 0:W], [self.psb[b]], [self.xbT_b[slot]])
            yield
            yield
        for c in range(TC):
            b = 2 + c % 2
            for k in range(8):
                self.MM(self.ps[b][:], self.xT[:, k, c * 128:(c + 1) * 128], self.w_in[:, k, 0:512], k == 0, k == 7,
                        [self.xT_b, wb, self.wb_e], [self.psb[b]])
            self.CP("act", self.xa_tm[slot3][:, c, :], self.ps[b][:], [self.psb[b]], [self.xa_b[slot3]])
            yield
            yield

    def l0_sideA_gen(self, off, T, i, nt):
        if i + 1 < nt:
            for _ in self.l0_proj_late_gen((i + 1) % 2):
                yield
            if i + 2 < nt:
                for _ in self.l0_proj_early_gen(off, T, i + 2, i % 2, (i + 2) % 3, self.conv_base + i + 1):
                    yield
                self.l0_halo((i + 1) % 2, i % 2, True, True)
            else:
                self.l0_halo((i + 1) % 2, None, True, False)

    def l0_proj_late_gen(self, slot):
        TC, W = self.TC, self.W
        wb = self.wbuf
        pn = [0]

        def bank():
            pn[0] += 1
            return 2 + pn[0] % 2

        def fm(col0):
            b = bank()
            for k in range(8):
                self.MM(self.ps[b][:, 0:W], self.w_in[:, k, col0:col0 + 128], self.xT[:, k, :], k == 0, k == 7,
                        [self.xT_b, wb, self.wb_e], [self.psb[b]])
            return b

        def tm(col0, c):
            b = bank()
            for k in range(8):
                self.MM(self.ps[b][:], self.xT[:, k, c * 128:(c + 1) * 128], self.w_in[:, k, col0:col0 + 512], k == 0, k == 7,
                        [self.xT_b, wb, self.wb_e], [self.psb[b]])
            return b
        for g in range(4):
            b = fm(512 + g * 128)
            self.ACT(self.zas[slot][:, g, :], self.ps[b][:, 0:W], AF.Silu, [self.psb[b]], [self.zas_b[slot]])
            yield
        for f in range(4):
            b = fm(2560 + f * 128)
            self.ACT(self.zbs[slot][:, f, :], self.ps[b][:, 0:W], AF.Silu, [self.psb[b]], [self.zbs_b[slot]])
            yield
        for c in range(TC):
            b = tm(2048, c)
            self.ACT(self.obs[slot][:, c, :], self.ps[b][:], AF.Sigmoid, [self.psb[b]], [self.obs_b[slot]])
            yield
        for c in range(TC):
            b = tm(1536, c)
            self.CP("act", self.vaug[slot][:, c, :, 0:128], self.ps[b][:].rearrange("p (h e) -> p h e", h=4),
                    [self.psb[b]], [self.vaug_b[slot]])
            yield
        b = bank()
        for c in range(TC):
            for k in range(8):
                self.MM(self.ps[b][:, c * 16:(c + 1) * 16], self.xT[:, k, c * 128:(c + 1) * 128], self.w_in[:, k, 3072:3088],
                        k == 0, k == 7, [self.xT_b, wb, self.wb_e], [self.psb[b]])
        gps = self.ps[b][:, 0:TC * 16].rearrange("p (c g) -> p c g", c=TC)
        gb = self.gate_b[slot]
        self.TT("dve", self.gig[slot][:], gps[:, :, 0:8], self.bgi[:].unsqueeze(1).broadcast_to([128, TC, 8]), ALU.add,
                [self.psb[b], self.cb], [gb])
        self.TT("dve", self.gtmp[:], gps[:, :, 8:16], self.bgf[:].unsqueeze(1).broadcast_to([128, TC, 8]), ALU.add,
                [self.psb[b], self.cb], [self.gtmp_b])
        yield
        self.ACT(self.gtmp[:], self.gtmp[:], AF.Exp, [self.gtmp_b], [self.gtmp_b], scale=-1.0)
        self.ACT(self.gnf[slot][:], self.gtmp[:], AF.Ln, [self.gtmp_b], [gb], bias=1.0)
        yield

    def l0_prologue_gen(self, slot):
        TC, W = self.TC, self.W
        xb, xbb = self.xbT[slot], self.xbT_b[slot]
        for f in range(4):
            ct, ctb = self.ctmp[f % 2], self.ctmp_b[f % 2]
            self.TS("dve", ct[:], xb[:, f, 1:W + 1], self.convw[:, 1, f:f + 1], self.convb[:, f:f + 1], ALU.mult, ALU.add,
                    [xbb, self.cb], [ctb])
            self.STT(ct[:], xb[:, f, 0:W], self.convw[:, 0, f:f + 1], ct[:], ALU.mult, ALU.add, [xbb, self.cb, ctb], [ctb])
            self.STT(ct[:], xb[:, f, 2:W + 2], self.convw[:, 2, f:f + 1], ct[:], ALU.mult, ALU.add, [xbb, self.cb, ctb], [ctb])
            yield
            self.ACT(self.xc[:, f, :], ct[:], AF.Silu, [ctb], [self.xcf_b[f]])
            self.ACT(self.xcb_l[slot][:, f, :], ct[:], AF.Silu, [ctb], [self.xcb_bl[slot][f]])
            self.ACT(self.skx_l[slot][:, f, :], self.xc[:, f, :], AF.Copy, [self.xcf_b[f], self.cb], [self.skx_bl[slot]], scale=self.skipv[:, f:f + 1])
            yield
        self.conv_decl += 1
        n = 0
        for h in range(4):
            for (dst, j) in ((self.qT_l[slot], 1), (self.kT_l[slot], 2)):
                b = 4 + (n % 2)
                n += 1
                self.MM(self.ps[b][:, 0:W], self.w_small[:, j, h, :], self.xcb_l[slot][:, h, :], True, True,
                        [self.wbuf, self.wb_e, self.xcb_bl[slot][h]], [self.psb[b]])
                self.CP(("act", "dve")[n % 2], dst[:, h, :], self.ps[b][:, 0:W], [self.psb[b]], [self.qk_bl[slot]])
            yield
        self.l0_gatevecs(slot, dirs=(0, 1))
        yield
        yield

    def l0_chunk_gen(self, off, T, i, slot, c):
        TC, W = self.TC, self.W
        nch = T // 128
        jc = i * TC + c
        par = jc % 2
        cs = slice(c * 128, (c + 1) * 128)
        va, vab = self.vaug[slot], self.vaug_b[slot]
        pT, pTb = self.pT[par], self.pT_b[par]
        nd, ndb = self.nd[par], self.nd_b[par]
        dd, ddb = self.dd[par], self.dd_b[par]
        hs, hs2, hsb, hs2b = self.hs[par], self.hs2[par], self.hs_b[par], self.hs2_b[par]
        hstat, hmv, hsc, hstb = self.hstat[par], self.hmv[par], self.hsc[par], self.hst_b[par]
        bs = self.bstn % 2
        self.bstn += 1
        self.DMA("sp", self.Bbf[bs][:].rearrange("p h e -> p (h e)"), self.bst_d[jc], "bst%d" % bs,
                 [self.bst_bufs[jc]], [self.Bbf_b[bs]])
        rs = self.resid_begin(self.d["x"], off + jc * 128, None)
        for dr in range(2):
            self.l0_vprime(slot, c, dr, par)
        yield
        for h in range(4):
            self.MM(self.ps[4][:, h * 128:(h + 1) * 128], self.kT_l[slot][:, h, cs], self.qT_l[slot][:, h, cs], True, True,
                    [self.qk_bl[slot]], [self.psb[4]])
        for dr in range(2):
            mask = (self.maskF, self.maskB)[dr]
            self.TT("dve", pT[:, dr, :, :], self.ps[4][:].rearrange("p (h e) -> p h e", h=4),
                    mask[:].unsqueeze(1).broadcast_to([128, 4, 128]), ALU.mult, [self.psb[4], self.cb], [pTb])
        yield
        while self.fdone < jc:
            yield
        for dr in range(2):
            Mbf, Mbfb = (self.Fbf, self.Fbf_b) if dr == 0 else (self.Bbf[bs], self.Bbf_b[bs])
            for h in range(4):
                combo = dr * 4 + h
                b, o = 5 + combo // 3, (combo % 3) * NV
                self.MM(self.ps[b][:, o:o + NV], pT[:, dr, h, :], self.vp[par][dr][:, h, :], True, False,
                        [pTb, self.vp_b[par][dr]], [self.psb[b]])
                self.MM(self.ps[b][:, o:o + NV], self.qT_l[slot][:, h, cs], Mbf[:, h, :], False, True,
                        [self.qk_bl[slot], Mbfb], [self.psb[b]])
        for bi, (b, n_) in enumerate(((5, 3), (6, 3), (7, 2))):
            pv_ = self.ps[b][:, 0:n_ * NV].rearrange("p (c e) -> p c e", e=NV)
            self.TT("dve", dd[:, bi * 3:bi * 3 + n_], pv_[:, :, 128], self.avec_l[slot][:, c, bi * 3:bi * 3 + n_], ALU.mult,
                    [self.psb[b], self.gv_bl[slot]], [ddb])
        self.STT(dd[:, 8:16], dd[:, 0:8], -1.0, dd[:, 0:8], ALU.mult, ALU.max, [ddb], [ddb])
        self.TS("dve", dd[:, 8:16], dd[:, 8:16], 1.0, None, ALU.max, None, [ddb], [ddb])
        self.S.op("dve", lambda e: e.reciprocal(out=dd[:, 16:24], in_=dd[:, 8:16]), [ddb], [ddb])
        self.TT("dve", dd[:, 16:24], dd[:, 16:24], self.avec_l[slot][:, c, :], ALU.mult, [ddb, self.gv_bl[slot]], [ddb])
        for bi, (b, n_) in enumerate(((5, 3), (6, 3), (7, 2))):
            pv_ = self.ps[b][:, 0:n_ * NV].rearrange("p (c e) -> p c e", e=NV)
            self.TT("dve", nd[:, bi * 3:bi * 3 + n_, :], pv_[:, :, 0:128],
                    dd[:, 16 + bi * 3:16 + bi * 3 + n_].unsqueeze(2).broadcast_to([128, n_, 128]), ALU.mult,
                    [self.psb[b], ddb], [ndb])
        yield
        self.l0_kprime(slot, c, par)
        yield
        self.l0_state_update(slot, c, 0, par, self.Fm, self.F_b, self.Fbf, self.Fbf_b)
        self.fdone = jc + 1
        yield
        self.TT("dve", hs[:], nd[:, 0:4, :], nd[:, 4:8, :], ALU.add, [ndb], [hsb])
        self.TT("dve", hs[:], hs[:], self.obs[slot][:, c, :].rearrange("p (h e) -> p h e", h=4), ALU.mult,
                [hsb, self.obs_b[slot]], [hsb])
        for h in range(4):
            self.S.op("dve", lambda e, h=h: e.bn_stats(out=hstat[:, h, :], in_=hs[:, h, :]), [hsb], [hstb])
        for h in range(4):
            self.S.op("dve", lambda e, h=h: e.bn_aggr(out=hmv[:, h, :], in_=hstat[:, h, :]), [hstb], [hstb])
        yield
        yield
        self.ACT(hsc[:, 0, :], hmv[:, :, 1], AF.Ln, [hstb], [hstb], bias=self.epsc[:, 0:1])
        self.ACT(hsc[:, 1, :], hsc[:, 0, :], AF.Exp, [hstb], [hstb], scale=-0.5)
        yield
        self.STT(hsc[:, 2, :], hmv[:, :, 0], -1.0, hsc[:, 1, :], ALU.mult, ALU.mult, [hstb], [hstb])
        yield
        for h in range(4):
            self.ACT(hs2[:, h, :], hs[:, h, :], AF.Identity, [hsb, hstb], [hs2b], bias=hsc[:, 2, h:h + 1], scale=hsc[:, 1, h:h + 1])
        yield
        for h in range(4):
            self.TR(self.ps[0][:, h * 128:(h + 1) * 128], hs2[:, h, :], self.ident_f[:], [hs2b, self.cb], [self.psb[0]])
        pv = self.ps[0][:].rearrange("p (h e) -> p h e", h=4)
        self.TT("dve", self.obT[:], pv, self.mhg[:].unsqueeze(2).broadcast_to([128, 4, 128]), ALU.mult, [self.psb[0], self.cb], [self.obT_b])
        self.TT("dve", self.obT[:], self.obT[:], self.skx_l[slot][:, :, cs], ALU.add, [self.obT_b, self.skx_bl[slot]], [self.obT_b])
        self.TT("dve", self.outbT[par][:], self.obT[:], self.zbs[slot][:, :, cs], ALU.mult, [self.obT_b, self.zbs_b[slot]], [self.outbT_b[par]])
        yield
        for g in range(4):
            blks = []
            if jc > 0:
                blks.append((jc - 1, 0))
            blks.append((jc, 1 if jc == 0 else (3 if jc == nch - 1 else 2)))
            if jc < nch - 1:
                blks.append((jc + 1, 4))
            for n_, (j2, blk) in enumerate(blks):
                i2, c2 = j2 // TC, j2 % TC
                s3 = i2 % 3
                self.MM(self.ps[1][:, g * 128:(g + 1) * 128], self.xa_tm[s3][:, c2, g * 128:(g + 1) * 128], self.PB[:, g, blk, :],
                        n_ == 0, n_ == len(blks) - 1, [self.xa_b[s3], self.cb], [self.psb[1]])
        self.CP("act", self.pooledT[par][:], self.ps[1][:].rearrange("p (h e) -> p h e", h=4), [self.psb[1]], [self.pooledT_b[par]])
        self.pool_done = max(self.pool_done, jc + 1)
        yield
        for g in range(4):
            self.MM(self.ps[7][:, g * 128:(g + 1) * 128], self.w_small[:, 0, g, :], self.pooledT[par][:, g, :], True, True,
                    [self.wbuf, self.wb_e, self.pooledT_b[par]], [self.psb[7]])
        self.TT("dve", self.oat[:], self.ps[7][:].rearrange("p (h e) -> p h e", h=4),
                self.pscale[:].unsqueeze(2).broadcast_to([128, 4, 128]), ALU.mult, [self.psb[7], self.cb], [self.oat_b])
        self.TT("dve", self.outaT[par][:], self.oat[:], self.zas[slot][:, :, cs], ALU.mult, [self.oat_b, self.zas_b[slot]], [self.outaT_b[par]])
        yield
        for hf in range(2):
            b = 2 + hf
            for f in range(8):
                lt = self.outaT[par][:, f, :] if f < 4 else self.outbT[par][:, f - 4, :]
                self.MM(self.ps[b][:], lt, self.w_out[:, f, hf * 512:(hf + 1) * 512], f == 0, f == 7,
                        [self.outaT_b[par], self.outbT_b[par], self.wbuf], [self.psb[b]])
            self.resid_half(rs, hf, b)
        yield
        dst = self.x1_d if 1 in self.layers else self.y_d
        dstb = self.x1_bufs if 1 in self.layers else None
        for _ in self.resid_finish_gen(rs, dst, off + jc * 128, dstb):
            yield

    def faster(self, g, r):
        while True:
            for _ in range(r):
                try:
                    next(g)
                except StopIteration:
                    return
            yield

    def run_gens(self, gens):
        gens = list(gens)
        while gens:
            for g in list(gens):
                try:
                    next(g)
                except StopIteration:
                    gens.remove(g)

    def l0_tile_gen(self, off, T, i, slot):
        for _ in self.l0_prologue_gen(slot):
            yield
        act = [self.l0_chunk_gen(off, T, i, slot, c) for c in range(self.TC)]
        while act:
            for g in list(act):
                try:
                    next(g)
                except StopIteration:
                    act.remove(g)
            yield

    def alloc_l1(self):
        if hasattr(self, "KT"):
            return
        TC, W = self.TC, self.W
        mc = self.maxch
        self.KT = self.sb([128, 2, mc * 128], BF16, "KT")
        self.KT_b = Buf()
        self.VA = self.sb([128, mc, 2, NV], BF16, "VA")
        self.VA_b = Buf()
        self.cosT = self.sb([128, mc, 2, 32], F32, "cosT")
        self.sinT = self.sb([128, mc, 2, 32], F32, "sinT")
        self.tab_b = Buf()
        self.ssq = self.sb([128, 16], F32, "ssq")
        self.ssq_b = Buf()
        self.junk = self.sb([128, 128], F32, "junk")
        self.junk_b = Buf()
        self.qn = self.sb([128, 8, 128], F32, "qn")
        self.qn_b = Buf()
        self.rt = [self.sb([128, 8, 2, 32], F32, "rt%d" % i) for i in range(2)]
        self.rt_b = Buf()
        self.qr = [self.sb([128, 8, 128], BF16, "qr%d" % i) for i in range(2)]
        self.qr_b = [Buf() for _ in range(2)]
        self.qrn = 0
        self.QT = [self.sb([128, 8, W], BF16, "QT%d" % i) for i in range(2)]
        self.QT_b = [Buf() for _ in range(2)]
        self.zs = [self.sb([128, TC, D], F32, "zs%d" % i) for i in range(2)]
        self.zs_b = [Buf() for _ in range(2)]
        self.PT = [self.sb([128, 512], BF16, "PT%d" % i) for i in range(3)]
        self.PT_b = [Buf() for _ in range(3)]
        self.og = [self.sb([128, TC, D], BF16, "og%d" % i) for i in range(2)]
        self.og_b = [Buf(), Buf()]
        self.ogT = self.sb([128, 8, 128], BF16, "ogT")
        self.ogT_b = Buf()
        self.rden = self.sb([128, 8], F32, "rden")
        self.rden_b = Buf()
        self.ptn = 0
        self.ktb = []
        for i in range(2):
            self.ktb.append((self.sb([128, 16], F32, "ssqk%d" % i), Buf(), self.sb([128, 2, 128], F32, "qnk%d" % i), Buf(),
                             [self.sb([128, 2, 2, 32], F32, "rtk%d_%d" % (i, j)) for j in range(2)], Buf(),
                             self.sb([128, 128], F32, "junkk%d" % i), Buf()))
        self.build_rope_tables()

    def build_rope_tables(self):
        S = self.S
        mc = self.maxch
        tb = self.tab_b
        R, Wr = [tb], [tb]
        A = self.cosT
        pidx = self.sb([128, 4], F32, "pidx")
        inv = self.sb([128, 32], F32, "inv")
        prow = self.sb([128, mc], F32, "prow")
        ne = mc * 64
        R = [tb, self.zs_b[0], self.zs_b[1], self.KT_b]
        Wr = R

        class _V:
            def __init__(self, ap):
                self.ap = ap

            def __getitem__(self, k):
                if isinstance(k, slice):
                    return self.ap
                return self.ap[k]
        ang = _V(self.zs[0][:].rearrange("p c d -> p (c d)")[:, 0:ne].rearrange("p (m a f) -> p m a f", a=2, f=32))
        kf = _V(self.zs[1][:].rearrange("p c d -> p (c d)")[:, 0:ne].rearrange("p (m a f) -> p m a f", a=2, f=32))
        ki = _V(self.KT[:].rearrange("p k t -> p (k t)").bitcast(I32)[:, 0:ne].rearrange("p (m a f) -> p m a f", a=2, f=32))
        S.op("pool", lambda e: e.iota(pidx[:, 0:1], pattern=[[0, 1]], base=0, channel_multiplier=1,
                                      allow_small_or_imprecise_dtypes=True), R, Wr)
        self.TS("dve", pidx[:, 1:2], pidx[:, 0:1], 64.0, None, ALU.is_ge, None, R, Wr)
        self.STT(pidx[:, 2:3], pidx[:, 1:2], -64.0, pidx[:, 0:1], ALU.mult, ALU.add, R, Wr)
        S.op("pool", lambda e: e.iota(inv[:], pattern=[[1, 32]], base=0, channel_multiplier=0,
                                      allow_small_or_imprecise_dtypes=True), R, Wr)
        self.ACT(inv[:], inv[:], AF.Exp, R, Wr, scale=-math.log(10000.0) / 32.0)
        S.op("pool", lambda e: e.iota(prow[:], pattern=[[2, mc]], base=0, channel_multiplier=0,
                                      allow_small_or_imprecise_dtypes=True), R, Wr)
        self.TS("dve", prow[:], prow[:], pidx[:, 1:2], None, ALU.add, None, R, Wr)
        self.TT("dve", ang[:, :, 0, :], prow[:].unsqueeze(2).broadcast_to([128, mc, 32]),
                inv[:].unsqueeze(1).broadcast_to([128, mc, 32]), ALU.mult, R, Wr)
        self.TS("dve", ang[:, :, 1, :], inv[:].unsqueeze(1).broadcast_to([128, mc, 32]), pidx[:, 2:3], None, ALU.mult, None, R, Wr)
        TWO_PI = 2.0 * math.pi
        for tab, shift in ((self.sinT, 0.0), (self.cosT, math.pi / 2.0)):
            self.TS("dve", tab[:], ang[:], shift, None, ALU.add, None, R, Wr)
            self.TS("dve", kf[:], tab[:], 1.0 / TWO_PI, None, ALU.mult, None, R, Wr)
            self.CP("dve", ki[:], kf[:], R, Wr)
            self.CP("dve", kf[:], ki[:], R, Wr)
            self.STT(tab[:], kf[:], -TWO_PI, tab[:], ALU.mult, ALU.add, R, Wr)
            self.TS("dve", kf[:], tab[:], math.pi, None, ALU.is_gt, None, R, Wr)
            self.STT(tab[:], kf[:], -TWO_PI, tab[:], ALU.mult, ALU.add, R, Wr)
            self.TS("dve", kf[:], tab[:], -math.pi, None, ALU.is_lt, None, R, Wr)
            self.STT(tab[:], kf[:], TWO_PI, tab[:], ALU.mult, ALU.add, R, Wr)
            self.TS("dve", tab[:], tab[:], math.pi, -math.pi, ALU.min, ALU.max, R, Wr)
            self.ACT(tab[:], tab[:], AF.Sin, R, Wr)

    def l1_norm_rope_gen(self, psrc_banks, nh, gain, jc, dst, dstb, tb=None):
        if tb is None:
            tb = (self.ssq, self.ssq_b, self.qn, self.qn_b, self.rt, self.rt_b, self.junk, self.junk_b)
        ssq, ssq_b, qn, qn_b, rt, rt_b, junk, junk_b = tb
        hh = 0
        for (b, n_) in psrc_banks:
            for j in range(n_):
                self.ACT(junk[:], self.ps[b][:, j * 128:(j + 1) * 128], AF.Square, [self.psb[b]], [junk_b, ssq_b],
                         accum=ssq[:, hh + j:hh + j + 1])
            hh += n_
        self.ACT(ssq[:, 8:8 + nh], ssq[:, 0:nh], AF.Ln, [ssq_b], [ssq_b], bias=self.epsc[:, 1:2], scale=1.0 / 128.0)
        self.ACT(ssq[:, 8:8 + nh], ssq[:, 8:8 + nh], AF.Exp, [ssq_b], [ssq_b], scale=-0.5)
        yield
        yield
        hh = 0
        for (b, n_) in psrc_banks:
            for j in range(n_):
                self.STT(qn[:, hh + j, :], self.ps[b][:, j * 128:(j + 1) * 128], ssq[:, 8 + hh + j:9 + hh + j], gain[:],
                         ALU.mult, ALU.mult, [self.psb[b], ssq_b, self.cb], [qn_b])
            hh += n_
        qv = qn[:, 0:nh, :].rearrange("p h (a t f) -> p h a t f", a=2, t=2)
        dv = dst.rearrange("p h (a t f) -> p h a t f", a=2, t=2)
        x1, x2 = qv[:, :, :, 0, :], qv[:, :, :, 1, :]
        cs = self.cosT[:, jc, :, :].unsqueeze(1).broadcast_to([128, nh, 2, 32])
        sn = self.sinT[:, jc, :, :].unsqueeze(1).broadcast_to([128, nh, 2, 32])
        t = [r[:, 0:nh, :, :] for r in rt]
        Rq = [qn_b, self.tab_b]
        self.TT("dve", t[0], x1, cs, ALU.mult, Rq, [rt_b])
        self.TT("dve", t[1], x2, sn, ALU.mult, Rq, [rt_b])
        self.TT("dve", dv[:, :, :, 0, :], t[0], t[1], ALU.subtract, [rt_b], [dstb])
        self.TT("dve", t[0], x2, cs, ALU.mult, Rq + [rt_b], [rt_b])
        self.TT("dve", t[1], x1, sn, ALU.mult, Rq + [rt_b], [rt_b])
        self.TT("dve", dv[:, :, :, 1, :], t[0], t[1], ALU.add, [rt_b], [dstb])

    def l1_norm_rope(self, psrc_banks, nh, gain, jc, dst, dstb):
        for _ in self.l1_norm_rope_gen(psrc_banks, nh, gain, jc, dst, dstb):
            pass

    def l1_q_gen(self, src, srcb, off, i, slot, qs):
        TC, W = self.TC, self.W
        wb = self.wbuf
        for c in range(TC):
            jc = i * TC + c
            q_ = self.qrn % 2
            self.qrn += 1
            pst = self.ps[2][:].bitcast(BF16)
            for k in range(8):
                self.TR(pst[:, k * 128:(k + 1) * 128], self.xbf[slot][:, c, k * 128:(k + 1) * 128], self.ident_b[:],
                        [self.xbf_c[slot][c], self.cb], [self.psb[2]])
            yield
            self.CP("dve", self.xT[:, :, c * 128:(c + 1) * 128], pst.rearrange("p (k t) -> p k t", k=8), [self.psb[2]], [self.xT_b])
            yield
            yield
            for hf in range(2):
                for k in range(8):
                    self.MM(self.ps[2 + hf][:], self.xT[:, k, c * 128:(c + 1) * 128], self.w_in[:, k, hf * 512:(hf + 1) * 512],
                            k == 0, k == 7, [self.xT_b, wb, self.wb_e], [self.psb[2 + hf]])
                    if k % 4 == 3:
                        yield
            yield
            for _ in self.l1_norm_rope_gen([(2, 4), (3, 4)], 8, self.qng, jc, self.qr[q_][:, 0:8, :], self.qr_b[q_]):
                yield
            yield
            for hf in range(2):
                for k in range(8):
                    self.MM(self.ps[2 + hf][:], self.xT[:, k, c * 128:(c + 1) * 128],
                            self.w_in[:, k, 1536 + hf * 512:1536 + (hf + 1) * 512], k == 0, k == 7, [self.xT_b, wb, self.wb_e], [self.psb[2 + hf]])
                    if k % 4 == 3:
                        yield
            yield
            for hf in range(2):
                self.CP("dve", self.zs[qs][:, c, hf * 512:(hf + 1) * 512], self.ps[2 + hf][:], [self.psb[2 + hf]], [self.zs_b[qs]])
            yield
            yield
            pst = self.ps[2][:].bitcast(BF16)
            for h in range(8):
                self.TR(pst[:, h * 128:(h + 1) * 128], self.qr[q_][:, h, :], self.ident_b[:], [self.qr_b[q_], self.cb], [self.psb[2]])
            yield
            yield
            self.CP("dve", self.QT[qs][:, :, c * 128:(c + 1) * 128], pst.rearrange("p (k t) -> p k t", k=8), [self.psb[2]], [self.QT_b[qs]])
            yield
            yield
        zall = self.zs[qs][:].rearrange("p c d -> p (c d)")
        self.ACT(zall, zall, AF.Silu, [self.zs_b[qs]], [self.zs_b[qs]])
        yield

    def l1_xload_gen(self, src, srcb, off, i, slot):
        self.load_x(src, off + i * self.W, slot, srcb)
        for _ in range(16):
            yield

    def l1_epi_gen(self, src, srcb, off, i, os_):
        TC = self.TC
        wb = self.wbuf
        for c in range(TC):
            jc = i * TC + c
            tok = off + jc * 128
            rs = self.resid_begin(src, tok, srcb)
            pst = self.ps[0][:].bitcast(BF16)
            for f in range(8):
                self.TR(pst[:, f * 128:(f + 1) * 128], self.og[os_][:, c, f * 128:(f + 1) * 128], self.ident_b[:],
                        [self.og_b[os_], self.cb], [self.psb[0]])
            yield
            yield
            self.CP("dve", self.ogT[:], pst.rearrange("p (k t) -> p k t", k=8), [self.psb[0]], [self.ogT_b])
            yield
            yield
            for hf in range(2):
                for f in range(8):
                    self.MM(self.ps[0][:], self.ogT[:, f, :], self.w_out[:, f, hf * 512:(hf + 1) * 512], f == 0, f == 7,
                            [self.ogT_b, wb], [self.psb[0]])
                yield
                yield
                yield
                yield
                self.resid_half(rs, hf, 0)
                yield
                yield
            for _ in self.resid_finish_gen(rs, self.y_d, tok, None):
                yield

    def l1_epi_steps(self, src, srcb, off, i, os_):
        return [self.l1_epi_gen(src, srcb, off, i, os_)]

    def l1_q_steps(self, src, srcb, off, i, slot, qs):
        return [self.l1_xload_gen(src, srcb, off, i, slot), self.l1_q_gen(src, srcb, off, i, slot, qs)]

    def l1_att_stage(self, nch, qs, os_, sideA, sideB):
        TC, W = self.TC, self.W
        KB = 512 // W
        nkb = nch // KB
        scale = DH ** -0.5
        iters = [(h, kb) for h in range(8) for kb in range(nkb)]
        SB = (6, 7, 1)
        kA = -(-90 // max(1, len(iters) - 2))
        kB = -(-60 // max(1, len(iters) - 2))

        def emit_S(n):
            h, kb = iters[n]
            kv = h // 4
            sb_ = SB[n % 3]
            for j in range(KB):
                kc = kb * KB + j
                self.MM(self.ps[sb_][:, j * W:(j + 1) * W], self.KT[:, kv, kc * 128:(kc + 1) * 128], self.QT[qs][:, h, :], True, True,
                        [self.KT_b, self.QT_b[qs]], [self.psb[sb_]])
        emit_S(0)
        if len(iters) > 1:
            emit_S(1)
        for n, (h, kb) in enumerate(iters):
            kv = h // 4
            if n + 2 < len(iters):
                emit_S(n + 2)
            sb_ = SB[n % 3]
            pi = self.ptn % 3
            self.ptn += 1
            self.ACT(self.PT[pi][:], self.ps[sb_][:], AF.Exp, [self.psb[sb_]], [self.PT_b[pi]], scale=scale)
            ob = 4 + (h % 2)
            for j in range(KB):
                kc = kb * KB + j
                for qc in range(TC):
                    self.MM(self.ps[ob][:, qc * NV:(qc + 1) * NV], self.PT[pi][:, j * W + qc * 128:j * W + (qc + 1) * 128],
                            self.VA[:, kc, kv, :], kc == 0 and qc == 0, kc == nch - 1, [self.PT_b[pi], self.VA_b], [self.psb[ob]], skip=True)
            if kb == nkb - 1:
                for qc in range(TC):
                    rc = (h % 2) * TC + qc
                    self.S.op("dve", lambda e, ob=ob, rc=rc, qc=qc: e.reciprocal(out=self.rden[:, rc:rc + 1],
                                                                              in_=self.ps[ob][:, qc * NV + 128:qc * NV + 129]),
                              [self.psb[ob]], [self.rden_b])
                    self.STT(self.og[os_][:, qc, h * 128:(h + 1) * 128], self.ps[ob][:, qc * NV:qc * NV + 128], self.rden[:, rc:rc + 1],
                             self.zs[qs][:, qc, h * 128:(h + 1) * 128], ALU.mult, ALU.mult, [self.psb[ob], self.rden_b, self.zs_b[qs]],
                             [self.og_b[os_]])
            for (lst, k_) in ((sideA, kA), (sideB, kB)):
                for _k in range(k_):
                    if lst:
                        try:
                            next(lst[0])
                        except StopIteration:
                            lst.pop(0)
        for lst in (sideB, sideA):
            while lst:
                for _ in lst.pop(0):
                    pass

    def layer1_seq(self, off, T):
        self.alloc_l1()
        TC, W = self.TC, self.W
        nt = T // W
        nch = T // 128
        wb = self.wbuf
        src = self.x1_d if 0 in self.layers else self.d["x"]
        srcb = self.x1_bufs if 0 in self.layers else None
        self.MS("pool", self.VA[:, :, :, 128:129], 1.0, [], [self.VA_b])
        def kv_chunk_gen(i, c):
            jc = i * TC + c
            par = jc % 2
            b = 2 + par
            for k in range(8):
                self.MM(self.ps[b][:], self.xT[:, k, c * 128:(c + 1) * 128], self.w_in[:, k, 1024:1536], k == 0, k == 7,
                        [self.xT_b, self.wb_e], [self.psb[b]])
            self.CP("act", self.VA[:, jc, :, 0:128], self.ps[b][:, 256:512].rearrange("p (h e) -> p h e", h=2),
                    [self.psb[b]], [self.VA_b])
            for _ in self.l1_norm_rope_gen([(b, 2)], 2, self.kng, jc, self.qr[par][:, 0:2, :], self.qr_b[par], self.ktb[par]):
                yield
            yield
            pst = self.ps[par][:].bitcast(BF16)
            for kv in range(2):
                self.TR(pst[:, kv * 128:(kv + 1) * 128], self.qr[par][:, kv, :], self.ident_b[:], [self.qr_b[par], self.cb], [self.psb[par]])
            self.CP("act", self.KT[:, :, jc * 128:(jc + 1) * 128], pst[:, 0:256].rearrange("p (k t) -> p k t", k=2),
                    [self.psb[par]], [self.KT_b])
            yield
        self.load_x(src, off, 0, srcb)
        for i in range(nt):
            slot = i % 2
            for c in range(TC):
                self.xT_chunk(c, c % 2, slot)
            if i + 1 < nt:
                self.load_x(src, off + (i + 1) * W, 1 - slot, srcb)
            self.run_gens([kv_chunk_gen(i, c) for c in range(TC)])
        for g_ in self.l1_q_steps(src, srcb, off, 0, 0, 0):
            for _ in g_:
                pass
        for i in range(nt):
            sideA, sideB = [], []
            if i + 1 < nt:
                sideA.append(self.l1_xload_gen(src, srcb, off, i + 1, (i + 1) % 2))
                sideA.append(self.l1_q_gen(src, srcb, off, i + 1, (i + 1) % 2, (i + 1) % 2))
            if i >= 1:
                sideB += self.l1_epi_steps(src, srcb, off, i - 1, (i - 1) % 2)
            self.l1_att_stage(nch, i % 2, i % 2, sideA, sideB)
        for g_ in self.l1_epi_steps(src, srcb, off, nt - 1, (nt - 1) % 2):
            for _ in g_:
                pass


def build_program(seq_lens, TC=2, layers=(0, 1)):
    k = K(seq_lens, TC=TC, layers=layers)
    return k.build()


_INPUT_NAMES = ["w_in_even", "w_pool", "pool_scale", "conv_w", "conv_b", "w_q_m", "w_k_m", "b_gate_i", "b_gate_f",
                "mh_norm_g", "skip", "w_out_even", "w_in_odd", "q_norm_g", "k_norm_g", "w_out_odd", "ln_g", "ln_b"]
_SHAPES = {"w_in_even": (D, IN0), "w_pool": (4, 128, 128), "pool_scale": (512,), "conv_w": (3, 512), "conv_b": (512,),
           "w_q_m": (4, 128, 128), "w_k_m": (4, 128, 128), "b_gate_i": (8,), "b_gate_f": (8,), "mh_norm_g": (512,),
           "skip": (512,), "w_out_even": (D, D), "w_in_odd": (D, IN1), "q_norm_g": (128,), "k_norm_g": (128,),
           "w_out_odd": (D, D), "ln_g": (2, D), "ln_b": (2, D)}


def weight_map(inputs):
    m = {}
    for nm in _INPUT_NAMES:
        m[nm] = np.ascontiguousarray(np.asarray(inputs[nm], dtype=np.float32).reshape(_SHAPES[nm]))
    return m


def kernel(**inputs):
    xp = np.asarray(inputs["x_prompt"], dtype=np.float32)
    xs = np.asarray(inputs["x_sample"], dtype=np.float32)
    n = 8
    nc = build_program([4096, 2048, 2048], TC=2, layers=(0, 1))
    wm = weight_map(inputs)
    in_maps = []
    for c in range(n):
        xcat = np.concatenate([xp[c], xs[2 * c], xs[2 * c + 1]], axis=0)
        m = dict(wm)
        m["x"] = np.ascontiguousarray(xcat)
        in_maps.append(m)
    res = run_bass_kernel_spmd(nc, in_maps, core_ids=list(range(n)))
    yp = np.empty_like(xp)
    ys = np.empty_like(xs)
    for c in range(n):
        y = res.results[c]["y"]
        yp[c] = y[0:4096]
        ys[2 * c] = y[4096:6144]
        ys[2 * c + 1] = y[6144:8192]
    return (yp, ys)
```

```python
import math
from contextlib import ExitStack

import numpy as np
import concourse.bass as bass
import concourse.mybir as mybir
from concourse.bass_utils import run_bass_kernel_spmd

F32 = mybir.dt.float32
BF16 = mybir.dt.bfloat16
I32 = mybir.dt.int32
AF = mybir.ActivationFunctionType
ALU = mybir.AluOpType
AX = mybir.AxisListType

D = 1024
IN0 = 3088
IN1 = 2560
ALPHA = 4.0 ** 0.25
LN_EPS = 1e-5
RMS_EPS = 1e-6
POOL_WINDOWS = (2, 4, 8, 16)
DH = 128
NV = 129


class Buf:
    __slots__ = ("name", "w", "r")

    def __init__(self, name=""):
        self.name = name
        self.w = None
        self.r = {}


class Sched:
    def __init__(self, nc, stack):
        self.nc = nc
        self.stack = stack
        self.sem = {}
        self.cnt = {}
        self.known = {}
        self.prog = {}
        for e in ("pe", "dve", "act", "pool", "sp"):
            self.sem[e] = stack.enter_context(nc.semaphore("s_" + e))
            self.cnt[e] = 0
            self.known[e] = {}
            self.prog[e] = []

    def _waits(self, eng, reads, writes):
        need = {}
        for b in reads:
            if b.w is not None and need.get(b.w[0], 0) < b.w[1]:
                need[b.w[0]] = b.w[1]
        for b in writes:
            if b.w is not None and need.get(b.w[0], 0) < b.w[1]:
                need[b.w[0]] = b.w[1]
            for s, c in b.r.items():
                if need.get(s, 0) < c:
                    need[s] = c
        waits = []
        kn = self.known[eng]
        for s, c in need.items():
            if s == "pe" and eng == "pe":
                continue
            if kn.get(s, 0) < c:
                kn[s] = c
                waits.append((self.sem[s], c * (16 if s.startswith("dq_") else 1)))
        return waits

    def _mark(self, src, my, reads, writes):
        for b in reads:
            if b.r.get(src, 0) < my:
                b.r[src] = my
        for b in writes:
            b.w = (src, my)
            b.r = {}

    def op(self, eng, fn, reads=(), writes=()):
        waits = self._waits(eng, reads, writes)
        self.cnt[eng] += 1
        my = self.cnt[eng]
        sem = self.sem[eng]

        def emit(e):
            for s, v in waits:
                e.wait_ge(s, v)
            fn(e).then_inc(sem, 1)
        self.prog[eng].append(emit)
        self._mark(eng, my, reads, writes)

    def dma(self, eng, out, in_, key, reads=(), writes=(), slow=False):
        q = "dq_" + key
        if q not in self.sem:
            self.sem[q] = self.stack.enter_context(self.nc.semaphore("s_" + q))
            self.cnt[q] = 0
        waits = self._waits(eng, reads, writes)
        self.cnt[q] += 1
        my = self.cnt[q]
        sem = self.sem[q]

        def emit(e):
            for s, v in waits:
                e.wait_ge(s, v)
            if slow:
                e.dma_start(out=out, in_=in_, allow_slow_non_contiguous=True).then_inc(sem, 16)
            else:
                e.dma_start(out=out, in_=in_).then_inc(sem, 16)
        self.prog[eng].append(emit)
        self._mark(q, my, reads, writes)

    def barrier(self):
        targets = [(q, self.cnt[q]) for q in self.sem if self.cnt.get(q, 0) > 0]
        for eng in ("pe", "dve", "act", "pool", "sp"):
            waits = []
            for q, c in targets:
                if q == eng and eng in ("pe", "sp"):
                    continue
                if self.known[eng].get(q, 0) < c:
                    self.known[eng][q] = c
                    waits.append((self.sem[q], c * (16 if q.startswith("dq_") else 1)))

            def emit(e, waits=waits):
                for s_, v in waits:
                    e.wait_ge(s_, v)
            self.prog[eng].append(emit)

    def finish(self, eng):
        targets = [(self.sem[q], self.cnt[q] * 16) for q in self.sem if q.startswith("dq_") and self.cnt[q] > 0]
        targets += [(self.sem[e], self.cnt[e]) for e in ("pe", "dve", "act", "pool") if self.cnt[e] > 0]

        def emit(e):
            for s, v in targets:
                e.wait_ge(s, v)
        self.prog[eng].append(emit)

    def emit_all(self, block):
        progs = self.prog

        @block.sync
        def _(e):
            for f in progs["sp"]:
                f(e)

        @block.tensor
        def _(e):
            for f in progs["pe"]:
                f(e)

        @block.vector
        def _(e):
            for f in progs["dve"]:
                f(e)

        @block.scalar
        def _(e):
            for f in progs["act"]:
                f(e)

        @block.gpsimd
        def _(e):
            for f in progs["pool"]:
                f(e)


class K:
    def __init__(self, seq_lens, TC=2, layers=(0, 1), dbg=False):
        self.seq_lens = list(seq_lens)
        self.TC = TC
        self.W = 128 * TC
        self.layers = layers
        self.NT = sum(seq_lens)
        self.maxch = max(seq_lens) // 128
        self.nc = bass.Bass("TRN2", target_bir_lowering=False)
        self.st = ExitStack()
        self.cur = self.st
        self.uid = 0

    def sb(self, shape, dt=F32, name=None):
        self.uid += 1
        return self.cur.enter_context(self.nc.sbuf_tensor(name or ("t%d" % self.uid), list(shape), dt))

    def din(self, name, shape, dt=F32):
        return self.nc.dram_tensor(name, list(shape), dt, kind="ExternalInput").ap()

    def ACT(self, out, in_, func, R, Wr, bias=None, scale=None, accum=None):
        kw = {}
        if bias is not None:
            kw["bias"] = bias
        if scale is not None:
            kw["scale"] = scale
        if accum is not None:
            kw["accum_out"] = accum
        self.S.op("act", lambda e: e.activation(out=out, in_=in_, func=func, **kw), R, Wr)

    def TT(self, eng, out, in0, in1, op, R, Wr):
        self.S.op(eng, lambda e: e.tensor_tensor(out=out, in0=in0, in1=in1, op=op), R, Wr)

    def TS(self, eng, out, in0, s1, s2, op0, op1, R, Wr):
        if op1 is None:
            self.S.op(eng, lambda e: e.tensor_scalar(out=out, in0=in0, scalar1=s1, scalar2=None, op0=op0), R, Wr)
        else:
            self.S.op(eng, lambda e: e.tensor_scalar(out=out, in0=in0, scalar1=s1, scalar2=s2, op0=op0, op1=op1), R, Wr)

    def STT(self, out, in0, scalar, in1, op0, op1, R, Wr):
        self.S.op("dve", lambda e: e.scalar_tensor_tensor(out=out, in0=in0, scalar=scalar, in1=in1, op0=op0, op1=op1), R, Wr)

    def CP(self, eng, out, in_, R, Wr):
        if eng == "act":
            self.S.op("act", lambda e: e.activation(out=out, in_=in_, func=AF.Copy), R, Wr)
        else:
            self.S.op(eng, lambda e: e.tensor_copy(out=out, in_=in_), R, Wr)

    def MM(self, out, lhsT, rhs, start, stop, R, Wr, skip=False):
        if skip:
            self.S.op("pe", lambda e: e.matmul(out, lhsT=lhsT, rhs=rhs, start=start, stop=stop, skip_group_check=True), R, Wr)
        else:
            self.S.op("pe", lambda e: e.matmul(out, lhsT=lhsT, rhs=rhs, start=start, stop=stop), R, Wr)

    def TR(self, out, in_, ident, R, Wr):
        self.S.op("pe", lambda e: e.transpose(out=out, in_=in_, identity=ident), R, Wr)

    def MS(self, eng, out, val, R, Wr):
        self.S.op(eng, lambda e: e.memset(out, val), R, Wr)

    def DMA(self, eng, out, in_, key, R, Wr, slow=False):
        self.S.dma(eng, out, in_, key, R, Wr, slow=slow)

    def build(self):
        nc = self.nc
        st = self.st
        TC, W, NT = self.TC, self.W, self.NT
        d = {}
        d["x"] = self.din("x", [NT, D])
        d["w_in_even"] = self.din("w_in_even", [D, IN0])
        d["w_pool"] = self.din("w_pool", [4, 128, 128])
        d["pool_scale"] = self.din("pool_scale", [512])
        d["conv_w"] = self.din("conv_w", [3, 512])
        d["conv_b"] = self.din("conv_b", [512])
        d["w_q_m"] = self.din("w_q_m", [4, 128, 128])
        d["w_k_m"] = self.din("w_k_m", [4, 128, 128])
        d["b_gate_i"] = self.din("b_gate_i", [8])
        d["b_gate_f"] = self.din("b_gate_f", [8])
        d["mh_norm_g"] = self.din("mh_norm_g", [512])
        d["skip"] = self.din("skip", [512])
        d["w_out_even"] = self.din("w_out_even", [D, D])
        d["w_in_odd"] = self.din("w_in_odd", [D, IN1])
        d["q_norm_g"] = self.din("q_norm_g", [128])
        d["k_norm_g"] = self.din("k_norm_g", [128])
        d["w_out_odd"] = self.din("w_out_odd", [D, D])
        d["ln_g"] = self.din("ln_g", [2, D])
        d["ln_b"] = self.din("ln_b", [2, D])
        self.d = d
        self.y_d = nc.dram_tensor("y", [NT, D], F32, kind="ExternalOutput").ap()
        self.x1_d = nc.dram_tensor("x1s", [NT, D], F32, kind="Internal").ap()
        self.bst_d = nc.dram_tensor("bsts", [self.maxch, 128, 4 * NV], BF16, kind="Internal").ap()
        self.x1_bufs = {}
        self.bst_bufs = {}

        self.S = Sched(nc, st)
        self.ps = []
        self.psb = []
        for i in range(8):
            self.ps.append(st.enter_context(nc.psum_tensor("ps%d" % i, [128, 512], F32)))
            self.psb.append(Buf("ps%d" % i))
        self.conv_decl = 0
        self.bconv_decl = 0
        self.xloaded_tok = -1
        self.alloc_common()
        self.setup_consts()
        if 0 in self.layers:
            l0st = ExitStack()
            self.cur = l0st
            self.setup_consts_l0()
            self.load_weights(0)
            off = 0
            for T in self.seq_lens:
                self.layer0_seq(off, T)
                off += T
            self.S.barrier()
            l0st.close()
            self.cur = self.st
        if 1 in self.layers:
            self.load_weights(1)
            off = 0
            for T in self.seq_lens:
                self.layer1_seq(off, T)
                off += T
        self.S.finish("sp")
        block = st.enter_context(nc.Block())
        self.S.emit_all(block)
        st.close()
        return nc

    def setup_consts(self):
        nc = self.nc
        cb = Buf("consts")
        self.cb = cb
        R, Wr = [cb], [cb]
        self.ident_f = self.sb([128, 128], F32, "ident_f")
        self.ident_b = self.sb([128, 128], BF16, "ident_b")
        S = self.S
        self.MS("pool", self.ident_f[:], 0.0, [], Wr)
        S.op("pool", lambda e: e.affine_select(out=self.ident_f[:], in_=self.ident_f[:], pattern=[[-1, 128]],
                                               compare_op=ALU.not_equal, fill=1.0, base=0, channel_multiplier=1), R, Wr)
        self.CP("dve", self.ident_b[:], self.ident_f[:], R, Wr)
        self.lng = self.sb([128, D], F32, "lng")
        self.lnb = self.sb([128, D], F32, "lnb")
        self.lnbuf = Buf("ln")
        self.epsc = self.sb([128, 2], F32, "epsc")
        self.MS("pool", self.epsc[:, 0:1], LN_EPS, [], Wr)
        self.MS("pool", self.epsc[:, 1:2], RMS_EPS, [], Wr)
        self.qng = self.sb([128, 128], F32, "qng")
        self.kng = self.sb([128, 128], F32, "kng")
        self.DMA("sp", self.qng[:], self.d["q_norm_g"].partition_broadcast(128), "c0", [], Wr)
        self.DMA("sp", self.kng[:], self.d["k_norm_g"].partition_broadcast(128), "c0", [], Wr)
        self.w_in = self.sb([128, 8, IN0], BF16, "w_in")
        self.w_out = self.sb([128, 8, D], BF16, "w_out")
        self.wbuf = Buf("weights")
        self.wb_e = Buf("weights_early")

    def setup_consts_l0(self):
        cb = self.cb
        R, Wr = [cb], [cb]
        S = self.S
        self.triF = self.sb([128, 128], F32, "triF")
        self.triB = self.sb([128, 128], F32, "triB")
        self.ones_f = self.sb([128, 128], F32, "ones_f")
        self.maskF = self.sb([128, 128], BF16, "maskF")
        self.maskB = self.sb([128, 128], BF16, "maskB")
        self.MS("pool", self.ones_f[:], 1.0, [], Wr)
        S.op("pool", lambda e: e.affine_select(out=self.triF[:], in_=self.ones_f[:], pattern=[[1, 128]],
                                               compare_op=ALU.is_ge, fill=0.0, base=0, channel_multiplier=-1), R, Wr)
        S.op("pool", lambda e: e.affine_select(out=self.triB[:], in_=self.ones_f[:], pattern=[[-1, 128]],
                                               compare_op=ALU.is_ge, fill=0.0, base=0, channel_multiplier=1), R, Wr)
        self.CP("dve", self.maskF[:], self.triF[:], R, Wr)
        self.CP("dve", self.maskB[:], self.triB[:], R, Wr)
        self.bgi = self.sb([128, 8], F32, "bgi")
        self.bgf = self.sb([128, 8], F32, "bgf")
        self.DMA("sp", self.bgi[:], self.d["b_gate_i"].partition_broadcast(128), "c1", [], Wr)
        self.DMA("sp", self.bgf[:], self.d["b_gate_f"].partition_broadcast(128), "c1", [], Wr)
        self.pscale = self.sb([128, 4], F32, "pscale")
        self.convw = self.sb([128, 3, 4], F32, "convw")
        self.convb = self.sb([128, 4], F32, "convb")
        self.mhg = self.sb([128, 4], F32, "mhg")
        self.skipv = self.sb([128, 4], F32, "skipv")
        self.DMA("sp", self.pscale[:], self.d["pool_scale"].rearrange("(g p) -> p g", p=128), "c1", [], Wr, slow=True)
        self.DMA("sp", self.convb[:], self.d["conv_b"].rearrange("(g p) -> p g", p=128), "c1", [], Wr, slow=True)
        self.DMA("sp", self.mhg[:], self.d["mh_norm_g"].rearrange("(g p) -> p g", p=128), "c1", [], Wr, slow=True)
        self.DMA("sp", self.skipv[:], self.d["skip"].rearrange("(g p) -> p g", p=128), "c1", [], Wr, slow=True)
        self.DMA("sp", self.convw[:], self.d["conv_w"].rearrange("j (g p) -> p j g", p=128), "c1", [], Wr, slow=True)
        self.w_small = self.sb([128, 3, 4, 128], BF16, "w_small")
        self.setup_pool_bands()

    def setup_pool_bands(self):
        S = self.S
        cb = self.cb
        R, Wr = [cb], [cb]
        self.PB = self.sb([128, 4, 5, 128], BF16, "PB")
        R = [cb, self.res_b[0]]
        Wr = [cb, self.res_b[0]]
        ws = self.res[0][:]
        tmp, tmp2, iot, rc, t2 = (ws[:, 0:128], ws[:, 128:256], ws[:, 256:384], ws[:, 384:512], ws[:, 512:640])

        class _V:
            def __init__(self, ap):
                self.ap = ap

            def __getitem__(self, k):
                return self.ap
        tmp, tmp2, iot, rc, t2 = _V(tmp), _V(tmp2), _V(iot), _V(rc), _V(t2)
        S.op("pool", lambda e: e.iota(iot[:], pattern=[[1, 128]], base=0, channel_multiplier=0,
                                      allow_small_or_imprecise_dtypes=True), R, Wr)
        for g, w in enumerate(POOL_WINDOWS):
            left = w // 2
            right = w - 1 - left

            def band(dst, base_lo, base_hi, val):
                self.MS("pool", dst, val, R, Wr)
                S.op("pool", lambda e: e.affine_select(out=dst, in_=dst, pattern=[[-1, 128]], compare_op=ALU.is_ge,
                                                       fill=0.0, base=base_lo, channel_multiplier=1), R, Wr)
                S.op("pool", lambda e: e.affine_select(out=dst, in_=dst, pattern=[[1, 128]], compare_op=ALU.is_ge,
                                                       fill=0.0, base=base_hi, channel_multiplier=-1), R, Wr)
            band(tmp[:], left - 128, 10000, 1.0 / w)
            self.CP("dve", self.PB[:, g, 0, :], tmp[:], R, Wr)
            band(tmp[:], 10000, right - 128, 1.0 / w)
            self.CP("dve", self.PB[:, g, 4, :], tmp[:], R, Wr)
            band(tmp[:], left, right, 1.0)
            self.TS("dve", tmp2[:], tmp[:], 1.0 / w, None, ALU.mult, None, R, Wr)
            self.TT("dve", tmp2[:], tmp2[:], self.ident_f[:], ALU.subtract, R, Wr)
            self.CP("dve", self.PB[:, g, 2, :], tmp2[:], R, Wr)
            self.TS("dve", t2[:], iot[:], float(-left), 0.0, ALU.add, ALU.max, R, Wr)
            self.STT(rc[:], iot[:], float(right + 1), t2[:], ALU.add, ALU.subtract, R, Wr)
            S.op("dve", lambda e: e.reciprocal(out=rc[:], in_=rc[:]), R, Wr)
            self.TT("dve", tmp2[:], tmp[:], rc[:], ALU.mult, R, Wr)
            self.TT("dve", tmp2[:], tmp2[:], self.ident_f[:], ALU.subtract, R, Wr)
            self.CP("dve", self.PB[:, g, 1, :], tmp2[:], R, Wr)
            self.TS("dve", t2[:], iot[:], -1.0, 128.0, ALU.mult, ALU.add, R, Wr)
            self.TS("dve", rc[:], t2[:], float(right + 1), float(left), ALU.min, ALU.add, R, Wr)
            S.op("dve", lambda e: e.reciprocal(out=rc[:], in_=rc[:]), R, Wr)
            self.TT("dve", tmp2[:], tmp[:], rc[:], ALU.mult, R, Wr)
            self.TT("dve", tmp2[:], tmp2[:], self.ident_f[:], ALU.subtract, R, Wr)
            self.CP("dve", self.PB[:, g, 3, :], tmp2[:], R, Wr)

    def load_weights(self, layer):
        d = self.d
        wb = self.wbuf
        if layer == 0:
            win, ncol, wout = d["w_in_even"], IN0, d["w_out_even"]
        else:
            win, ncol, wout = d["w_in_odd"], IN1, d["w_out_odd"]
        we = self.wb_e
        win3 = win.rearrange("(k p) c -> p k c", p=128)
        wout3 = wout.rearrange("(k p) c -> p k c", p=128)

        def cols(c0, c1, key, buf):
            self.DMA("pool", self.w_in[:, :, c0:c1], win3[:, :, c0:c1], key, [], [buf])
        if layer == 0:
            cols(1024, 1536, "wle", we)
            cols(1536, 2048, "wle", we)
            cols(3072, 3088, "wle", we)
            for j, nm in enumerate(("w_pool", "w_q_m", "w_k_m")):
                self.DMA("pool", self.w_small[:, j, :, :], d[nm].rearrange("h d e -> d h e"), "wle", [], [we])
            cols(0, 1024, "wld", wb)
            cols(2048, 3072, "wld", wb)
        else:
            cols(1024, 1536, "wle", we)
            cols(0, 1024, "wld", wb)
            cols(1536, 2560, "wld", wb)
        self.DMA("pool", self.w_out[:], wout3, "wld", [], [wb])
        self.DMA("sp", self.lng[:], d["ln_g"][layer].partition_broadcast(128), "lnp", [], [self.lnbuf])
        self.DMA("sp", self.lnb[:], d["ln_b"][layer].partition_broadcast(128), "lnp", [], [self.lnbuf])

    def alloc_common(self):
        if hasattr(self, "xin"):
            return
        TC, W = self.TC, self.W
        self.xin = True
        self.xbf = [self.sb([128, TC, D], BF16, "xbf%d" % i) for i in range(2)]
        self.xbf_b = Buf("xbf")
        self.xbf_c = [[Buf() for _ in range(TC)] for _ in range(2)]
        self.xT = self.sb([128, 8, W], BF16, "xT")
        self.xT_b = Buf("xT")
        self.res = [self.sb([128, D], F32, "res%d" % i) for i in range(2)]
        self.res_b = [Buf("res0"), Buf("res1")]
        self.stat_l = [self.sb([128, 2, 6], F32, "stat%d" % i) for i in range(2)]
        self.mv_l = [self.sb([128, 8], F32, "mv%d" % i) for i in range(2)]
        self.stat_bl = [Buf("stat0"), Buf("stat1")]
        self.resn = 0

    def load_x(self, src_d, tok0, sl, src_bufs=None):
        TC, W = self.TC, self.W
        R = []
        if src_bufs is not None:
            for c in range(TC):
                R.append(src_bufs[(tok0 // 128) + c])
        self.DMA("pool", self.xbf[sl][:], src_d[tok0:tok0 + W, :].rearrange("(c p) d -> p c d", p=128),
                 "xbf%d" % sl, R, self.xbf_c[sl])

    def xT_chunk(self, c, bank, sl):
        pst = self.ps[bank][:].bitcast(BF16)
        for k in range(8):
            self.TR(pst[:, k * 128:(k + 1) * 128], self.xbf[sl][:, c, k * 128:(k + 1) * 128], self.ident_b[:],
                    [self.xbf_c[sl][c], self.cb], [self.psb[bank]])
        self.CP(("act", "dve")[c % 2], self.xT[:, :, c * 128:(c + 1) * 128],
                pst.rearrange("p (k t) -> p k t", k=8), [self.psb[bank]], [self.xT_b])

    def load_xT(self, src_d, tok0, sl, src_bufs=None):
        self.load_x(src_d, tok0, sl, src_bufs)
        for c in range(self.TC):
            self.xT_chunk(c, c % 2, sl)

    def resid_begin(self, src_d, tok, src_bufs=None):
        rs = self.resn % 2
        self.resn += 1
        R = [src_bufs[tok // 128]] if src_bufs is not None else []
        self.DMA("sp", self.res[rs][:], src_d[tok:tok + 128, :], "resx%d" % rs, R, [self.res_b[rs]])
        return rs

    def resid_half(self, rs, hf, bank):
        res, rb = self.res[rs], self.res_b[rs]
        self.STT(res[:, hf * 512:(hf + 1) * 512], res[:, hf * 512:(hf + 1) * 512], ALPHA,
                 self.ps[bank][:], ALU.mult, ALU.add, [rb, self.psb[bank]], [rb])

    def resid_finish_gen(self, rs, dst_d, tok, dst_bufs=None):
        res, rb = self.res[rs], self.res_b[rs]
        sb_ = self.stat_bl[rs]
        stat, mv = self.stat_l[rs], self.mv_l[rs]
        for hf in range(2):
            self.S.op("dve", lambda e, hf=hf: e.bn_stats(out=stat[:, hf, :], in_=res[:, hf * 512:(hf + 1) * 512]), [rb], [sb_])
        self.S.op("dve", lambda e: e.bn_aggr(out=mv[:, 0:2], in_=stat[:].rearrange("p a b -> p (a b)")), [sb_], [sb_])
        yield
        yield
        yield
        yield
        self.ACT(mv[:, 3:4], mv[:, 1:2], AF.Ln, [sb_], [sb_], bias=self.epsc[:, 0:1])
        self.ACT(mv[:, 4:5], mv[:, 3:4], AF.Exp, [sb_], [sb_], scale=-0.5)
        yield
        yield
        self.STT(res[:], res[:], mv[:, 0:1], self.lng[:], ALU.subtract, ALU.mult, [rb, sb_, self.lnbuf], [rb])
        self.STT(res[:], res[:], mv[:, 4:5], self.lnb[:], ALU.mult, ALU.add, [rb, sb_, self.lnbuf], [rb])
        yield
        yield
        yield
        yield
        yield
        yield
        Wr = []
        if dst_bufs is not None:
            b = dst_bufs.setdefault(tok // 128, Buf("dst%d" % (tok // 128)))
            Wr = [b]
        self.DMA("sp", dst_d[tok:tok + 128, :], res[:], "res%d" % rs, [rb], Wr)

    def resid_finish(self, rs, dst_d, tok, dst_bufs=None):
        for _ in self.resid_finish_gen(rs, dst_d, tok, dst_bufs):
            pass

    def resid_ln_store(self, ybanks, src_d, src_bufs, dst_d, tok, dst_bufs=None):
        rs = self.resid_begin(src_d, tok, src_bufs)
        for hf in range(2):
            self.resid_half(rs, hf, ybanks[hf])
        self.resid_finish(rs, dst_d, tok, dst_bufs)

    def alloc_l0(self):
        if hasattr(self, "xa_tm"):
            return
        TC, W = self.TC, self.W
        self.alloc_common()
        self.xa_tm = [self.sb([128, TC, 512], BF16, "xa%d" % i) for i in range(3)]
        self.xa_b = [Buf() for _ in range(3)]
        self.vaug = [self.sb([128, TC, 4, NV], BF16, "vaug%d" % i) for i in range(2)]
        self.vaug_b = [Buf() for _ in range(2)]
        self.obs = [self.sb([128, TC, 512], BF16, "obs%d" % i) for i in range(2)]
        self.obs_b = [Buf() for _ in range(2)]
        self.zas = [self.sb([128, 4, W], BF16, "zas%d" % i) for i in range(2)]
        self.zas_b = [Buf() for _ in range(2)]
        self.zbs = [self.sb([128, 4, W], BF16, "zbs%d" % i) for i in range(2)]
        self.zbs_b = [Buf() for _ in range(2)]
        self.xbT = [self.sb([128, 4, W + 2], F32, "xbT%d" % i) for i in range(2)]
        self.xbT_b = [Buf() for _ in range(2)]
        self.gig = [self.sb([128, TC, 8], F32, "gig%d" % i) for i in range(2)]
        self.gnf = [self.sb([128, TC, 8], F32, "gnf%d" % i) for i in range(2)]
        self.gate_b = [Buf() for _ in range(2)]
        self.gtmp = self.sb([128, TC, 8], F32, "gtmp")
        self.gtmp_b = Buf()
        self.ctmp = [self.sb([128, W], F32, "ctmp%d" % i) for i in range(2)]
        self.ctmp_b = [Buf(), Buf()]
        self.xc = self.sb([128, 4, W], F32, "xc")
        self.xcb_l = [self.sb([128, 4, W], BF16, "xcb%d" % i) for i in range(2)]
        self.skx_l = [self.sb([128, 4, W], F32, "skx%d" % i) for i in range(2)]
        self.xc_b = Buf()
        self.xcf_b = [Buf() for _ in range(4)]
        self.xcb_bl = [[Buf() for _ in range(4)] for _ in range(2)]
        self.skx_bl = [Buf(), Buf()]
        self.qT_l = [self.sb([128, 4, W], BF16, "qT%d" % i) for i in range(2)]
        self.kT_l = [self.sb([128, 4, W], BF16, "kT%d" % i) for i in range(2)]
        self.qk_bl = [Buf(), Buf()]
        P2 = range(2)
        self.kp = [self.sb([128, 4, 128], BF16, "kp%d" % i) for i in P2]
        self.kp_b = [Buf() for _ in P2]
        self.vp = [[self.sb([128, 4, NV], BF16, "vp%d_%d" % (i, j)) for j in range(2)] for i in P2]
        self.vp_b = [[Buf(), Buf()] for _ in P2]
        self.avec_l = [self.sb([128, TC, 8], F32, "avec%d" % i) for i in range(2)]
        self.bvec_l = [self.sb([128, TC, 8], F32, "bvec%d" % i) for i in range(2)]
        self.egv_l = [self.sb([128, TC, 8], F32, "egv%d" % i) for i in range(2)]
        self.gv_bl = [Buf(), Buf()]
        self.pT = [self.sb([128, 2, 4, 128], BF16, "pT%d" % i) for i in P2]
        self.pT_b = [Buf() for _ in P2]
        self.nd = [self.sb([128, 8, 128], F32, "nd%d" % i) for i in P2]
        self.nd_b = [Buf() for _ in P2]
        self.dd = [self.sb([128, 24], F32, "dd%d" % i) for i in P2]
        self.dd_b = [Buf() for _ in P2]
        self.hs = [self.sb([128, 4, 128], F32, "hs%d" % i) for i in P2]
        self.hs2 = [self.sb([128, 4, 128], F32, "hs2%d" % i) for i in P2]
        self.hs_b = [Buf() for _ in P2]
        self.hs2_b = [Buf() for _ in P2]
        self.hstat = [self.sb([128, 4, 6], F32, "hstat%d" % i) for i in P2]
        self.hmv = [self.sb([128, 4, 2], F32, "hmv%d" % i) for i in P2]
        self.hsc = [self.sb([128, 3, 4], F32, "hsc%d" % i) for i in P2]
        self.hst_b = [Buf() for _ in P2]
        self.obT = self.sb([128, 4, 128], F32, "obT")
        self.obT_b = Buf()
        self.outbT = [self.sb([128, 4, 128], BF16, "outbT%d" % i) for i in P2]
        self.outbT_b = [Buf() for _ in P2]
        self.pooledT = [self.sb([128, 4, 128], BF16, "pooledT%d" % i) for i in P2]
        self.pooledT_b = [Buf() for _ in P2]
        self.outaT = [self.sb([128, 4, 128], BF16, "outaT%d" % i) for i in P2]
        self.outaT_b = [Buf() for _ in P2]
        self.oat = self.sb([128, 4, 128], F32, "oat")
        self.oat_b = Buf()
        self.Fm = self.sb([128, 4, NV], F32, "Fm")
        self.Fbf = self.sb([128, 4, NV], BF16, "Fbf")
        self.F_b = Buf()
        self.Fbf_b = Buf()
        self.Bbf = [self.sb([128, 4, NV], BF16, "Bbf%d" % i) for i in range(2)]
        self.Bbf_b = [Buf() for _ in range(2)]
        self.bstn = 0

    def l0_proj(self, off, T, i, slot, slot3, full):
        TC, W = self.TC, self.W
        nt = T // W
        tok0 = off + i * W
        wb = self.wbuf
        self.load_xT(self.d["x"], tok0, slot)
        pn = [0]

        def bank():
            b = 2 + (pn[0] % 2)
            pn[0] += 1
            return b
        en = [0]

        def eng2():
            en[0] += 1
            return ("act", "dve")[en[0] % 2]

        def proj_tm(col0, ncols, c):
            b = bank()
            for k in range(8):
                self.MM(self.ps[b][:, 0:ncols], self.xT[:, k, c * 128:(c + 1) * 128], self.w_in[:, k, col0:col0 + ncols],
                        k == 0, k == 7, [self.xT_b, wb, self.wb_e], [self.psb[b]])
            return b

        def proj_fm(col0):
            b = bank()
            for k in range(8):
                self.MM(self.ps[b][:, 0:W], self.w_in[:, k, col0:col0 + 128], self.xT[:, k, :],
                        k == 0, k == 7, [self.xT_b, wb, self.wb_e], [self.psb[b]])
            return b
        for c in range(TC):
            if full:
                b = proj_tm(0, 512, c)
                self.CP("act", self.xa_tm[slot3][:, c, :], self.ps[b][:], [self.psb[b]], [self.xa_b[slot3]])
            b = proj_tm(1536, 512, c)
            self.CP(eng2(), self.vaug[slot][:, c, :, 0:128], self.ps[b][:].rearrange("p (h e) -> p h e", h=4),
                    [self.psb[b]], [self.vaug_b[slot]])
            if full:
                b = proj_tm(2048, 512, c)
                self.ACT(self.obs[slot][:, c, :], self.ps[b][:], AF.Sigmoid, [self.psb[b]], [self.obs_b[slot]])
        b = bank()
        for c in range(TC):
            for k in range(8):
                self.MM(self.ps[b][:, c * 16:(c + 1) * 16], self.xT[:, k, c * 128:(c + 1) * 128], self.w_in[:, k, 3072:3088],
                        k == 0, k == 7, [self.xT_b, wb, self.wb_e], [self.psb[b]])
        gps = self.ps[b][:, 0:TC * 16].rearrange("p (c g) -> p c g", c=TC)
        gb = self.gate_b[slot]
        self.TT("dve", self.gig[slot][:], gps[:, :, 0:8], self.bgi[:].unsqueeze(1).broadcast_to([128, TC, 8]), ALU.add,
                [self.psb[b], self.cb], [gb])
        self.TT("dve", self.gtmp[:], gps[:, :, 8:16], self.bgf[:].unsqueeze(1).broadcast_to([128, TC, 8]), ALU.add,
                [self.psb[b], self.cb], [self.gtmp_b])
        self.ACT(self.gtmp[:], self.gtmp[:], AF.Exp, [self.gtmp_b], [self.gtmp_b], scale=-1.0)
        self.ACT(self.gnf[slot][:], self.gtmp[:], AF.Ln, [self.gtmp_b], [gb], bias=1.0)
        if full:
            for g in range(4):
                b = proj_fm(512 + g * 128)
                self.ACT(self.zas[slot][:, g, :], self.ps[b][:, 0:W], AF.Silu, [self.psb[b]], [self.zas_b[slot]])
        for f in range(4):
            b = proj_fm(1024 + f * 128)
            self.CP(eng2(), self.xbT[slot][:, f, 1:W + 1], self.ps[b][:, 0:W], [self.psb[b]], [self.xbT_b[slot]])
        if full:
            for f in range(4):
                b = proj_fm(2560 + f * 128)
                self.ACT(self.zbs[slot][:, f, :], self.ps[b][:, 0:W], AF.Silu, [self.psb[b]], [self.zbs_b[slot]])

    def l0_halo(self, slot_lo, slot_hi, has_lo, has_hi):
        W = self.W
        if has_lo and has_hi:
            self.CP("act", self.xbT[slot_lo][:, :, W + 1:W + 2], self.xbT[slot_hi][:, :, 1:2],
                    [self.xbT_b[slot_hi]], [self.xbT_b[slot_lo]])
            self.CP("act", self.xbT[slot_hi][:, :, 0:1], self.xbT[slot_lo][:, :, W:W + 1],
                    [self.xbT_b[slot_lo]], [self.xbT_b[slot_hi]])
        elif has_hi:
            self.MS("pool", self.xbT[slot_hi][:, :, 0:1], 0.0, [], [self.xbT_b[slot_hi]])
        elif has_lo:
            self.MS("pool", self.xbT[slot_lo][:, :, W + 1:W + 2], 0.0, [], [self.xbT_b[slot_lo]])

    def l0_conv_qk(self, slot, need_q):
        TC, W = self.TC, self.W
        xb, xbb = self.xbT[slot], self.xbT_b[slot]
        for f in range(4):
            ct, ctb = self.ctmp[f % 2], self.ctmp_b[f % 2]
            self.TS("dve", ct[:], xb[:, f, 1:W + 1], self.convw[:, 1, f:f + 1], self.convb[:, f:f + 1], ALU.mult, ALU.add,
                    [xbb, self.cb], [ctb])
            self.STT(ct[:], xb[:, f, 0:W], self.convw[:, 0, f:f + 1], ct[:], ALU.mult, ALU.add, [xbb, self.cb, ctb], [ctb])
            self.STT(ct[:], xb[:, f, 2:W + 2], self.convw[:, 2, f:f + 1], ct[:], ALU.mult, ALU.add, [xbb, self.cb, ctb], [ctb])
            self.ACT(self.xc[:, f, :], ct[:], AF.Silu, [ctb], [self.xcf_b[f]])
            self.ACT(self.xcb_l[slot][:, f, :], ct[:], AF.Silu, [ctb], [self.xcb_bl[slot][f]])
            if need_q:
                self.ACT(self.skx_l[slot][:, f, :], self.xc[:, f, :], AF.Copy, [self.xcf_b[f], self.cb], [self.skx_bl[slot]], scale=self.skipv[:, f:f + 1])
        if need_q:
            n = 0
            for h in range(4):
                for (dst, j) in ((self.qT_l[slot], 1), (self.kT_l[slot], 2)):
                    b = 4 + (n % 2)
                    n += 1
                    self.MM(self.ps[b][:, 0:W], self.w_small[:, j, h, :], self.xcb_l[slot][:, h, :], True, True,
                            [self.wbuf, self.wb_e, self.xcb_bl[slot][h]], [self.psb[b]])
                    self.CP(("act", "dve")[n % 2], dst[:, h, :], self.ps[b][:, 0:W], [self.psb[b]], [self.qk_bl[slot]])

    def l0_gatevecs(self, slot, dirs):
        TC = self.TC
        b = 6
        gb = self.gate_b[slot]
        for c in range(TC):
            o = c * 16
            if 0 in dirs:
                self.MM(self.ps[b][:, o:o + 4], self.triF[:], self.gnf[slot][:, c, 0:4], True, True, [gb, self.cb], [self.psb[b]])
            if 1 in dirs:
                self.MM(self.ps[b][:, o + 4:o + 8], self.triB[:], self.gnf[slot][:, c, 4:8], True, True, [gb, self.cb], [self.psb[b]])
            self.MM(self.ps[b][:, o + 8:o + 16], self.ones_f[:], self.gnf[slot][:, c, 0:8], True, True, [gb, self.cb], [self.psb[b]])
        gps = self.ps[b][:, 0:TC * 16].rearrange("p (c g) -> p c g", c=TC)
        lo, hi = (0 if 0 in dirs else 4), (8 if 1 in dirs else 4)
        self.ACT(self.avec_l[slot][:, :, lo:hi], gps[:, :, lo:hi], AF.Exp, [self.psb[b]], [self.gv_bl[slot]], scale=-1.0)
        self.ACT(self.egv_l[slot][:, :, lo:hi], gps[:, :, 8 + lo:8 + hi], AF.Exp, [self.psb[b]], [self.gv_bl[slot]], scale=-1.0)
        self.TT("dve", self.bvec_l[slot][:, :, lo:hi], self.gig[slot][:, :, lo:hi], gps[:, :, lo:hi], ALU.add, [gb, self.psb[b]], [self.gv_bl[slot]])
        self.ACT(self.bvec_l[slot][:, :, lo:hi], self.bvec_l[slot][:, :, lo:hi], AF.Exp, [self.gv_bl[slot]], [self.gv_bl[slot]], bias=math.log(DH ** -0.5))

    def l0_kprime(self, slot, c, par):
        b = 7
        for h in range(4):
            self.MM(self.ps[b][:, h * 128:(h + 1) * 128], self.xcb_l[slot][:, h, c * 128:(c + 1) * 128], self.w_small[:, 2, h, :], True, True,
                    [self.xcb_bl[slot][h], self.wb_e], [self.psb[b]])
        self.CP("act", self.kp[par][:], self.ps[b][:].rearrange("p (h e) -> p h e", h=4), [self.psb[b]], [self.kp_b[par]])

    def l0_vprime(self, slot, c, dr, par):
        self.TT("dve", self.vp[par][dr][:], self.vaug[slot][:, c, :, :],
                self.bvec_l[slot][:, c, dr * 4:dr * 4 + 4].unsqueeze(2).broadcast_to([128, 4, NV]), ALU.mult,
                [self.vaug_b[slot], self.gv_bl[slot]], [self.vp_b[par][dr]])

    def l0_state_update(self, slot, c, dr, par, Mm, Mb, Mbf, Mbfb):
        for h in range(4):
            b, o = (5, h * NV) if h < 3 else (6, 0)
            self.MM(self.ps[b][:, o:o + NV], self.kp[par][:, h, :], self.vp[par][dr][:, h, :], True, True,
                    [self.kp_b[par], self.vp_b[par][dr]], [self.psb[b]])
        self.TT("dve", Mm[:, 0:3, :], Mm[:, 0:3, :], self.ps[5][:, 0:3 * NV].rearrange("p (h e) -> p h e", h=3), ALU.add,
                [Mb, self.psb[5]], [Mb])
        self.TT("dve", Mm[:, 3, :], Mm[:, 3, :], self.ps[6][:, 0:NV], ALU.add, [Mb, self.psb[6]], [Mb])
        self.TT("dve", Mm[:], Mm[:], self.egv_l[slot][:, c, dr * 4:dr * 4 + 4].unsqueeze(2).broadcast_to([128, 4, NV]), ALU.mult,
                [Mb, self.gv_bl[slot]], [Mb])
        if Mbf is not None:
            self.CP("act", Mbf[:], Mm[:], [Mb], [Mbfb])

    def layer0_seq(self, off, T):
        self.alloc_l0()
        TC, W = self.TC, self.W
        nt = T // W
        nch = T // 128
        for s in range(2):
            self.MS("pool", self.vaug[s][:, :, :, 128:129], 1.0, [], [self.vaug_b[s]])
        Bm, Bb = self.Fm, self.F_b
        self.MS("pool", Bm[:], 0.0, [], [Bb])
        sl = lambda t: (nt - 1 - t) % 2
        bbase = self.bconv_decl
        self.l0_bwd_proj_early(off, T, nt - 1, sl(nt - 1), preloaded=False)
        self.l0_halo(sl(nt - 1), None, True, False)
        g0 = [self.l0_bwd_proj_late_gen(sl(nt - 1))]
        if nt >= 2:
            g0.append(self.l0_xload_gen(off + (nt - 2) * W, sl(nt - 2)))
        self.run_gens(g0)
        if nt >= 2:
            self.l0_bwd_proj_early(off, T, nt - 2, sl(nt - 2), preloaded=True)
            self.l0_halo(sl(nt - 2), sl(nt - 1), True, True)
        for ip in range(nt - 1, 0, -1):
            i = ip - 1
            gens = [self.l0_bwd_tile_gen(off, T, ip, sl(ip), Bm, Bb),
                    self.l0_bwd_sideA_gen(off, i, sl(i), bbase + (nt - 1 - ip) + 1)]
            if i - 2 >= 0:
                pass
            if i - 1 >= 0 and ip < nt - 0:
                if not (i - 1 == nt - 2):
                    gens.append(self.l0_xload_gen(off + (i - 1) * W, sl(i - 1)))
            self.run_gens(gens)
        self.l0_halo(None, sl(0), False, True)
        self.run_gens([self.l0_bwd_tile_gen(off, T, 0, sl(0), Bm, Bb)])
        self.MS("pool", self.Fm[:], 0.0, [], [self.F_b])
        self.MS("pool", self.Fbf[:], 0.0, [], [self.Fbf_b])
        self.fdone = 0
        self.pool_done = 0
        self.conv_base = self.conv_decl
        self.l0_proj_early(off, T, 0, 0, 0)
        self.l0_halo(None, 0, False, True)
        side0 = [self.l0_proj_late_gen(0)]
        if nt > 1:
            side0.append(self.l0_xload_gen(off + W, 1))
        self.run_gens(side0)
        if nt > 1:
            self.l0_proj_early(off, T, 1, 1, 1, preloaded=True)
            self.l0_halo(0, 1, True, True)
        else:
            self.l0_halo(0, None, True, False)
        for i in range(nt):
            slot = i % 2
            side = [self.faster(self.l0_sideA_gen(off, T, i, nt), 2)]
            if i + 2 < nt:
                side.append(self.l0_xload_gen(off + (i + 2) * W, slot))
            self.run_gens([self.l0_tile_gen(off, T, i, slot)] + side)

    def l0_xload_gen(self, tok0, slot):
        self.load_x(self.d["x"], tok0, slot)
        for _ in range(12):
            yield
        self.xloaded_tok = tok0

    def l0_xT(self, tok0, slot, preloaded):
        if not preloaded:
            self.load_x(self.d["x"], tok0, slot)
        for c in range(self.TC):
            self.xT_chunk(c, c % 2, slot)

    def l0_bwd_proj_early(self, off, T, i, slot, preloaded=False):
        TC, W = self.TC, self.W
        wb = self.wb_e
        self.l0_xT(off + i * W, slot, preloaded)
        for f in range(4):
            b = 2 + f % 2
            for k in range(8):
                self.MM(self.ps[b][:, 0:W], self.w_in[:, k, 1024 + f * 128:1024 + (f + 1) * 128], self.xT[:, k, :], k == 0, k == 7,
                        [self.xT_b, wb, self.wb_e], [self.psb[b]])
            self.CP(("act", "dve")[f % 2], self.xbT[slot][:, f, 1:W + 1], self.ps[b][:, 0:W], [self.psb[b]], [self.xbT_b[slot]])

    def l0_bwd_early_gen(self, tok0, slot, need):
        TC, W = self.TC, self.W
        while self.xloaded_tok != tok0 or self.bconv_decl < need:
            yield
        for c in range(TC):
            self.xT_chunk(c, 1, slot)
            yield
            yield
        for f in range(4):
            for k in range(8):
                self.MM(self.ps[1][:, 0:W], self.w_in[:, k, 1024 + f * 128:1024 + (f + 1) * 128], self.xT[:, k, :], k == 0, k == 7,
                        [self.xT_b, self.wb_e], [self.psb[1]])
                if k == 3:
                    yield
            self.CP(("act", "dve")[f % 2], self.xbT[slot][:, f, 1:W + 1], self.ps[1][:, 0:W], [self.psb[1]], [self.xbT_b[slot]])
            yield

    def l0_bwd_sideA_gen(self, off, i, slot_i, need):
        W = self.W
        for _ in self.l0_bwd_proj_late_gen(slot_i):
            yield
        if i - 1 >= 0:
            for _ in self.l0_bwd_early_gen(off + (i - 1) * W, 1 - slot_i, need):
                yield
            self.l0_halo(1 - slot_i, slot_i, True, True)

    def l0_bwd_proj_late_gen(self, slot):
        TC, W = self.TC, self.W
        wb = self.wb_e
        for c in range(TC):
            b = 2 + c % 2
            for k in range(8):
                self.MM(self.ps[b][:], self.xT[:, k, c * 128:(c + 1) * 128], self.w_in[:, k, 1536:2048], k == 0, k == 7,
                        [self.xT_b, wb, self.wb_e], [self.psb[b]])
            self.CP("act", self.vaug[slot][:, c, :, 0:128], self.ps[b][:].rearrange("p (h e) -> p h e", h=4),
                    [self.psb[b]], [self.vaug_b[slot]])
            yield
        b = 2
        for c in range(TC):
            for k in range(8):
                self.MM(self.ps[b][:, c * 16:(c + 1) * 16], self.xT[:, k, c * 128:(c + 1) * 128], self.w_in[:, k, 3072:3088],
                        k == 0, k == 7, [self.xT_b, wb, self.wb_e], [self.psb[b]])
        gps = self.ps[b][:, 0:TC * 16].rearrange("p (c g) -> p c g", c=TC)
        gb = self.gate_b[slot]
        self.TT("dve", self.gig[slot][:], gps[:, :, 0:8], self.bgi[:].unsqueeze(1).broadcast_to([128, TC, 8]), ALU.add,
                [self.psb[b], self.cb], [gb])
        self.TT("dve", self.gtmp[:], gps[:, :, 8:16], self.bgf[:].unsqueeze(1).broadcast_to([128, TC, 8]), ALU.add,
                [self.psb[b], self.cb], [self.gtmp_b])
        yield
        self.ACT(self.gtmp[:], self.gtmp[:], AF.Exp, [self.gtmp_b], [self.gtmp_b], scale=-1.0)
        self.ACT(self.gnf[slot][:], self.gtmp[:], AF.Ln, [self.gtmp_b], [gb], bias=1.0)
        yield

    def l0_bwd_tile_gen(self, off, T, i, slot, Bm, Bb):
        TC, W = self.TC, self.W
        xb, xbb = self.xbT[slot], self.xbT_b[slot]
        for f in range(4):
            ct, ctb = self.ctmp[f % 2], self.ctmp_b[f % 2]
            self.TS("dve", ct[:], xb[:, f, 1:W + 1], self.convw[:, 1, f:f + 1], self.convb[:, f:f + 1], ALU.mult, ALU.add,
                    [xbb, self.cb], [ctb])
            self.STT(ct[:], xb[:, f, 0:W], self.convw[:, 0, f:f + 1], ct[:], ALU.mult, ALU.add, [xbb, self.cb, ctb], [ctb])
            self.STT(ct[:], xb[:, f, 2:W + 2], self.convw[:, 2, f:f + 1], ct[:], ALU.mult, ALU.add, [xbb, self.cb, ctb], [ctb])
            yield
            self.ACT(self.xcb_l[slot][:, f, :], ct[:], AF.Silu, [ctb], [self.xcb_bl[slot][f]])
            yield
        self.bconv_decl += 1
        self.l0_gatevecs(slot, dirs=(1,))
        yield
        yield
        for cc in range(TC):
            c = TC - 1 - cc
            jc = i * TC + c
            par = cc % 2
            bs = self.bstn % 2
            self.bstn += 1
            self.CP("act", self.Bbf[bs][:], Bm[:], [Bb], [self.Bbf_b[bs]])
            bb = self.bst_bufs.setdefault(jc, Buf("bst%d" % jc))
            self.DMA("sp", self.bst_d[jc], self.Bbf[bs][:].rearrange("p h e -> p (h e)"), "bst%d" % bs, [self.Bbf_b[bs]], [bb])
            self.l0_kprime(slot, c, par)
            self.l0_vprime(slot, c, 1, par)
            yield
            yield
            self.l0_state_update(slot, c, 1, par, Bm, Bb, None, None)
            yield

    def l0_proj_early(self, off, T, i, slot, slot3, preloaded=False):
        TC, W = self.TC, self.W
        wb = self.wbuf
        self.l0_xT(off + i * W, slot, preloaded)
        for f in range(4):
            b = 2 + f % 2
            for k in range(8):
                self.MM(self.ps[b][:, 0:W], self.w_in[:, k, 1024 + f * 128:1024 + (f + 1) * 128], self.xT[:, k, :], k == 0, k == 7,
                        [self.xT_b, wb, self.wb_e], [self.psb[b]])
            self.CP(("act", "dve")[f % 2], self.xbT[slot][:, f, 1:W + 1], self.ps[b][:, 0:W], [self.psb[b]], [self.xbT_b[slot]])
        for c in range(TC):
            b = 2 + c % 2
            for k in range(8):
                self.MM(self.ps[b][:], self.xT[:, k, c * 128:(c + 1) * 128], self.w_in[:, k, 0:512], k == 0, k == 7,
                        [self.xT_b, wb, self.wb_e], [self.psb[b]])
            self.CP("act", self.xa_tm[slot3][:, c, :], self.ps[b][:], [self.psb[b]], [self.xa_b[slot3]])

    def l0_proj_early_gen(self, off, T, i, slot, slot3, need_conv):
        TC, W = self.TC, self.W
        wb = self.wbuf
        while self.xloaded_tok != off + i * W or self.conv_decl < need_conv or self.pool_done < (i - 2) * TC + 1:
            yield
        for c in range(TC):
            self.xT_chunk(c, 1, slot)
            yield
            yield
        for f in range(4):
            b = 2 + f % 2
            for k in range(8):
                self.MM(self.ps[b][:, 0:W], self.w_in[:, k, 1024 + f * 128:1024 + (f + 1) * 128], self.xT[:, k, :], k == 0, k == 7,
                        [self.xT_b, wb, self.wb_e], [self.psb[b]])
            self.CP(("act", "dve")[f % 2], self.xbT[slot][:, f, 1:W + 1], self.ps[b][:, 0:W], [self.psb[b]], [self.xbT_b[slot]])
            yield
            yield
        for c in range(TC):
            b = 2 + c % 2
            for k in range(8):
                self.MM(self.ps[b][:], self.xT[:, k, c * 128:(c + 1) * 128], self.w_in[:, k, 0:512], k == 0, k == 7,
                        [self.xT_b, wb, self.wb_e], [self.psb[b]])
            self.CP("act", self.xa_tm[slot3][:, c, :], self.ps[b][:], [self.psb[b]], [self.xa_b[slot3]])
            yield
            yield

    def l0_sideA_gen(self, off, T, i, nt):
        if i + 1 < nt:
            for _ in self.l0_proj_late_gen((i + 1) % 2):
                yield
            if i + 2 < nt:
                for _ in self.l0_proj_early_gen(off, T, i + 2, i % 2, (i + 2) % 3, self.conv_base + i + 1):
                    yield
                self.l0_halo((i + 1) % 2, i % 2, True, True)
            else:
                self.l0_halo((i + 1) % 2, None, True, False)

    def l0_proj_late_gen(self, slot):
        TC, W = self.TC, self.W
        wb = self.wbuf
        pn = [0]

        def bank():
            pn[0] += 1
            return 2 + pn[0] % 2

        def fm(col0):
            b = bank()
            for k in range(8):
                self.MM(self.ps[b][:, 0:W], self.w_in[:, k, col0:col0 + 128], self.xT[:, k, :], k == 0, k == 7,
                        [self.xT_b, wb, self.wb_e], [self.psb[b]])
            return b

        def tm(col0, c):
            b = bank()
            for k in range(8):
                self.MM(self.ps[b][:], self.xT[:, k, c * 128:(c + 1) * 128], self.w_in[:, k, col0:col0 + 512], k == 0, k == 7,
                        [self.xT_b, wb, self.wb_e], [self.psb[b]])
            return b
        for g in range(4):
            b = fm(512 + g * 128)
            self.ACT(self.zas[slot][:, g, :], self.ps[b][:, 0:W], AF.Silu, [self.psb[b]], [self.zas_b[slot]])
            yield
        for f in range(4):
            b = fm(2560 + f * 128)
            self.ACT(self.zbs[slot][:, f, :], self.ps[b][:, 0:W], AF.Silu, [self.psb[b]], [self.zbs_b[slot]])
            yield
        for c in range(TC):
            b = tm(2048, c)
            self.ACT(self.obs[slot][:, c, :], self.ps[b][:], AF.Sigmoid, [self.psb[b]], [self.obs_b[slot]])
            yield
        for c in range(TC):
            b = tm(1536, c)
            self.CP("act", self.vaug[slot][:, c, :, 0:128], self.ps[b][:].rearrange("p (h e) -> p h e", h=4),
                    [self.psb[b]], [self.vaug_b[slot]])
            yield
        b = bank()
        for c in range(TC):
            for k in range(8):
                self.MM(self.ps[b][:, c * 16:(c + 1) * 16], self.xT[:, k, c * 128:(c + 1) * 128], self.w_in[:, k, 3072:3088],
                        k == 0, k == 7, [self.xT_b, wb, self.wb_e], [self.psb[b]])
        gps = self.ps[b][:, 0:TC * 16].rearrange("p (c g) -> p c g", c=TC)
        gb = self.gate_b[slot]
        self.TT("dve", self.gig[slot][:], gps[:, :, 0:8], self.bgi[:].unsqueeze(1).broadcast_to([128, TC, 8]), ALU.add,
                [self.psb[b], self.cb], [gb])
        self.TT("dve", self.gtmp[:], gps[:, :, 8:16], self.bgf[:].unsqueeze(1).broadcast_to([128, TC, 8]), ALU.add,
                [self.psb[b], self.cb], [self.gtmp_b])
        yield
        self.ACT(self.gtmp[:], self.gtmp[:], AF.Exp, [self.gtmp_b], [self.gtmp_b], scale=-1.0)
        self.ACT(self.gnf[slot][:], self.gtmp[:], AF.Ln, [self.gtmp_b], [gb], bias=1.0)
        yield

    def l0_prologue_gen(self, slot):
        TC, W = self.TC, self.W
        xb, xbb = self.xbT[slot], self.xbT_b[slot]
        for f in range(4):
            ct, ctb = self.ctmp[f % 2], self.ctmp_b[f % 2]
            self.TS("dve", ct[:], xb[:, f, 1:W + 1], self.convw[:, 1, f:f + 1], self.convb[:, f:f + 1], ALU.mult, ALU.add,
                    [xbb, self.cb], [ctb])
            self.STT(ct[:], xb[:, f, 0:W], self.convw[:, 0, f:f + 1], ct[:], ALU.mult, ALU.add, [xbb, self.cb, ctb], [ctb])
            self.STT(ct[:], xb[:, f, 2:W + 2], self.convw[:, 2, f:f + 1], ct[:], ALU.mult, ALU.add, [xbb, self.cb, ctb], [ctb])
            yield
            self.ACT(self.xc[:, f, :], ct[:], AF.Silu, [ctb], [self.xcf_b[f]])
            self.ACT(self.xcb_l[slot][:, f, :], ct[:], AF.Silu, [ctb], [self.xcb_bl[slot][f]])
            self.ACT(self.skx_l[slot][:, f, :], self.xc[:, f, :], AF.Copy, [self.xcf_b[f], self.cb], [self.skx_bl[slot]], scale=self.skipv[:, f:f + 1])
            yield
        self.conv_decl += 1
        n = 0
        for h in range(4):
            for (dst, j) in ((self.qT_l[slot], 1), (self.kT_l[slot], 2)):
                b = 4 + (n % 2)
                n += 1
                self.MM(self.ps[b][:, 0:W], self.w_small[:, j, h, :], self.xcb_l[slot][:, h, :], True, True,
                        [self.wbuf, self.wb_e, self.xcb_bl[slot][h]], [self.psb[b]])
                self.CP("dve", dst[:, h, :], self.ps[b][:, 0:W], [self.psb[b]], [self.qk_bl[slot]])
            yield
        self.l0_gatevecs(slot, dirs=(0, 1))
        yield
        yield

    def l0_chunk_gen(self, off, T, i, slot, c):
        TC, W = self.TC, self.W
        nch = T // 128
        jc = i * TC + c
        par = jc % 2
        cs = slice(c * 128, (c + 1) * 128)
        va, vab = self.vaug[slot], self.vaug_b[slot]
        pT, pTb = self.pT[par], self.pT_b[par]
        nd, ndb = self.nd[par], self.nd_b[par]
        dd, ddb = self.dd[par], self.dd_b[par]
        hs, hs2, hsb, hs2b = self.hs[par], self.hs2[par], self.hs_b[par], self.hs2_b[par]
        hstat, hmv, hsc, hstb = self.hstat[par], self.hmv[par], self.hsc[par], self.hst_b[par]
        bs = self.bstn % 2
        self.bstn += 1
        self.DMA("sp", self.Bbf[bs][:].rearrange("p h e -> p (h e)"), self.bst_d[jc], "bst%d" % bs,
                 [self.bst_bufs[jc]], [self.Bbf_b[bs]])
        rs = self.resid_begin(self.d["x"], off + jc * 128, None)
        for dr in range(2):
            self.l0_vprime(slot, c, dr, par)
        yield
        for h in range(4):
            self.MM(self.ps[4][:, h * 128:(h + 1) * 128], self.kT_l[slot][:, h, cs], self.qT_l[slot][:, h, cs], True, True,
                    [self.qk_bl[slot]], [self.psb[4]])
        for dr in range(2):
            mask = (self.maskF, self.maskB)[dr]
            self.TT("dve", pT[:, dr, :, :], self.ps[4][:].rearrange("p (h e) -> p h e", h=4),
                    mask[:].unsqueeze(1).broadcast_to([128, 4, 128]), ALU.mult, [self.psb[4], self.cb], [pTb])
        yield
        while self.fdone < jc:
            yield
        for dr in range(2):
            Mbf, Mbfb = (self.Fbf, self.Fbf_b) if dr == 0 else (self.Bbf[bs], self.Bbf_b[bs])
            for h in range(4):
                combo = dr * 4 + h
                b, o = 5 + combo // 3, (combo % 3) * NV
                self.MM(self.ps[b][:, o:o + NV], pT[:, dr, h, :], self.vp[par][dr][:, h, :], True, False,
                        [pTb, self.vp_b[par][dr]], [self.psb[b]])
                self.MM(self.ps[b][:, o:o + NV], self.qT_l[slot][:, h, cs], Mbf[:, h, :], False, True,
                        [self.qk_bl[slot], Mbfb], [self.psb[b]])
        for bi, (b, n_) in enumerate(((5, 3), (6, 3), (7, 2))):
            pv_ = self.ps[b][:, 0:n_ * NV].rearrange("p (c e) -> p c e", e=NV)
            self.TT("dve", dd[:, bi * 3:bi * 3 + n_], pv_[:, :, 128], self.avec_l[slot][:, c, bi * 3:bi * 3 + n_], ALU.mult,
                    [self.psb[b], self.gv_bl[slot]], [ddb])
        self.STT(dd[:, 8:16], dd[:, 0:8], -1.0, dd[:, 0:8], ALU.mult, ALU.max, [ddb], [ddb])
        self.TS("dve", dd[:, 8:16], dd[:, 8:16], 1.0, None, ALU.max, None, [ddb], [ddb])
        self.S.op("dve", lambda e: e.reciprocal(out=dd[:, 16:24], in_=dd[:, 8:16]), [ddb], [ddb])
        self.TT("dve", dd[:, 16:24], dd[:, 16:24], self.avec_l[slot][:, c, :], ALU.mult, [ddb, self.gv_bl[slot]], [ddb])
        for bi, (b, n_) in enumerate(((5, 3), (6, 3), (7, 2))):
            pv_ = self.ps[b][:, 0:n_ * NV].rearrange("p (c e) -> p c e", e=NV)
            self.TT("dve", nd[:, bi * 3:bi * 3 + n_, :], pv_[:, :, 0:128],
                    dd[:, 16 + bi * 3:16 + bi * 3 + n_].unsqueeze(2).broadcast_to([128, n_, 128]), ALU.mult,
                    [self.psb[b], ddb], [ndb])
        yield
        self.l0_kprime(slot, c, par)
        yield
        self.l0_state_update(slot, c, 0, par, self.Fm, self.F_b, self.Fbf, self.Fbf_b)
        self.fdone = jc + 1
        yield
        self.TT("dve", hs[:], nd[:, 0:4, :], nd[:, 4:8, :], ALU.add, [ndb], [hsb])
        self.TT("dve", hs[:], hs[:], self.obs[slot][:, c, :].rearrange("p (h e) -> p h e", h=4), ALU.mult,
                [hsb, self.obs_b[slot]], [hsb])
        for h in range(4):
            self.S.op("dve", lambda e, h=h: e.bn_stats(out=hstat[:, h, :], in_=hs[:, h, :]), [hsb], [hstb])
        for h in range(4):
            self.S.op("dve", lambda e, h=h: e.bn_aggr(out=hmv[:, h, :], in_=hstat[:, h, :]), [hstb], [hstb])
        yield
        yield
        self.ACT(hsc[:, 0, :], hmv[:, :, 1], AF.Ln, [hstb], [hstb], bias=self.epsc[:, 0:1])
        self.ACT(hsc[:, 1, :], hsc[:, 0, :], AF.Exp, [hstb], [hstb], scale=-0.5)
        yield
        self.TT("dve", hs2[:], hs[:], hmv[:, :, 0].unsqueeze(2).broadcast_to([128, 4, 128]), ALU.subtract, [hsb, hstb], [hs2b])
        self.TT("dve", hs2[:], hs2[:], hsc[:, 1, :].unsqueeze(2).broadcast_to([128, 4, 128]), ALU.mult, [hs2b, hstb], [hs2b])
        yield
        for h in range(4):
            self.TR(self.ps[0][:, h * 128:(h + 1) * 128], hs2[:, h, :], self.ident_f[:], [hs2b, self.cb], [self.psb[0]])
        pv = self.ps[0][:].rearrange("p (h e) -> p h e", h=4)
        self.TT("dve", self.obT[:], pv, self.mhg[:].unsqueeze(2).broadcast_to([128, 4, 128]), ALU.mult, [self.psb[0], self.cb], [self.obT_b])
        self.TT("dve", self.obT[:], self.obT[:], self.skx_l[slot][:, :, cs], ALU.add, [self.obT_b, self.skx_bl[slot]], [self.obT_b])
        self.TT("dve", self.outbT[par][:], self.obT[:], self.zbs[slot][:, :, cs], ALU.mult, [self.obT_b, self.zbs_b[slot]], [self.outbT_b[par]])
        yield
        for g in range(4):
            blks = []
            if jc > 0:
                blks.append((jc - 1, 0))
            blks.append((jc, 1 if jc == 0 else (3 if jc == nch - 1 else 2)))
            if jc < nch - 1:
                blks.append((jc + 1, 4))
            for n_, (j2, blk) in enumerate(blks):
                i2, c2 = j2 // TC, j2 % TC
                s3 = i2 % 3
                self.MM(self.ps[1][:, g * 128:(g + 1) * 128], self.xa_tm[s3][:, c2, g * 128:(g + 1) * 128], self.PB[:, g, blk, :],
                        n_ == 0, n_ == len(blks) - 1, [self.xa_b[s3], self.cb], [self.psb[1]])
        self.CP("act", self.pooledT[par][:], self.ps[1][:].rearrange("p (h e) -> p h e", h=4), [self.psb[1]], [self.pooledT_b[par]])
        self.pool_done = max(self.pool_done, jc + 1)
        yield
        for g in range(4):
            self.MM(self.ps[7][:, g * 128:(g + 1) * 128], self.w_small[:, 0, g, :], self.pooledT[par][:, g, :], True, True,
                    [self.wbuf, self.wb_e, self.pooledT_b[par]], [self.psb[7]])
        self.TT("dve", self.oat[:], self.ps[7][:].rearrange("p (h e) -> p h e", h=4),
                self.pscale[:].unsqueeze(2).broadcast_to([128, 4, 128]), ALU.mult, [self.psb[7], self.cb], [self.oat_b])
        self.TT("dve", self.outaT[par][:], self.oat[:], self.zas[slot][:, :, cs], ALU.mult, [self.oat_b, self.zas_b[slot]], [self.outaT_b[par]])
        yield
        for hf in range(2):
            b = 2 + hf
            for f in range(8):
                lt = self.outaT[par][:, f, :] if f < 4 else self.outbT[par][:, f - 4, :]
                self.MM(self.ps[b][:], lt, self.w_out[:, f, hf * 512:(hf + 1) * 512], f == 0, f == 7,
                        [self.outaT_b[par], self.outbT_b[par], self.wbuf], [self.psb[b]])
            self.resid_half(rs, hf, b)
        yield
        dst = self.x1_d if 1 in self.layers else self.y_d
        dstb = self.x1_bufs if 1 in self.layers else None
        for _ in self.resid_finish_gen(rs, dst, off + jc * 128, dstb):
            yield

    def faster(self, g, r):
        while True:
            for _ in range(r):
                try:
                    next(g)
                except StopIteration:
                    return
            yield

    def run_gens(self, gens):
        gens = list(gens)
        while gens:
            for g in list(gens):
                try:
                    next(g)
                except StopIteration:
                    gens.remove(g)

    def l0_tile_gen(self, off, T, i, slot):
        for _ in self.l0_prologue_gen(slot):
            yield
        act = [self.l0_chunk_gen(off, T, i, slot, c) for c in range(self.TC)]
        while act:
            for g in list(act):
                try:
                    next(g)
                except StopIteration:
                    act.remove(g)
            yield

    def alloc_l1(self):
        if hasattr(self, "KT"):
            return
        TC, W = self.TC, self.W
        mc = self.maxch
        self.KT = self.sb([128, 2, mc * 128], BF16, "KT")
        self.KT_b = Buf()
        self.VA = self.sb([128, mc, 2, NV], BF16, "VA")
        self.VA_b = Buf()
        self.cosT = self.sb([128, mc, 2, 32], F32, "cosT")
        self.sinT = self.sb([128, mc, 2, 32], F32, "sinT")
        self.tab_b = Buf()
        self.ssq = self.sb([128, 16], F32, "ssq")
        self.ssq_b = Buf()
        self.junk = self.sb([128, 128], F32, "junk")
        self.junk_b = Buf()
        self.qn = self.sb([128, 8, 128], F32, "qn")
        self.qn_b = Buf()
        self.rt = [self.sb([128, 8, 2, 32], F32, "rt%d" % i) for i in range(2)]
        self.rt_b = Buf()
        self.qr = [self.sb([128, 8, 128], BF16, "qr%d" % i) for i in range(2)]
        self.qr_b = [Buf() for _ in range(2)]
        self.qrn = 0
        self.QT = [self.sb([128, 8, W], BF16, "QT%d" % i) for i in range(2)]
        self.QT_b = [Buf() for _ in range(2)]
        self.zs = [self.sb([128, TC, D], F32, "zs%d" % i) for i in range(2)]
        self.zs_b = [Buf() for _ in range(2)]
        self.PT = [self.sb([128, 512], BF16, "PT%d" % i) for i in range(3)]
        self.PT_b = [Buf() for _ in range(3)]
        self.og = [self.sb([128, TC, D], BF16, "og%d" % i) for i in range(2)]
        self.og_b = [Buf(), Buf()]
        self.ogT = self.sb([128, 8, 128], BF16, "ogT")
        self.ogT_b = Buf()
        self.rden = self.sb([128, 8], F32, "rden")
        self.rden_b = Buf()
        self.ptn = 0
        self.ktb = []
        for i in range(2):
            self.ktb.append((self.sb([128, 16], F32, "ssqk%d" % i), Buf(), self.sb([128, 2, 128], F32, "qnk%d" % i), Buf(),
                             [self.sb([128, 2, 2, 32], F32, "rtk%d_%d" % (i, j)) for j in range(2)], Buf(),
                             self.sb([128, 128], F32, "junkk%d" % i), Buf()))
        self.build_rope_tables()

    def build_rope_tables(self):
        S = self.S
        mc = self.maxch
        tb = self.tab_b
        R, Wr = [tb], [tb]
        A = self.cosT
        pidx = self.sb([128, 4], F32, "pidx")
        inv = self.sb([128, 32], F32, "inv")
        prow = self.sb([128, mc], F32, "prow")
        ne = mc * 64
        R = [tb, self.zs_b[0], self.zs_b[1], self.KT_b]
        Wr = R

        class _V:
            def __init__(self, ap):
                self.ap = ap

            def __getitem__(self, k):
                if isinstance(k, slice):
                    return self.ap
                return self.ap[k]
        ang = _V(self.zs[0][:].rearrange("p c d -> p (c d)")[:, 0:ne].rearrange("p (m a f) -> p m a f", a=2, f=32))
        kf = _V(self.zs[1][:].rearrange("p c d -> p (c d)")[:, 0:ne].rearrange("p (m a f) -> p m a f", a=2, f=32))
        ki = _V(self.KT[:].rearrange("p k t -> p (k t)").bitcast(I32)[:, 0:ne].rearrange("p (m a f) -> p m a f", a=2, f=32))
        S.op("pool", lambda e: e.iota(pidx[:, 0:1], pattern=[[0, 1]], base=0, channel_multiplier=1,
                                      allow_small_or_imprecise_dtypes=True), R, Wr)
        self.TS("dve", pidx[:, 1:2], pidx[:, 0:1], 64.0, None, ALU.is_ge, None, R, Wr)
        self.STT(pidx[:, 2:3], pidx[:, 1:2], -64.0, pidx[:, 0:1], ALU.mult, ALU.add, R, Wr)
        S.op("pool", lambda e: e.iota(inv[:], pattern=[[1, 32]], base=0, channel_multiplier=0,
                                      allow_small_or_imprecise_dtypes=True), R, Wr)
        self.ACT(inv[:], inv[:], AF.Exp, R, Wr, scale=-math.log(10000.0) / 32.0)
        S.op("pool", lambda e: e.iota(prow[:], pattern=[[2, mc]], base=0, channel_multiplier=0,
                                      allow_small_or_imprecise_dtypes=True), R, Wr)
        self.TS("dve", prow[:], prow[:], pidx[:, 1:2], None, ALU.add, None, R, Wr)
        self.TT("dve", ang[:, :, 0, :], prow[:].unsqueeze(2).broadcast_to([128, mc, 32]),
                inv[:].unsqueeze(1).broadcast_to([128, mc, 32]), ALU.mult, R, Wr)
        self.TS("dve", ang[:, :, 1, :], inv[:].unsqueeze(1).broadcast_to([128, mc, 32]), pidx[:, 2:3], None, ALU.mult, None, R, Wr)
        TWO_PI = 2.0 * math.pi
        for tab, shift in ((self.sinT, 0.0), (self.cosT, math.pi / 2.0)):
            self.TS("dve", tab[:], ang[:], shift, None, ALU.add, None, R, Wr)
            self.TS("dve", kf[:], tab[:], 1.0 / TWO_PI, None, ALU.mult, None, R, Wr)
            self.CP("dve", ki[:], kf[:], R, Wr)
            self.CP("dve", kf[:], ki[:], R, Wr)
            self.STT(tab[:], kf[:], -TWO_PI, tab[:], ALU.mult, ALU.add, R, Wr)
            self.TS("dve", kf[:], tab[:], math.pi, None, ALU.is_gt, None, R, Wr)
            self.STT(tab[:], kf[:], -TWO_PI, tab[:], ALU.mult, ALU.add, R, Wr)
            self.TS("dve", kf[:], tab[:], -math.pi, None, ALU.is_lt, None, R, Wr)
            self.STT(tab[:], kf[:], TWO_PI, tab[:], ALU.mult, ALU.add, R, Wr)
            self.TS("dve", tab[:], tab[:], math.pi, -math.pi, ALU.min, ALU.max, R, Wr)
            self.ACT(tab[:], tab[:], AF.Sin, R, Wr)

    def l1_norm_rope_gen(self, psrc_banks, nh, gain, jc, dst, dstb, tb=None):
        if tb is None:
            tb = (self.ssq, self.ssq_b, self.qn, self.qn_b, self.rt, self.rt_b, self.junk, self.junk_b)
        ssq, ssq_b, qn, qn_b, rt, rt_b, junk, junk_b = tb
        hh = 0
        for (b, n_) in psrc_banks:
            for j in range(n_):
                self.ACT(junk[:], self.ps[b][:, j * 128:(j + 1) * 128], AF.Square, [self.psb[b]], [junk_b, ssq_b],
                         accum=ssq[:, hh + j:hh + j + 1])
            hh += n_
        self.ACT(ssq[:, 8:8 + nh], ssq[:, 0:nh], AF.Ln, [ssq_b], [ssq_b], bias=self.epsc[:, 1:2], scale=1.0 / 128.0)
        self.ACT(ssq[:, 8:8 + nh], ssq[:, 8:8 + nh], AF.Exp, [ssq_b], [ssq_b], scale=-0.5)
        yield
        yield
        hh = 0
        for (b, n_) in psrc_banks:
            for j in range(n_):
                self.STT(qn[:, hh + j, :], self.ps[b][:, j * 128:(j + 1) * 128], ssq[:, 8 + hh + j:9 + hh + j], gain[:],
                         ALU.mult, ALU.mult, [self.psb[b], ssq_b, self.cb], [qn_b])
            hh += n_
        qv = qn[:, 0:nh, :].rearrange("p h (a t f) -> p h a t f", a=2, t=2)
        dv = dst.rearrange("p h (a t f) -> p h a t f", a=2, t=2)
        x1, x2 = qv[:, :, :, 0, :], qv[:, :, :, 1, :]
        cs = self.cosT[:, jc, :, :].unsqueeze(1).broadcast_to([128, nh, 2, 32])
        sn = self.sinT[:, jc, :, :].unsqueeze(1).broadcast_to([128, nh, 2, 32])
        t = [r[:, 0:nh, :, :] for r in rt]
        Rq = [qn_b, self.tab_b]
        self.TT("dve", t[0], x1, cs, ALU.mult, Rq, [rt_b])
        self.TT("dve", t[1], x2, sn, ALU.mult, Rq, [rt_b])
        self.TT("dve", dv[:, :, :, 0, :], t[0], t[1], ALU.subtract, [rt_b], [dstb])
        self.TT("dve", t[0], x2, cs, ALU.mult, Rq + [rt_b], [rt_b])
        self.TT("dve", t[1], x1, sn, ALU.mult, Rq + [rt_b], [rt_b])
        self.TT("dve", dv[:, :, :, 1, :], t[0], t[1], ALU.add, [rt_b], [dstb])

    def l1_norm_rope(self, psrc_banks, nh, gain, jc, dst, dstb):
        for _ in self.l1_norm_rope_gen(psrc_banks, nh, gain, jc, dst, dstb):
            pass

    def l1_q_gen(self, src, srcb, off, i, slot, qs):
        TC, W = self.TC, self.W
        wb = self.wbuf
        for c in range(TC):
            jc = i * TC + c
            q_ = self.qrn % 2
            self.qrn += 1
            pst = self.ps[2][:].bitcast(BF16)
            for k in range(8):
                self.TR(pst[:, k * 128:(k + 1) * 128], self.xbf[slot][:, c, k * 128:(k + 1) * 128], self.ident_b[:],
                        [self.xbf_c[slot][c], self.cb], [self.psb[2]])
            yield
            self.CP("dve", self.xT[:, :, c * 128:(c + 1) * 128], pst.rearrange("p (k t) -> p k t", k=8), [self.psb[2]], [self.xT_b])
            yield
            yield
            for hf in range(2):
                for k in range(8):
                    self.MM(self.ps[2 + hf][:], self.xT[:, k, c * 128:(c + 1) * 128], self.w_in[:, k, hf * 512:(hf + 1) * 512],
                            k == 0, k == 7, [self.xT_b, wb, self.wb_e], [self.psb[2 + hf]])
                    if k % 4 == 3:
                        yield
            yield
            for _ in self.l1_norm_rope_gen([(2, 4), (3, 4)], 8, self.qng, jc, self.qr[q_][:, 0:8, :], self.qr_b[q_]):
                yield
            yield
            for hf in range(2):
                for k in range(8):
                    self.MM(self.ps[2 + hf][:], self.xT[:, k, c * 128:(c + 1) * 128],
                            self.w_in[:, k, 1536 + hf * 512:1536 + (hf + 1) * 512], k == 0, k == 7, [self.xT_b, wb, self.wb_e], [self.psb[2 + hf]])
                    if k % 4 == 3:
                        yield
            yield
            for hf in range(2):
                self.CP("dve", self.zs[qs][:, c, hf * 512:(hf + 1) * 512], self.ps[2 + hf][:], [self.psb[2 + hf]], [self.zs_b[qs]])
            yield
            yield
            pst = self.ps[2][:].bitcast(BF16)
            for h in range(8):
                self.TR(pst[:, h * 128:(h + 1) * 128], self.qr[q_][:, h, :], self.ident_b[:], [self.qr_b[q_], self.cb], [self.psb[2]])
            yield
            yield
            self.CP("dve", self.QT[qs][:, :, c * 128:(c + 1) * 128], pst.rearrange("p (k t) -> p k t", k=8), [self.psb[2]], [self.QT_b[qs]])
            yield
            yield
        zall = self.zs[qs][:].rearrange("p c d -> p (c d)")
        self.ACT(zall, zall, AF.Silu, [self.zs_b[qs]], [self.zs_b[qs]])
        yield

    def l1_xload_gen(self, src, srcb, off, i, slot):
        self.load_x(src, off + i * self.W, slot, srcb)
        for _ in range(16):
            yield

    def l1_epi_gen(self, src, srcb, off, i, os_):
        TC = self.TC
        wb = self.wbuf
        for c in range(TC):
            jc = i * TC + c
            tok = off + jc * 128
            rs = self.resid_begin(src, tok, srcb)
            pst = self.ps[0][:].bitcast(BF16)
            for f in range(8):
                self.TR(pst[:, f * 128:(f + 1) * 128], self.og[os_][:, c, f * 128:(f + 1) * 128], self.ident_b[:],
                        [self.og_b[os_], self.cb], [self.psb[0]])
            yield
            yield
            self.CP("dve", self.ogT[:], pst.rearrange("p (k t) -> p k t", k=8), [self.psb[0]], [self.ogT_b])
            yield
            yield
            for hf in range(2):
                for f in range(8):
                    self.MM(self.ps[0][:], self.ogT[:, f, :], self.w_out[:, f, hf * 512:(hf + 1) * 512], f == 0, f == 7,
                            [self.ogT_b, wb], [self.psb[0]])
                yield
                yield
                yield
                yield
                self.resid_half(rs, hf, 0)
                yield
                yield
            for _ in self.resid_finish_gen(rs, self.y_d, tok, None):
                yield

    def l1_epi_steps(self, src, srcb, off, i, os_):
        return [self.l1_epi_gen(src, srcb, off, i, os_)]

    def l1_q_steps(self, src, srcb, off, i, slot, qs):
        return [self.l1_xload_gen(src, srcb, off, i, slot), self.l1_q_gen(src, srcb, off, i, slot, qs)]

    def l1_att_stage(self, nch, qs, os_, sideA, sideB):
        TC, W = self.TC, self.W
        KB = 512 // W
        nkb = nch // KB
        scale = DH ** -0.5
        iters = [(h, kb) for h in range(8) for kb in range(nkb)]
        SB = (6, 7, 1)
        kA = -(-90 // max(1, len(iters) - 2))
        kB = -(-60 // max(1, len(iters) - 2))

        def emit_S(n):
            h, kb = iters[n]
            kv = h // 4
            sb_ = SB[n % 3]
            for j in range(KB):
                kc = kb * KB + j
                self.MM(self.ps[sb_][:, j * W:(j + 1) * W], self.KT[:, kv, kc * 128:(kc + 1) * 128], self.QT[qs][:, h, :], True, True,
                        [self.KT_b, self.QT_b[qs]], [self.psb[sb_]])
        emit_S(0)
        if len(iters) > 1:
            emit_S(1)
        for n, (h, kb) in enumerate(iters):
            kv = h // 4
            if n + 2 < len(iters):
                emit_S(n + 2)
            sb_ = SB[n % 3]
            pi = self.ptn % 3
            self.ptn += 1
            self.ACT(self.PT[pi][:], self.ps[sb_][:], AF.Exp, [self.psb[sb_]], [self.PT_b[pi]], scale=scale)
            for (lst, k_) in ((sideA, kA), (sideB, kB)):
                for _k in range(k_):
                    if lst:
                        try:
                            next(lst[0])
                        except StopIteration:
                            lst.pop(0)
            ob = 4 + (h % 2)
            for j in range(KB):
                kc = kb * KB + j
                for qc in range(TC):
                    self.MM(self.ps[ob][:, qc * NV:(qc + 1) * NV], self.PT[pi][:, j * W + qc * 128:j * W + (qc + 1) * 128],
                            self.VA[:, kc, kv, :], kc == 0 and qc == 0, kc == nch - 1, [self.PT_b[pi], self.VA_b], [self.psb[ob]], skip=True)
            if kb == nkb - 1:
                for qc in range(TC):
                    rc = (h % 2) * TC + qc
                    self.S.op("dve", lambda e, ob=ob, rc=rc, qc=qc: e.reciprocal(out=self.rden[:, rc:rc + 1],
                                                                              in_=self.ps[ob][:, qc * NV + 128:qc * NV + 129]),
                              [self.psb[ob]], [self.rden_b])
                    self.STT(self.og[os_][:, qc, h * 128:(h + 1) * 128], self.ps[ob][:, qc * NV:qc * NV + 128], self.rden[:, rc:rc + 1],
                             self.zs[qs][:, qc, h * 128:(h + 1) * 128], ALU.mult, ALU.mult, [self.psb[ob], self.rden_b, self.zs_b[qs]],
                             [self.og_b[os_]])
        for lst in (sideB, sideA):
            while lst:
                for _ in lst.pop(0):
                    pass

    def layer1_seq(self, off, T):
        self.alloc_l1()
        TC, W = self.TC, self.W
        nt = T // W
        nch = T // 128
        wb = self.wbuf
        src = self.x1_d if 0 in self.layers else self.d["x"]
        srcb = self.x1_bufs if 0 in self.layers else None
        self.MS("pool", self.VA[:, :, :, 128:129], 1.0, [], [self.VA_b])
        def kv_chunk_gen(i, c):
            jc = i * TC + c
            par = jc % 2
            b = 2 + par
            for k in range(8):
                self.MM(self.ps[b][:], self.xT[:, k, c * 128:(c + 1) * 128], self.w_in[:, k, 1024:1536], k == 0, k == 7,
                        [self.xT_b, self.wb_e], [self.psb[b]])
            self.CP("act", self.VA[:, jc, :, 0:128], self.ps[b][:, 256:512].rearrange("p (h e) -> p h e", h=2),
                    [self.psb[b]], [self.VA_b])
            for _ in self.l1_norm_rope_gen([(b, 2)], 2, self.kng, jc, self.qr[par][:, 0:2, :], self.qr_b[par], self.ktb[par]):
                yield
            yield
            pst = self.ps[par][:].bitcast(BF16)
            for kv in range(2):
                self.TR(pst[:, kv * 128:(kv + 1) * 128], self.qr[par][:, kv, :], self.ident_b[:], [self.qr_b[par], self.cb], [self.psb[par]])
            self.CP("act", self.KT[:, :, jc * 128:(jc + 1) * 128], pst[:, 0:256].rearrange("p (k t) -> p k t", k=2),
                    [self.psb[par]], [self.KT_b])
            yield
        self.load_x(src, off, 0, srcb)
        for i in range(nt):
            slot = i % 2
            for c in range(TC):
                self.xT_chunk(c, c % 2, slot)
            if i + 1 < nt:
                self.load_x(src, off + (i + 1) * W, 1 - slot, srcb)
            self.run_gens([kv_chunk_gen(i, c) for c in range(TC)])
        for g_ in self.l1_q_steps(src, srcb, off, 0, 0, 0):
            for _ in g_:
                pass
        for i in range(nt):
            sideA, sideB = [], []
            if i + 1 < nt:
                sideA.append(self.l1_xload_gen(src, srcb, off, i + 1, (i + 1) % 2))
                sideA.append(self.l1_q_gen(src, srcb, off, i + 1, (i + 1) % 2, (i + 1) % 2))
            if i >= 1:
                sideB += self.l1_epi_steps(src, srcb, off, i - 1, (i - 1) % 2)
            self.l1_att_stage(nch, i % 2, i % 2, sideA, sideB)
        for g_ in self.l1_epi_steps(src, srcb, off, nt - 1, (nt - 1) % 2):
            for _ in g_:
                pass


def build_program(seq_lens, TC=2, layers=(0, 1)):
    k = K(seq_lens, TC=TC, layers=layers)
    return k.build()


_INPUT_NAMES = ["w_in_even", "w_pool", "pool_scale", "conv_w", "conv_b", "w_q_m", "w_k_m", "b_gate_i", "b_gate_f",
                "mh_norm_g", "skip", "w_out_even", "w_in_odd", "q_norm_g", "k_norm_g", "w_out_odd", "ln_g", "ln_b"]
_SHAPES = {"w_in_even": (D, IN0), "w_pool": (4, 128, 128), "pool_scale": (512,), "conv_w": (3, 512), "conv_b": (512,),
           "w_q_m": (4, 128, 128), "w_k_m": (4, 128, 128), "b_gate_i": (8,), "b_gate_f": (8,), "mh_norm_g": (512,),
           "skip": (512,), "w_out_even": (D, D), "w_in_odd": (D, IN1), "q_norm_g": (128,), "k_norm_g": (128,),
           "w_out_odd": (D, D), "ln_g": (2, D), "ln_b": (2, D)}


def weight_map(inputs):
    m = {}
    for nm in _INPUT_NAMES:
        m[nm] = np.ascontiguousarray(np.asarray(inputs[nm], dtype=np.float32).reshape(_SHAPES[nm]))
    return m


def kernel(**inputs):
    xp = np.asarray(inputs["x_prompt"], dtype=np.float32)
    xs = np.asarray(inputs["x_sample"], dtype=np.float32)
    n = 8
    nc = build_program([4096, 2048, 2048], TC=2, layers=(0, 1))
    wm = weight_map(inputs)
    in_maps = []
    for c in range(n):
        xcat = np.concatenate([xp[c], xs[2 * c], xs[2 * c + 1]], axis=0)
        m = dict(wm)
        m["x"] = np.ascontiguousarray(xcat)
        in_maps.append(m)
    res = run_bass_kernel_spmd(nc, in_maps, core_ids=list(range(n)))
    yp = np.empty_like(xp)
    ys = np.empty_like(xs)
    for c in range(n):
        y = res.results[c]["y"]
        yp[c] = y[0:4096]
        ys[2 * c] = y[4096:6144]
        ys[2 * c + 1] = y[6144:8192]
    return (yp, ys)
```

```python
import math
from contextlib import ExitStack

import numpy as np
import concourse.bass as bass
import concourse.mybir as mybir
from concourse.bass_utils import run_bass_kernel_spmd

F32 = mybir.dt.float32
BF16 = mybir.dt.bfloat16
I32 = mybir.dt.int32
AF = mybir.ActivationFunctionType
ALU = mybir.AluOpType
AX = mybir.AxisListType

D = 1024
IN0 = 3088
IN1 = 2560
ALPHA = 4.0 ** 0.25
LN_EPS = 1e-5
RMS_EPS = 1e-6
POOL_WINDOWS = (2, 4, 8, 16)
DH = 128
NV = 129


class Buf:
    __slots__ = ("name", "w", "r")

    def __init__(self, name=""):
        self.name = name
        self.w = None
        self.r = {}


class Sched:
    def __init__(self, nc, stack):
        self.nc = nc
        self.stack = stack
        self.sem = {}
        self.cnt = {}
        self.known = {}
        self.prog = {}
        for e in ("pe", "dve", "act", "pool", "sp"):
            self.sem[e] = stack.enter_context(nc.semaphore("s_" + e))
            self.cnt[e] = 0
            self.known[e] = {}
            self.prog[e] = []

    def _waits(self, eng, reads, writes):
        need = {}
        for b in reads:
            if b.w is not None and need.get(b.w[0], 0) < b.w[1]:
                need[b.w[0]] = b.w[1]
        for b in writes:
            if b.w is not None and need.get(b.w[0], 0) < b.w[1]:
                need[b.w[0]] = b.w[1]
            for s, c in b.r.items():
                if need.get(s, 0) < c:
                    need[s] = c
        waits = []
        kn = self.known[eng]
        for s, c in need.items():
            if s == "pe" and eng == "pe":
                continue
            if kn.get(s, 0) < c:
                kn[s] = c
                waits.append((self.sem[s], c * (16 if s.startswith("dq_") else 1)))
        return waits

    def _mark(self, src, my, reads, writes):
        for b in reads:
            if b.r.get(src, 0) < my:
                b.r[src] = my
        for b in writes:
            b.w = (src, my)
            b.r = {}

    def op(self, eng, fn, reads=(), writes=()):
        waits = self._waits(eng, reads, writes)
        self.cnt[eng] += 1
        my = self.cnt[eng]
        sem = self.sem[eng]

        def emit(e):
            for s, v in waits:
                e.wait_ge(s, v)
            fn(e).then_inc(sem, 1)
        self.prog[eng].append(emit)
        self._mark(eng, my, reads, writes)

    def dma(self, eng, out, in_, key, reads=(), writes=(), slow=False):
        q = "dq_" + key
        if q not in self.sem:
            self.sem[q] = self.stack.enter_context(self.nc.semaphore("s_" + q))
            self.cnt[q] = 0
        waits = self._waits(eng, reads, writes)
        self.cnt[q] += 1
        my = self.cnt[q]
        sem = self.sem[q]

        def emit(e):
            for s, v in waits:
                e.wait_ge(s, v)
            if slow:
                e.dma_start(out=out, in_=in_, allow_slow_non_contiguous=True).then_inc(sem, 16)
            else:
                e.dma_start(out=out, in_=in_).then_inc(sem, 16)
        self.prog[eng].append(emit)
        self._mark(q, my, reads, writes)

    def barrier(self):
        targets = [(q, self.cnt[q]) for q in self.sem if self.cnt.get(q, 0) > 0]
        for eng in ("pe", "dve", "act", "pool", "sp"):
            waits = []
            for q, c in targets:
                if q == eng and eng in ("pe", "sp"):
                    continue
                if self.known[eng].get(q, 0) < c:
                    self.known[eng][q] = c
                    waits.append((self.sem[q], c * (16 if q.startswith("dq_") else 1)))

            def emit(e, waits=waits):
                for s_, v in waits:
                    e.wait_ge(s_, v)
            self.prog[eng].append(emit)

    def finish(self, eng):
        targets = [(self.sem[q], self.cnt[q] * 16) for q in self.sem if q.startswith("dq_") and self.cnt[q] > 0]
        targets += [(self.sem[e], self.cnt[e]) for e in ("pe", "dve", "act", "pool") if self.cnt[e] > 0]

        def emit(e):
            for s, v in targets:
                e.wait_ge(s, v)
        self.prog[eng].append(emit)

    def emit_all(self, block):
        progs = self.prog

        @block.sync
        def _(e):
            for f in progs["sp"]:
                f(e)

        @block.tensor
        def _(e):
            for f in progs["pe"]:
                f(e)

        @block.vector
        def _(e):
            for f in progs["dve"]:
                f(e)

        @block.scalar
        def _(e):
            for f in progs["act"]:
                f(e)

        @block.gpsimd
        def _(e):
            for f in progs["pool"]:
                f(e)


class K:
    def __init__(self, seq_lens, TC=2, layers=(0, 1), dbg=False):
        self.seq_lens = list(seq_lens)
        self.TC = TC
        self.W = 128 * TC
        self.layers = layers
        self.NT = sum(seq_lens)
        self.maxch = max(seq_lens) // 128
        self.nc = bass.Bass("TRN2", target_bir_lowering=False)
        self.st = ExitStack()
        self.cur = self.st
        self.uid = 0

    def sb(self, shape, dt=F32, name=None):
        self.uid += 1
        return self.cur.enter_context(self.nc.sbuf_tensor(name or ("t%d" % self.uid), list(shape), dt))

    def din(self, name, shape, dt=F32):
        return self.nc.dram_tensor(name, list(shape), dt, kind="ExternalInput").ap()

    def ACT(self, out, in_, func, R, Wr, bias=None, scale=None, accum=None):
        kw = {}
        if bias is not None:
            kw["bias"] = bias
        if scale is not None:
            kw["scale"] = scale
        if accum is not None:
            kw["accum_out"] = accum
        self.S.op("act", lambda e: e.activation(out=out, in_=in_, func=func, **kw), R, Wr)

    def TT(self, eng, out, in0, in1, op, R, Wr):
        self.S.op(eng, lambda e: e.tensor_tensor(out=out, in0=in0, in1=in1, op=op), R, Wr)

    def TS(self, eng, out, in0, s1, s2, op0, op1, R, Wr):
        if op1 is None:
            self.S.op(eng, lambda e: e.tensor_scalar(out=out, in0=in0, scalar1=s1, scalar2=None, op0=op0), R, Wr)
        else:
            self.S.op(eng, lambda e: e.tensor_scalar(out=out, in0=in0, scalar1=s1, scalar2=s2, op0=op0, op1=op1), R, Wr)

    def STT(self, out, in0, scalar, in1, op0, op1, R, Wr):
        self.S.op("dve", lambda e: e.scalar_tensor_tensor(out=out, in0=in0, scalar=scalar, in1=in1, op0=op0, op1=op1), R, Wr)

    def CP(self, eng, out, in_, R, Wr):
        if eng == "act":
            self.S.op("act", lambda e: e.activation(out=out, in_=in_, func=AF.Copy), R, Wr)
        else:
            self.S.op(eng, lambda e: e.tensor_copy(out=out, in_=in_), R, Wr)

    def MM(self, out, lhsT, rhs, start, stop, R, Wr, skip=False):
        if skip:
            self.S.op("pe", lambda e: e.matmul(out, lhsT=lhsT, rhs=rhs, start=start, stop=stop, skip_group_check=True), R, Wr)
        else:
            self.S.op("pe", lambda e: e.matmul(out, lhsT=lhsT, rhs=rhs, start=start, stop=stop), R, Wr)

    def TR(self, out, in_, ident, R, Wr):
        self.S.op("pe", lambda e: e.transpose(out=out, in_=in_, identity=ident), R, Wr)

    def MS(self, eng, out, val, R, Wr):
        self.S.op(eng, lambda e: e.memset(out, val), R, Wr)

    def DMA(self, eng, out, in_, key, R, Wr, slow=False):
        self.S.dma(eng, out, in_, key, R, Wr, slow=slow)

    def build(self):
        nc = self.nc
        st = self.st
        TC, W, NT = self.TC, self.W, self.NT
        d = {}
        d["x"] = self.din("x", [NT, D])
        d["w_in_even"] = self.din("w_in_even", [D, IN0])
        d["w_pool"] = self.din("w_pool", [4, 128, 128])
        d["pool_scale"] = self.din("pool_scale", [512])
        d["conv_w"] = self.din("conv_w", [3, 512])
        d["conv_b"] = self.din("conv_b", [512])
        d["w_q_m"] = self.din("w_q_m", [4, 128, 128])
        d["w_k_m"] = self.din("w_k_m", [4, 128, 128])
        d["b_gate_i"] = self.din("b_gate_i", [8])
        d["b_gate_f"] = self.din("b_gate_f", [8])
        d["mh_norm_g"] = self.din("mh_norm_g", [512])
        d["skip"] = self.din("skip", [512])
        d["w_out_even"] = self.din("w_out_even", [D, D])
        d["w_in_odd"] = self.din("w_in_odd", [D, IN1])
        d["q_norm_g"] = self.din("q_norm_g", [128])
        d["k_norm_g"] = self.din("k_norm_g", [128])
        d["w_out_odd"] = self.din("w_out_odd", [D, D])
        d["ln_g"] = self.din("ln_g", [2, D])
        d["ln_b"] = self.din("ln_b", [2, D])
        self.d = d
        self.y_d = nc.dram_tensor("y", [NT, D], F32, kind="ExternalOutput").ap()
        self.x1_d = nc.dram_tensor("x1s", [NT, D], F32, kind="Internal").ap()
        self.bst_d = nc.dram_tensor("bsts", [self.maxch, 128, 4 * NV], BF16, kind="Internal").ap()
        self.x1_bufs = {}
        self.bst_bufs = {}

        self.S = Sched(nc, st)
        self.ps = []
        self.psb = []
        for i in range(8):
            self.ps.append(st.enter_context(nc.psum_tensor("ps%d" % i, [128, 512], F32)))
            self.psb.append(Buf("ps%d" % i))
        self.prologue_done_for = None
        self.conv_decl = 0
        self.bconv_decl = 0
        self.xloaded_tok = -1
        self.alloc_common()
        self.setup_consts()
        if 0 in self.layers:
            l0st = ExitStack()
            self.cur = l0st
            self.setup_consts_l0()
            self.load_weights(0)
            off = 0
            for T in self.seq_lens:
                self.layer0_seq(off, T)
                off += T
            self.S.barrier()
            l0st.close()
            self.cur = self.st
        if 1 in self.layers:
            self.load_weights(1)
            off = 0
            for T in self.seq_lens:
                self.layer1_seq(off, T)
                off += T
        self.S.finish("sp")
        block = st.enter_context(nc.Block())
        self.S.emit_all(block)
        st.close()
        return nc

    def setup_consts(self):
        nc = self.nc
        cb = Buf("consts")
        self.cb = cb
        R, Wr = [cb], [cb]
        self.ident_f = self.sb([128, 128], F32, "ident_f")
        self.ident_b = self.sb([128, 128], BF16, "ident_b")
        S = self.S
        self.MS("pool", self.ident_f[:], 0.0, [], Wr)
        S.op("pool", lambda e: e.affine_select(out=self.ident_f[:], in_=self.ident_f[:], pattern=[[-1, 128]],
                                               compare_op=ALU.not_equal, fill=1.0, base=0, channel_multiplier=1), R, Wr)
        self.CP("dve", self.ident_b[:], self.ident_f[:], R, Wr)
        self.lng = self.sb([128, D], F32, "lng")
        self.lnb = self.sb([128, D], F32, "lnb")
        self.lnbuf = Buf("ln")
        self.epsc = self.sb([128, 2], F32, "epsc")
        self.MS("pool", self.epsc[:, 0:1], LN_EPS, [], Wr)
        self.MS("pool", self.epsc[:, 1:2], RMS_EPS, [], Wr)
        self.qng = self.sb([128, 128], F32, "qng")
        self.kng = self.sb([128, 128], F32, "kng")
        self.DMA("sp", self.qng[:], self.d["q_norm_g"].partition_broadcast(128), "c0", [], Wr)
        self.DMA("sp", self.kng[:], self.d["k_norm_g"].partition_broadcast(128), "c0", [], Wr)
        self.w_in = self.sb([128, 8, IN0], BF16, "w_in")
        self.w_out = self.sb([128, 8, D], BF16, "w_out")
        self.wbuf = Buf("weights")
        self.wb_e = Buf("weights_early")

    def setup_consts_l0(self):
        cb = self.cb
        R, Wr = [cb], [cb]
        S = self.S
        self.triF = self.sb([128, 128], F32, "triF")
        self.triB = self.sb([128, 128], F32, "triB")
        self.ones_f = self.sb([128, 128], F32, "ones_f")
        self.maskF = self.sb([128, 128], BF16, "maskF")
        self.maskB = self.sb([128, 128], BF16, "maskB")
        self.MS("pool", self.ones_f[:], 1.0, [], Wr)
        S.op("pool", lambda e: e.affine_select(out=self.triF[:], in_=self.ones_f[:], pattern=[[1, 128]],
                                               compare_op=ALU.is_ge, fill=0.0, base=0, channel_multiplier=-1), R, Wr)
        S.op("pool", lambda e: e.affine_select(out=self.triB[:], in_=self.ones_f[:], pattern=[[-1, 128]],
                                               compare_op=ALU.is_ge, fill=0.0, base=0, channel_multiplier=1), R, Wr)
        self.CP("dve", self.maskF[:], self.triF[:], R, Wr)
        self.CP("dve", self.maskB[:], self.triB[:], R, Wr)
        self.bgi = self.sb([128, 8], F32, "bgi")
        self.bgf = self.sb([128, 8], F32, "bgf")
        self.DMA("sp", self.bgi[:], self.d["b_gate_i"].partition_broadcast(128), "c1", [], Wr)
        self.DMA("sp", self.bgf[:], self.d["b_gate_f"].partition_broadcast(128), "c1", [], Wr)
        self.pscale = self.sb([128, 4], F32, "pscale")
        self.convw = self.sb([128, 3, 4], F32, "convw")
        self.convb = self.sb([128, 4], F32, "convb")
        self.mhg = self.sb([128, 4], F32, "mhg")
        self.skipv = self.sb([128, 4], F32, "skipv")
        self.DMA("sp", self.pscale[:], self.d["pool_scale"].rearrange("(g p) -> p g", p=128), "c1", [], Wr, slow=True)
        self.DMA("sp", self.convb[:], self.d["conv_b"].rearrange("(g p) -> p g", p=128), "c1", [], Wr, slow=True)
        self.DMA("sp", self.mhg[:], self.d["mh_norm_g"].rearrange("(g p) -> p g", p=128), "c1", [], Wr, slow=True)
        self.DMA("sp", self.skipv[:], self.d["skip"].rearrange("(g p) -> p g", p=128), "c1", [], Wr, slow=True)
        self.DMA("sp", self.convw[:], self.d["conv_w"].rearrange("j (g p) -> p j g", p=128), "c1", [], Wr, slow=True)
        self.w_small = self.sb([128, 3, 4, 128], BF16, "w_small")
        self.setup_pool_bands()

    def setup_pool_bands(self):
        S = self.S
        cb = self.cb
        R, Wr = [cb], [cb]
        self.PB = self.sb([128, 4, 5, 128], BF16, "PB")
        R = [cb, self.res_b[0]]
        Wr = [cb, self.res_b[0]]
        ws = self.res[0][:]
        tmp, tmp2, iot, rc, t2 = (ws[:, 0:128], ws[:, 128:256], ws[:, 256:384], ws[:, 384:512], ws[:, 512:640])

        class _V:
            def __init__(self, ap):
                self.ap = ap

            def __getitem__(self, k):
                return self.ap
        tmp, tmp2, iot, rc, t2 = _V(tmp), _V(tmp2), _V(iot), _V(rc), _V(t2)
        S.op("pool", lambda e: e.iota(iot[:], pattern=[[1, 128]], base=0, channel_multiplier=0,
                                      allow_small_or_imprecise_dtypes=True), R, Wr)
        for g, w in enumerate(POOL_WINDOWS):
            left = w // 2
            right = w - 1 - left

            def band(dst, base_lo, base_hi, val):
                self.MS("pool", dst, val, R, Wr)
                S.op("pool", lambda e: e.affine_select(out=dst, in_=dst, pattern=[[-1, 128]], compare_op=ALU.is_ge,
                                                       fill=0.0, base=base_lo, channel_multiplier=1), R, Wr)
                S.op("pool", lambda e: e.affine_select(out=dst, in_=dst, pattern=[[1, 128]], compare_op=ALU.is_ge,
                                                       fill=0.0, base=base_hi, channel_multiplier=-1), R, Wr)
            band(tmp[:], left - 128, 10000, 1.0 / w)
            self.CP("dve", self.PB[:, g, 0, :], tmp[:], R, Wr)
            band(tmp[:], 10000, right - 128, 1.0 / w)
            self.CP("dve", self.PB[:, g, 4, :], tmp[:], R, Wr)
            band(tmp[:], left, right, 1.0)
            self.TS("dve", tmp2[:], tmp[:], 1.0 / w, None, ALU.mult, None, R, Wr)
            self.TT("dve", tmp2[:], tmp2[:], self.ident_f[:], ALU.subtract, R, Wr)
            self.CP("dve", self.PB[:, g, 2, :], tmp2[:], R, Wr)
            self.TS("dve", t2[:], iot[:], float(-left), 0.0, ALU.add, ALU.max, R, Wr)
            self.STT(rc[:], iot[:], float(right + 1), t2[:], ALU.add, ALU.subtract, R, Wr)
            S.op("dve", lambda e: e.reciprocal(out=rc[:], in_=rc[:]), R, Wr)
            self.TT("dve", tmp2[:], tmp[:], rc[:], ALU.mult, R, Wr)
            self.TT("dve", tmp2[:], tmp2[:], self.ident_f[:], ALU.subtract, R, Wr)
            self.CP("dve", self.PB[:, g, 1, :], tmp2[:], R, Wr)
            self.TS("dve", t2[:], iot[:], -1.0, 128.0, ALU.mult, ALU.add, R, Wr)
            self.TS("dve", rc[:], t2[:], float(right + 1), float(left), ALU.min, ALU.add, R, Wr)
            S.op("dve", lambda e: e.reciprocal(out=rc[:], in_=rc[:]), R, Wr)
            self.TT("dve", tmp2[:], tmp[:], rc[:], ALU.mult, R, Wr)
            self.TT("dve", tmp2[:], tmp2[:], self.ident_f[:], ALU.subtract, R, Wr)
            self.CP("dve", self.PB[:, g, 3, :], tmp2[:], R, Wr)

    def load_weights(self, layer):
        d = self.d
        wb = self.wbuf
        if layer == 0:
            win, ncol, wout = d["w_in_even"], IN0, d["w_out_even"]
        else:
            win, ncol, wout = d["w_in_odd"], IN1, d["w_out_odd"]
        we = self.wb_e
        win3 = win.rearrange("(k p) c -> p k c", p=128)
        wout3 = wout.rearrange("(k p) c -> p k c", p=128)

        def cols(c0, c1, key, buf):
            self.DMA("pool", self.w_in[:, :, c0:c1], win3[:, :, c0:c1], key, [], [buf])
        if layer == 0:
            cols(1024, 1536, "wle", we)
            cols(1536, 2048, "wle", we)
            cols(3072, 3088, "wle", we)
            for j, nm in enumerate(("w_pool", "w_q_m", "w_k_m")):
                self.DMA("pool", self.w_small[:, j, :, :], d[nm].rearrange("h d e -> d h e"), "wle", [], [we])
            cols(0, 1024, "wld", wb)
            cols(2048, 3072, "wld", wb)
        else:
            cols(1024, 1536, "wle", we)
            cols(0, 1024, "wld", wb)
            cols(1536, 2560, "wld", wb)
        self.DMA("pool", self.w_out[:], wout3, "wld", [], [wb])
        self.DMA("sp", self.lng[:], d["ln_g"][layer].partition_broadcast(128), "lnp", [], [self.lnbuf])
        self.DMA("sp", self.lnb[:], d["ln_b"][layer].partition_broadcast(128), "lnp", [], [self.lnbuf])

    def alloc_common(self):
        if hasattr(self, "xin"):
            return
        TC, W = self.TC, self.W
        self.xin = True
        self.xbf = [self.sb([128, TC, D], BF16, "xbf%d" % i) for i in range(2)]
        self.xbf_b = Buf("xbf")
        self.xbf_c = [[Buf() for _ in range(TC)] for _ in range(2)]
        self.xT = self.sb([128, 8, W], BF16, "xT")
        self.xT_b = Buf("xT")
        self.res = [self.sb([128, D], F32, "res%d" % i) for i in range(2)]
        self.res_b = [Buf("res0"), Buf("res1")]
        self.stat_l = [self.sb([128, 2, 6], F32, "stat%d" % i) for i in range(2)]
        self.mv_l = [self.sb([128, 8], F32, "mv%d" % i) for i in range(2)]
        self.stat_bl = [Buf("stat0"), Buf("stat1")]
        self.resn = 0

    def load_x(self, src_d, tok0, sl, src_bufs=None):
        TC, W = self.TC, self.W
        R = []
        if src_bufs is not None:
            for c in range(TC):
                R.append(src_bufs[(tok0 // 128) + c])
        self.DMA("pool", self.xbf[sl][:], src_d[tok0:tok0 + W, :].rearrange("(c p) d -> p c d", p=128),
                 "xbf%d" % sl, R, self.xbf_c[sl])

    def xT_chunk(self, c, bank, sl):
        pst = self.ps[bank][:].bitcast(BF16)
        for k in range(8):
            self.TR(pst[:, k * 128:(k + 1) * 128], self.xbf[sl][:, c, k * 128:(k + 1) * 128], self.ident_b[:],
                    [self.xbf_c[sl][c], self.cb], [self.psb[bank]])
        self.CP(("act", "dve")[c % 2], self.xT[:, :, c * 128:(c + 1) * 128],
                pst.rearrange("p (k t) -> p k t", k=8), [self.psb[bank]], [self.xT_b])

    def load_xT(self, src_d, tok0, sl, src_bufs=None):
        self.load_x(src_d, tok0, sl, src_bufs)
        for c in range(self.TC):
            self.xT_chunk(c, c % 2, sl)

    def resid_begin(self, src_d, tok, src_bufs=None):
        rs = self.resn % 2
        self.resn += 1
        R = [src_bufs[tok // 128]] if src_bufs is not None else []
        self.DMA("sp", self.res[rs][:], src_d[tok:tok + 128, :], "resx%d" % rs, R, [self.res_b[rs]])
        return rs

    def resid_half(self, rs, hf, bank):
        res, rb = self.res[rs], self.res_b[rs]
        self.STT(res[:, hf * 512:(hf + 1) * 512], res[:, hf * 512:(hf + 1) * 512], ALPHA,
                 self.ps[bank][:], ALU.mult, ALU.add, [rb, self.psb[bank]], [rb])

    def resid_finish_gen(self, rs, dst_d, tok, dst_bufs=None):
        res, rb = self.res[rs], self.res_b[rs]
        sb_ = self.stat_bl[rs]
        stat, mv = self.stat_l[rs], self.mv_l[rs]
        for hf in range(2):
            self.S.op("dve", lambda e, hf=hf: e.bn_stats(out=stat[:, hf, :], in_=res[:, hf * 512:(hf + 1) * 512]), [rb], [sb_])
        self.S.op("dve", lambda e: e.bn_aggr(out=mv[:, 0:2], in_=stat[:].rearrange("p a b -> p (a b)")), [sb_], [sb_])
        yield
        yield
        yield
        yield
        self.ACT(mv[:, 3:4], mv[:, 1:2], AF.Ln, [sb_], [sb_], bias=self.epsc[:, 0:1])
        self.ACT(mv[:, 4:5], mv[:, 3:4], AF.Exp, [sb_], [sb_], scale=-0.5)
        yield
        yield
        self.STT(res[:], res[:], mv[:, 0:1], self.lng[:], ALU.subtract, ALU.mult, [rb, sb_, self.lnbuf], [rb])
        self.STT(res[:], res[:], mv[:, 4:5], self.lnb[:], ALU.mult, ALU.add, [rb, sb_, self.lnbuf], [rb])
        yield
        yield
        yield
        yield
        yield
        yield
        Wr = []
        if dst_bufs is not None:
            b = dst_bufs.setdefault(tok // 128, Buf("dst%d" % (tok // 128)))
            Wr = [b]
        self.DMA("sp", dst_d[tok:tok + 128, :], res[:], "res%d" % rs, [rb], Wr)

    def resid_finish(self, rs, dst_d, tok, dst_bufs=None):
        for _ in self.resid_finish_gen(rs, dst_d, tok, dst_bufs):
            pass

    def resid_ln_store(self, ybanks, src_d, src_bufs, dst_d, tok, dst_bufs=None):
        rs = self.resid_begin(src_d, tok, src_bufs)
        for hf in range(2):
            self.resid_half(rs, hf, ybanks[hf])
        self.resid_finish(rs, dst_d, tok, dst_bufs)

    def alloc_l0(self):
        if hasattr(self, "xa_tm"):
            return
        TC, W = self.TC, self.W
        self.alloc_common()
        self.xa_tm = [self.sb([128, TC, 512], BF16, "xa%d" % i) for i in range(3)]
        self.xa_b = [Buf() for _ in range(3)]
        self.vaug = [self.sb([128, TC, 4, NV], BF16, "vaug%d" % i) for i in range(2)]
        self.vaug_b = [Buf() for _ in range(2)]
        self.obs = [self.sb([128, TC, 512], BF16, "obs%d" % i) for i in range(2)]
        self.obs_b = [Buf() for _ in range(2)]
        self.zas = [self.sb([128, 4, W], BF16, "zas%d" % i) for i in range(2)]
        self.zas_b = [Buf() for _ in range(2)]
        self.zbs = [self.sb([128, 4, W], BF16, "zbs%d" % i) for i in range(2)]
        self.zbs_b = [Buf() for _ in range(2)]
        self.xbT = [self.sb([128, 4, W + 2], F32, "xbT%d" % i) for i in range(2)]
        self.xbT_b = [Buf() for _ in range(2)]
        self.gig = [self.sb([128, TC, 8], F32, "gig%d" % i) for i in range(2)]
        self.gnf = [self.sb([128, TC, 8], F32, "gnf%d" % i) for i in range(2)]
        self.gate_b = [Buf() for _ in range(2)]
        self.gtmp = self.sb([128, TC, 8], F32, "gtmp")
        self.gtmp_b = Buf()
        self.ctmp = [self.sb([128, W], F32, "ctmp%d" % i) for i in range(2)]
        self.ctmp_b = [Buf(), Buf()]
        self.xc = self.sb([128, 4, W], F32, "xc")
        self.xcb_l = [self.sb([128, 4, W], BF16, "xcb%d" % i) for i in range(2)]
        self.skx_l = [self.sb([128, 4, W], F32, "skx%d" % i) for i in range(2)]
        self.xc_b = Buf()
        self.xcf_b = [Buf() for _ in range(4)]
        self.xcb_bl = [[Buf() for _ in range(4)] for _ in range(2)]
        self.skx_bl = [Buf(), Buf()]
        self.qT_l = [self.sb([128, 4, W], BF16, "qT%d" % i) for i in range(2)]
        self.kT_l = [self.sb([128, 4, W], BF16, "kT%d" % i) for i in range(2)]
        self.qk_bl = [Buf(), Buf()]
        P2 = range(2)
        self.kp = [self.sb([128, 4, 128], BF16, "kp%d" % i) for i in P2]
        self.kp_b = [Buf() for _ in P2]
        self.vp = [[self.sb([128, 4, NV], BF16, "vp%d_%d" % (i, j)) for j in range(2)] for i in P2]
        self.vp_b = [[Buf(), Buf()] for _ in P2]
        self.avec_l = [self.sb([128, TC, 8], F32, "avec%d" % i) for i in range(2)]
        self.bvec_l = [self.sb([128, TC, 8], F32, "bvec%d" % i) for i in range(2)]
        self.egv_l = [self.sb([128, TC, 8], F32, "egv%d" % i) for i in range(2)]
        self.gv_bl = [Buf(), Buf()]
        self.pT = [self.sb([128, 2, 4, 128], BF16, "pT%d" % i) for i in P2]
        self.pT_b = [Buf() for _ in P2]
        self.nd = [self.sb([128, 8, 128], F32, "nd%d" % i) for i in P2]
        self.nd_b = [Buf() for _ in P2]
        self.dd = [self.sb([128, 24], F32, "dd%d" % i) for i in P2]
        self.dd_b = [Buf() for _ in P2]
        self.hs = [self.sb([128, 4, 128], F32, "hs%d" % i) for i in P2]
        self.hs2 = [self.sb([128, 4, 128], F32, "hs2%d" % i) for i in P2]
        self.hs_b = [Buf() for _ in P2]
        self.hs2_b = [Buf() for _ in P2]
        self.hstat = [self.sb([128, 4, 6], F32, "hstat%d" % i) for i in P2]
        self.hmv = [self.sb([128, 4, 2], F32, "hmv%d" % i) for i in P2]
        self.hsc = [self.sb([128, 3, 4], F32, "hsc%d" % i) for i in P2]
        self.hst_b = [Buf() for _ in P2]
        self.obT = self.sb([128, 4, 128], F32, "obT")
        self.obT_b = Buf()
        self.outbT = [self.sb([128, 4, 128], BF16, "outbT%d" % i) for i in P2]
        self.outbT_b = [Buf() for _ in P2]
        self.pooledT = [self.sb([128, 4, 128], BF16, "pooledT%d" % i) for i in P2]
        self.pooledT_b = [Buf() for _ in P2]
        self.outaT = [self.sb([128, 4, 128], BF16, "outaT%d" % i) for i in P2]
        self.outaT_b = [Buf() for _ in P2]
        self.oat = self.sb([128, 4, 128], F32, "oat")
        self.oat_b = Buf()
        self.Fm = self.sb([128, 4, NV], F32, "Fm")
        self.Fbf = self.sb([128, 4, NV], BF16, "Fbf")
        self.F_b = Buf()
        self.Fbf_b = Buf()
        self.Bbf = [self.sb([128, 4, NV], BF16, "Bbf%d" % i) for i in range(2)]
        self.Bbf_b = [Buf() for _ in range(2)]
        self.bstn = 0

    def l0_proj(self, off, T, i, slot, slot3, full):
        TC, W = self.TC, self.W
        nt = T // W
        tok0 = off + i * W
        wb = self.wbuf
        self.load_xT(self.d["x"], tok0, slot)
        pn = [0]

        def bank():
            b = 2 + (pn[0] % 2)
            pn[0] += 1
            return b
        en = [0]

        def eng2():
            en[0] += 1
            return ("act", "dve")[en[0] % 2]

        def proj_tm(col0, ncols, c):
            b = bank()
            for k in range(8):
                self.MM(self.ps[b][:, 0:ncols], self.xT[:, k, c * 128:(c + 1) * 128], self.w_in[:, k, col0:col0 + ncols],
                        k == 0, k == 7, [self.xT_b, wb, self.wb_e], [self.psb[b]])
            return b

        def proj_fm(col0):
            b = bank()
            for k in range(8):
                self.MM(self.ps[b][:, 0:W], self.w_in[:, k, col0:col0 + 128], self.xT[:, k, :],
                        k == 0, k == 7, [self.xT_b, wb, self.wb_e], [self.psb[b]])
            return b
        for c in range(TC):
            if full:
                b = proj_tm(0, 512, c)
                self.CP("act", self.xa_tm[slot3][:, c, :], self.ps[b][:], [self.psb[b]], [self.xa_b[slot3]])
            b = proj_tm(1536, 512, c)
            self.CP(eng2(), self.vaug[slot][:, c, :, 0:128], self.ps[b][:].rearrange("p (h e) -> p h e", h=4),
                    [self.psb[b]], [self.vaug_b[slot]])
            if full:
                b = proj_tm(2048, 512, c)
                self.ACT(self.obs[slot][:, c, :], self.ps[b][:], AF.Sigmoid, [self.psb[b]], [self.obs_b[slot]])
        b = bank()
        for c in range(TC):
            for k in range(8):
                self.MM(self.ps[b][:, c * 16:(c + 1) * 16], self.xT[:, k, c * 128:(c + 1) * 128], self.w_in[:, k, 3072:3088],
                        k == 0, k == 7, [self.xT_b, wb, self.wb_e], [self.psb[b]])
        gps = self.ps[b][:, 0:TC * 16].rearrange("p (c g) -> p c g", c=TC)
        gb = self.gate_b[slot]
        self.TT("dve", self.gig[slot][:], gps[:, :, 0:8], self.bgi[:].unsqueeze(1).broadcast_to([128, TC, 8]), ALU.add,
                [self.psb[b], self.cb], [gb])
        self.TT("dve", self.gtmp[:], gps[:, :, 8:16], self.bgf[:].unsqueeze(1).broadcast_to([128, TC, 8]), ALU.add,
                [self.psb[b], self.cb], [self.gtmp_b])
        self.ACT(self.gtmp[:], self.gtmp[:], AF.Exp, [self.gtmp_b], [self.gtmp_b], scale=-1.0)
        self.ACT(self.gnf[slot][:], self.gtmp[:], AF.Ln, [self.gtmp_b], [gb], bias=1.0)
        if full:
            for g in range(4):
                b = proj_fm(512 + g * 128)
                self.ACT(self.zas[slot][:, g, :], self.ps[b][:, 0:W], AF.Silu, [self.psb[b]], [self.zas_b[slot]])
        for f in range(4):
            b = proj_fm(1024 + f * 128)
            self.CP(eng2(), self.xbT[slot][:, f, 1:W + 1], self.ps[b][:, 0:W], [self.psb[b]], [self.xbT_b[slot]])
        if full:
            for f in range(4):
                b = proj_fm(2560 + f * 128)
                self.ACT(self.zbs[slot][:, f, :], self.ps[b][:, 0:W], AF.Silu, [self.psb[b]], [self.zbs_b[slot]])

    def l0_halo(self, slot_lo, slot_hi, has_lo, has_hi):
        W = self.W
        if has_lo and has_hi:
            self.CP("act", self.xbT[slot_lo][:, :, W + 1:W + 2], self.xbT[slot_hi][:, :, 1:2],
                    [self.xbT_b[slot_hi]], [self.xbT_b[slot_lo]])
            self.CP("act", self.xbT[slot_hi][:, :, 0:1], self.xbT[slot_lo][:, :, W:W + 1],
                    [self.xbT_b[slot_lo]], [self.xbT_b[slot_hi]])
        elif has_hi:
            self.MS("pool", self.xbT[slot_hi][:, :, 0:1], 0.0, [], [self.xbT_b[slot_hi]])
        elif has_lo:
            self.MS("pool", self.xbT[slot_lo][:, :, W + 1:W + 2], 0.0, [], [self.xbT_b[slot_lo]])

    def l0_conv_qk(self, slot, need_q):
        TC, W = self.TC, self.W
        xb, xbb = self.xbT[slot], self.xbT_b[slot]
        for f in range(4):
            ct, ctb = self.ctmp[f % 2], self.ctmp_b[f % 2]
            self.TS("dve", ct[:], xb[:, f, 1:W + 1], self.convw[:, 1, f:f + 1], self.convb[:, f:f + 1], ALU.mult, ALU.add,
                    [xbb, self.cb], [ctb])
            self.STT(ct[:], xb[:, f, 0:W], self.convw[:, 0, f:f + 1], ct[:], ALU.mult, ALU.add, [xbb, self.cb, ctb], [ctb])
            self.STT(ct[:], xb[:, f, 2:W + 2], self.convw[:, 2, f:f + 1], ct[:], ALU.mult, ALU.add, [xbb, self.cb, ctb], [ctb])
            self.ACT(self.xc[:, f, :], ct[:], AF.Silu, [ctb], [self.xcf_b[f]])
            self.ACT(self.xcb_l[slot][:, f, :], ct[:], AF.Silu, [ctb], [self.xcb_bl[slot][f]])
            if need_q:
                self.ACT(self.skx_l[slot][:, f, :], self.xc[:, f, :], AF.Copy, [self.xcf_b[f], self.cb], [self.skx_bl[slot]], scale=self.skipv[:, f:f + 1])
        if need_q:
            n = 0
            for h in range(4):
                for (dst, j) in ((self.qT_l[slot], 1), (self.kT_l[slot], 2)):
                    b = 4 + (n % 2)
                    n += 1
                    self.MM(self.ps[b][:, 0:W], self.w_small[:, j, h, :], self.xcb_l[slot][:, h, :], True, True,
                            [self.wbuf, self.wb_e, self.xcb_bl[slot][h]], [self.psb[b]])
                    self.CP(("act", "dve")[n % 2], dst[:, h, :], self.ps[b][:, 0:W], [self.psb[b]], [self.qk_bl[slot]])

    def l0_gatevecs(self, slot, dirs):
        TC = self.TC
        b = 6
        gb = self.gate_b[slot]
        for c in range(TC):
            o = c * 16
            if 0 in dirs:
                self.MM(self.ps[b][:, o:o + 4], self.triF[:], self.gnf[slot][:, c, 0:4], True, True, [gb, self.cb], [self.psb[b]])
            if 1 in dirs:
                self.MM(self.ps[b][:, o + 4:o + 8], self.triB[:], self.gnf[slot][:, c, 4:8], True, True, [gb, self.cb], [self.psb[b]])
            self.MM(self.ps[b][:, o + 8:o + 16], self.ones_f[:], self.gnf[slot][:, c, 0:8], True, True, [gb, self.cb], [self.psb[b]])
        gps = self.ps[b][:, 0:TC * 16].rearrange("p (c g) -> p c g", c=TC)
        lo, hi = (0 if 0 in dirs else 4), (8 if 1 in dirs else 4)
        self.ACT(self.avec_l[slot][:, :, lo:hi], gps[:, :, lo:hi], AF.Exp, [self.psb[b]], [self.gv_bl[slot]], scale=-1.0)
        self.ACT(self.egv_l[slot][:, :, lo:hi], gps[:, :, 8 + lo:8 + hi], AF.Exp, [self.psb[b]], [self.gv_bl[slot]], scale=-1.0)
        self.TT("dve", self.bvec_l[slot][:, :, lo:hi], self.gig[slot][:, :, lo:hi], gps[:, :, lo:hi], ALU.add, [gb, self.psb[b]], [self.gv_bl[slot]])
        self.ACT(self.bvec_l[slot][:, :, lo:hi], self.bvec_l[slot][:, :, lo:hi], AF.Exp, [self.gv_bl[slot]], [self.gv_bl[slot]], bias=math.log(DH ** -0.5))

    def l0_kprime(self, slot, c, par):
        b = 7
        for h in range(4):
            self.MM(self.ps[b][:, h * 128:(h + 1) * 128], self.xcb_l[slot][:, h, c * 128:(c + 1) * 128], self.w_small[:, 2, h, :], True, True,
                    [self.xcb_bl[slot][h], self.wb_e], [self.psb[b]])
        self.CP("act", self.kp[par][:], self.ps[b][:].rearrange("p (h e) -> p h e", h=4), [self.psb[b]], [self.kp_b[par]])

    def l0_vprime(self, slot, c, dr, par):
        self.TT("dve", self.vp[par][dr][:], self.vaug[slot][:, c, :, :],
                self.bvec_l[slot][:, c, dr * 4:dr * 4 + 4].unsqueeze(2).broadcast_to([128, 4, NV]), ALU.mult,
                [self.vaug_b[slot], self.gv_bl[slot]], [self.vp_b[par][dr]])

    def l0_state_update(self, slot, c, dr, par, Mm, Mb, Mbf, Mbfb):
        for h in range(4):
            b, o = (5, h * NV) if h < 3 else (6, 0)
            self.MM(self.ps[b][:, o:o + NV], self.kp[par][:, h, :], self.vp[par][dr][:, h, :], True, True,
                    [self.kp_b[par], self.vp_b[par][dr]], [self.psb[b]])
        self.TT("dve", Mm[:, 0:3, :], Mm[:, 0:3, :], self.ps[5][:, 0:3 * NV].rearrange("p (h e) -> p h e", h=3), ALU.add,
                [Mb, self.psb[5]], [Mb])
        self.TT("dve", Mm[:, 3, :], Mm[:, 3, :], self.ps[6][:, 0:NV], ALU.add, [Mb, self.psb[6]], [Mb])
        self.TT("dve", Mm[:], Mm[:], self.egv_l[slot][:, c, dr * 4:dr * 4 + 4].unsqueeze(2).broadcast_to([128, 4, NV]), ALU.mult,
                [Mb, self.gv_bl[slot]], [Mb])
        if Mbf is not None:
            self.CP("act", Mbf[:], Mm[:], [Mb], [Mbfb])

    def layer0_seq(self, off, T):
        self.alloc_l0()
        TC, W = self.TC, self.W
        nt = T // W
        nch = T // 128
        for s in range(2):
            self.MS("pool", self.vaug[s][:, :, :, 128:129], 1.0, [], [self.vaug_b[s]])
        Bm, Bb = self.Fm, self.F_b
        self.MS("pool", Bm[:], 0.0, [], [Bb])
        sl = lambda t: (nt - 1 - t) % 2
        bbase = self.bconv_decl
        self.l0_bwd_proj_early(off, T, nt - 1, sl(nt - 1), preloaded=False)
        self.l0_halo(sl(nt - 1), None, True, False)
        g0 = [self.l0_bwd_proj_late_gen(sl(nt - 1))]
        if nt >= 2:
            g0.append(self.l0_xload_gen(off + (nt - 2) * W, sl(nt - 2)))
        self.run_gens(g0)
        if nt >= 2:
            self.l0_bwd_proj_early(off, T, nt - 2, sl(nt - 2), preloaded=True)
            self.l0_halo(sl(nt - 2), sl(nt - 1), True, True)
        for ip in range(nt - 1, 0, -1):
            i = ip - 1
            gens = [self.l0_bwd_tile_gen(off, T, ip, sl(ip), Bm, Bb),
                    self.l0_bwd_sideA_gen(off, i, sl(i), bbase + (nt - 1 - ip) + 1)]
            if i - 2 >= 0:
                pass
            if i - 1 >= 0 and ip < nt - 0:
                if not (i - 1 == nt - 2):
                    gens.append(self.l0_xload_gen(off + (i - 1) * W, sl(i - 1)))
            self.run_gens(gens)
        self.l0_halo(None, sl(0), False, True)
        self.run_gens([self.l0_bwd_tile_gen(off, T, 0, sl(0), Bm, Bb)])
        self.MS("pool", self.Fm[:], 0.0, [], [self.F_b])
        self.MS("pool", self.Fbf[:], 0.0, [], [self.Fbf_b])
        self.fdone = 0
        self.pool_done = 0
        self.conv_base = self.conv_decl
        self.l0_proj_early(off, T, 0, 0, 0)
        self.l0_halo(None, 0, False, True)
        side0 = [self.l0_proj_late_gen(0)]
        if nt > 1:
            side0.append(self.l0_xload_gen(off + W, 1))
        self.run_gens(side0)
        if nt > 1:
            self.l0_proj_early(off, T, 1, 1, 1, preloaded=True)
            self.l0_halo(0, 1, True, True)
        else:
            self.l0_halo(0, None, True, False)
        for i in range(nt):
            slot = i % 2
            side = [self.faster(self.l0_sideA_gen(off, T, i, nt), 2)]
            if i + 2 < nt:
                side.append(self.l0_xload_gen(off + (i + 2) * W, slot))
            self.run_gens([self.l0_tile_gen(off, T, i, slot)] + side)

    def l0_xload_gen(self, tok0, slot):
        self.load_x(self.d["x"], tok0, slot)
        for _ in range(12):
            yield
        self.xloaded_tok = tok0

    def l0_xT(self, tok0, slot, preloaded):
        if not preloaded:
            self.load_x(self.d["x"], tok0, slot)
        for c in range(self.TC):
            self.xT_chunk(c, c % 2, slot)

    def l0_bwd_proj_early(self, off, T, i, slot, preloaded=False):
        TC, W = self.TC, self.W
        wb = self.wb_e
        self.l0_xT(off + i * W, slot, preloaded)
        for f in range(4):
            b = 2 + f % 2
            for k in range(8):
                self.MM(self.ps[b][:, 0:W], self.w_in[:, k, 1024 + f * 128:1024 + (f + 1) * 128], self.xT[:, k, :], k == 0, k == 7,
                        [self.xT_b, wb, self.wb_e], [self.psb[b]])
            self.CP(("act", "dve")[f % 2], self.xbT[slot][:, f, 1:W + 1], self.ps[b][:, 0:W], [self.psb[b]], [self.xbT_b[slot]])

    def l0_bwd_early_gen(self, tok0, slot, need):
        TC, W = self.TC, self.W
        while self.xloaded_tok != tok0 or self.bconv_decl < need:
            yield
        for c in range(TC):
            self.xT_chunk(c, 1, slot)
            yield
            yield
        for f in range(4):
            for k in range(8):
                self.MM(self.ps[1][:, 0:W], self.w_in[:, k, 1024 + f * 128:1024 + (f + 1) * 128], self.xT[:, k, :], k == 0, k == 7,
                        [self.xT_b, self.wb_e], [self.psb[1]])
                if k == 3:
                    yield
            self.CP(("act", "dve")[f % 2], self.xbT[slot][:, f, 1:W + 1], self.ps[1][:, 0:W], [self.psb[1]], [self.xbT_b[slot]])
            yield

    def l0_bwd_sideA_gen(self, off, i, slot_i, need):
        W = self.W
        for _ in self.l0_bwd_proj_late_gen(slot_i):
            yield
        if i - 1 >= 0:
            for _ in self.l0_bwd_early_gen(off + (i - 1) * W, 1 - slot_i, need):
                yield
            self.l0_halo(1 - slot_i, slot_i, True, True)

    def l0_bwd_proj_late_gen(self, slot):
        TC, W = self.TC, self.W
        wb = self.wb_e
        for c in range(TC):
            b = 2 + c % 2
            for k in range(8):
                self.MM(self.ps[b][:], self.xT[:, k, c * 128:(c + 1) * 128], self.w_in[:, k, 1536:2048], k == 0, k == 7,
                        [self.xT_b, wb, self.wb_e], [self.psb[b]])
            self.CP("act", self.vaug[slot][:, c, :, 0:128], self.ps[b][:].rearrange("p (h e) -> p h e", h=4),
                    [self.psb[b]], [self.vaug_b[slot]])
            yield
        b = 2
        for c in range(TC):
            for k in range(8):
                self.MM(self.ps[b][:, c * 16:(c + 1) * 16], self.xT[:, k, c * 128:(c + 1) * 128], self.w_in[:, k, 3072:3088],
                        k == 0, k == 7, [self.xT_b, wb, self.wb_e], [self.psb[b]])
        gps = self.ps[b][:, 0:TC * 16].rearrange("p (c g) -> p c g", c=TC)
        gb = self.gate_b[slot]
        self.TT("dve", self.gig[slot][:], gps[:, :, 0:8], self.bgi[:].unsqueeze(1).broadcast_to([128, TC, 8]), ALU.add,
                [self.psb[b], self.cb], [gb])
        self.TT("dve", self.gtmp[:], gps[:, :, 8:16], self.bgf[:].unsqueeze(1).broadcast_to([128, TC, 8]), ALU.add,
                [self.psb[b], self.cb], [self.gtmp_b])
        yield
        self.ACT(self.gtmp[:], self.gtmp[:], AF.Exp, [self.gtmp_b], [self.gtmp_b], scale=-1.0)
        self.ACT(self.gnf[slot][:], self.gtmp[:], AF.Ln, [self.gtmp_b], [gb], bias=1.0)
        yield

    def l0_bwd_tile_gen(self, off, T, i, slot, Bm, Bb):
        TC, W = self.TC, self.W
        xb, xbb = self.xbT[slot], self.xbT_b[slot]
        for f in range(4):
            ct, ctb = self.ctmp[f % 2], self.ctmp_b[f % 2]
            self.TS("dve", ct[:], xb[:, f, 1:W + 1], self.convw[:, 1, f:f + 1], self.convb[:, f:f + 1], ALU.mult, ALU.add,
                    [xbb, self.cb], [ctb])
            self.STT(ct[:], xb[:, f, 0:W], self.convw[:, 0, f:f + 1], ct[:], ALU.mult, ALU.add, [xbb, self.cb, ctb], [ctb])
            self.STT(ct[:], xb[:, f, 2:W + 2], self.convw[:, 2, f:f + 1], ct[:], ALU.mult, ALU.add, [xbb, self.cb, ctb], [ctb])
            yield
            self.ACT(self.xcb_l[slot][:, f, :], ct[:], AF.Silu, [ctb], [self.xcb_bl[slot][f]])
            yield
        self.bconv_decl += 1
        self.l0_gatevecs(slot, dirs=(1,))
        yield
        yield
        for cc in range(TC):
            c = TC - 1 - cc
            jc = i * TC + c
            par = cc % 2
            bs = self.bstn % 2
            self.bstn += 1
            self.CP("act", self.Bbf[bs][:], Bm[:], [Bb], [self.Bbf_b[bs]])
            bb = self.bst_bufs.setdefault(jc, Buf("bst%d" % jc))
            self.DMA("sp", self.bst_d[jc], self.Bbf[bs][:].rearrange("p h e -> p (h e)"), "bst%d" % bs, [self.Bbf_b[bs]], [bb])
            self.l0_kprime(slot, c, par)
            self.l0_vprime(slot, c, 1, par)
            yield
            yield
            self.l0_state_update(slot, c, 1, par, Bm, Bb, None, None)
            yield

    def l0_proj_early(self, off, T, i, slot, slot3, preloaded=False):
        TC, W = self.TC, self.W
        wb = self.wbuf
        self.l0_xT(off + i * W, slot, preloaded)
        for f in range(4):
            b = 2 + f % 2
            for k in range(8):
                self.MM(self.ps[b][:, 0:W], self.w_in[:, k, 1024 + f * 128:1024 + (f + 1) * 128], self.xT[:, k, :], k == 0, k == 7,
                        [self.xT_b, wb, self.wb_e], [self.psb[b]])
            self.CP(("act", "dve")[f % 2], self.xbT[slot][:, f, 1:W + 1], self.ps[b][:, 0:W], [self.psb[b]], [self.xbT_b[slot]])
        for c in range(TC):
            b = 2 + c % 2
            for k in range(8):
                self.MM(self.ps[b][:], self.xT[:, k, c * 128:(c + 1) * 128], self.w_in[:, k, 0:512], k == 0, k == 7,
                        [self.xT_b, wb, self.wb_e], [self.psb[b]])
            self.CP("act", self.xa_tm[slot3][:, c, :], self.ps[b][:], [self.psb[b]], [self.xa_b[slot3]])

    def l0_proj_early_gen(self, off, T, i, slot, slot3, need_conv):
        TC, W = self.TC, self.W
        wb = self.wbuf
        while self.xloaded_tok != off + i * W or self.conv_decl < need_conv or self.pool_done < (i - 2) * TC + 1:
            yield
        for c in range(TC):
            self.xT_chunk(c, 1, slot)
            yield
            yield
        for f in range(4):
            b = 2 + f % 2
            for k in range(8):
                self.MM(self.ps[b][:, 0:W], self.w_in[:, k, 1024 + f * 128:1024 + (f + 1) * 128], self.xT[:, k, :], k == 0, k == 7,
                        [self.xT_b, wb, self.wb_e], [self.psb[b]])
            self.CP(("act", "dve")[f % 2], self.xbT[slot][:, f, 1:W + 1], self.ps[b][:, 0:W], [self.psb[b]], [self.xbT_b[slot]])
            yield
            yield
        for c in range(TC):
            b = 2 + c % 2
            for k in range(8):
                self.MM(self.ps[b][:], self.xT[:, k, c * 128:(c + 1) * 128], self.w_in[:, k, 0:512], k == 0, k == 7,
                        [self.xT_b, wb, self.wb_e], [self.psb[b]])
            self.CP("act", self.xa_tm[slot3][:, c, :], self.ps[b][:], [self.psb[b]], [self.xa_b[slot3]])
            yield
            yield

    def l0_sideA_gen(self, off, T, i, nt):
        if i + 1 < nt:
            for _ in self.l0_proj_late_gen((i + 1) % 2):
                yield
            if i + 2 < nt:
                for _ in self.l0_proj_early_gen(off, T, i + 2, i % 2, (i + 2) % 3, self.conv_base + i + 1):
                    yield
                self.l0_halo((i + 1) % 2, i % 2, True, True)
            else:
                self.l0_halo((i + 1) % 2, None, True, False)
            for _ in self.l0_prologue_gen((i + 1) % 2):
                yield
            self.prologue_done_for = (off, i + 1)

    def l0_proj_late_gen(self, slot):
        TC, W = self.TC, self.W
        wb = self.wbuf
        pn = [0]

        def bank():
            pn[0] += 1
            return 2 + pn[0] % 2

        def fm(col0):
            b = bank()
            for k in range(8):
                self.MM(self.ps[b][:, 0:W], self.w_in[:, k, col0:col0 + 128], self.xT[:, k, :], k == 0, k == 7,
                        [self.xT_b, wb, self.wb_e], [self.psb[b]])
            return b

        def tm(col0, c):
            b = bank()
            for k in range(8):
                self.MM(self.ps[b][:], self.xT[:, k, c * 128:(c + 1) * 128], self.w_in[:, k, col0:col0 + 512], k == 0, k == 7,
                        [self.xT_b, wb, self.wb_e], [self.psb[b]])
            return b
        for g in range(4):
            b = fm(512 + g * 128)
            self.ACT(self.zas[slot][:, g, :], self.ps[b][:, 0:W], AF.Silu, [self.psb[b]], [self.zas_b[slot]])
            yield
        for f in range(4):
            b = fm(2560 + f * 128)
            self.ACT(self.zbs[slot][:, f, :], self.ps[b][:, 0:W], AF.Silu, [self.psb[b]], [self.zbs_b[slot]])
            yield
        for c in range(TC):
            b = tm(2048, c)
            self.ACT(self.obs[slot][:, c, :], self.ps[b][:], AF.Sigmoid, [self.psb[b]], [self.obs_b[slot]])
            yield
        for c in range(TC):
            b = tm(1536, c)
            self.CP("act", self.vaug[slot][:, c, :, 0:128], self.ps[b][:].rearrange("p (h e) -> p h e", h=4),
                    [self.psb[b]], [self.vaug_b[slot]])
            yield
        b = bank()
        for c in range(TC):
            for k in range(8):
                self.MM(self.ps[b][:, c * 16:(c + 1) * 16], self.xT[:, k, c * 128:(c + 1) * 128], self.w_in[:, k, 3072:3088],
                        k == 0, k == 7, [self.xT_b, wb, self.wb_e], [self.psb[b]])
        gps = self.ps[b][:, 0:TC * 16].rearrange("p (c g) -> p c g", c=TC)
        gb = self.gate_b[slot]
        self.TT("dve", self.gig[slot][:], gps[:, :, 0:8], self.bgi[:].unsqueeze(1).broadcast_to([128, TC, 8]), ALU.add,
                [self.psb[b], self.cb], [gb])
        self.TT("dve", self.gtmp[:], gps[:, :, 8:16], self.bgf[:].unsqueeze(1).broadcast_to([128, TC, 8]), ALU.add,
                [self.psb[b], self.cb], [self.gtmp_b])
        yield
        self.ACT(self.gtmp[:], self.gtmp[:], AF.Exp, [self.gtmp_b], [self.gtmp_b], scale=-1.0)
        self.ACT(self.gnf[slot][:], self.gtmp[:], AF.Ln, [self.gtmp_b], [gb], bias=1.0)
        yield

    def l0_prologue_gen(self, slot):
        TC, W = self.TC, self.W
        xb, xbb = self.xbT[slot], self.xbT_b[slot]
        for f in range(4):
            ct, ctb = self.ctmp[f % 2], self.ctmp_b[f % 2]
            self.TS("dve", ct[:], xb[:, f, 1:W + 1], self.convw[:, 1, f:f + 1], self.convb[:, f:f + 1], ALU.mult, ALU.add,
                    [xbb, self.cb], [ctb])
            self.STT(ct[:], xb[:, f, 0:W], self.convw[:, 0, f:f + 1], ct[:], ALU.mult, ALU.add, [xbb, self.cb, ctb], [ctb])
            self.STT(ct[:], xb[:, f, 2:W + 2], self.convw[:, 2, f:f + 1], ct[:], ALU.mult, ALU.add, [xbb, self.cb, ctb], [ctb])
            yield
            self.ACT(self.xc[:, f, :], ct[:], AF.Silu, [ctb], [self.xcf_b[f]])
            self.ACT(self.xcb_l[slot][:, f, :], ct[:], AF.Silu, [ctb], [self.xcb_bl[slot][f]])
            self.ACT(self.skx_l[slot][:, f, :], self.xc[:, f, :], AF.Copy, [self.xcf_b[f], self.cb], [self.skx_bl[slot]], scale=self.skipv[:, f:f + 1])
            yield
        self.conv_decl += 1
        n = 0
        for h in range(4):
            for (dst, j) in ((self.qT_l[slot], 1), (self.kT_l[slot], 2)):
                b = 4 + (n % 2)
                n += 1
                self.MM(self.ps[b][:, 0:W], self.w_small[:, j, h, :], self.xcb_l[slot][:, h, :], True, True,
                        [self.wbuf, self.wb_e, self.xcb_bl[slot][h]], [self.psb[b]])
                self.CP("dve", dst[:, h, :], self.ps[b][:, 0:W], [self.psb[b]], [self.qk_bl[slot]])
            yield
        self.l0_gatevecs(slot, dirs=(0, 1))
        yield
        yield

    def l0_chunk_gen(self, off, T, i, slot, c):
        TC, W = self.TC, self.W
        nch = T // 128
        jc = i * TC + c
        par = jc % 2
        cs = slice(c * 128, (c + 1) * 128)
        va, vab = self.vaug[slot], self.vaug_b[slot]
        pT, pTb = self.pT[par], self.pT_b[par]
        nd, ndb = self.nd[par], self.nd_b[par]
        dd, ddb = self.dd[par], self.dd_b[par]
        hs, hs2, hsb, hs2b = self.hs[par], self.hs2[par], self.hs_b[par], self.hs2_b[par]
        hstat, hmv, hsc, hstb = self.hstat[par], self.hmv[par], self.hsc[par], self.hst_b[par]
        bs = self.bstn % 2
        self.bstn += 1
        self.DMA("sp", self.Bbf[bs][:].rearrange("p h e -> p (h e)"), self.bst_d[jc], "bst%d" % bs,
                 [self.bst_bufs[jc]], [self.Bbf_b[bs]])
        rs = self.resid_begin(self.d["x"], off + jc * 128, None)
        for dr in range(2):
            self.l0_vprime(slot, c, dr, par)
        yield
        for h in range(4):
            self.MM(self.ps[4][:, h * 128:(h + 1) * 128], self.kT_l[slot][:, h, cs], self.qT_l[slot][:, h, cs], True, True,
                    [self.qk_bl[slot]], [self.psb[4]])
        for dr in range(2):
            mask = (self.maskF, self.maskB)[dr]
            self.TT("dve", pT[:, dr, :, :], self.ps[4][:].rearrange("p (h e) -> p h e", h=4),
                    mask[:].unsqueeze(1).broadcast_to([128, 4, 128]), ALU.mult, [self.psb[4], self.cb], [pTb])
        yield
        while self.fdone < jc:
            yield
        for dr in range(2):
            Mbf, Mbfb = (self.Fbf, self.Fbf_b) if dr == 0 else (self.Bbf[bs], self.Bbf_b[bs])
            for h in range(4):
                combo = dr * 4 + h
                b, o = 5 + combo // 3, (combo % 3) * NV
                self.MM(self.ps[b][:, o:o + NV], pT[:, dr, h, :], self.vp[par][dr][:, h, :], True, False,
                        [pTb, self.vp_b[par][dr]], [self.psb[b]])
                self.MM(self.ps[b][:, o:o + NV], self.qT_l[slot][:, h, cs], Mbf[:, h, :], False, True,
                        [self.qk_bl[slot], Mbfb], [self.psb[b]])
        for bi, (b, n_) in enumerate(((5, 3), (6, 3), (7, 2))):
            pv_ = self.ps[b][:, 0:n_ * NV].rearrange("p (c e) -> p c e", e=NV)
            self.TT("dve", dd[:, bi * 3:bi * 3 + n_], pv_[:, :, 128], self.avec_l[slot][:, c, bi * 3:bi * 3 + n_], ALU.mult,
                    [self.psb[b], self.gv_bl[slot]], [ddb])
        self.STT(dd[:, 8:16], dd[:, 0:8], -1.0, dd[:, 0:8], ALU.mult, ALU.max, [ddb], [ddb])
        self.TS("dve", dd[:, 8:16], dd[:, 8:16], 1.0, None, ALU.max, None, [ddb], [ddb])
        self.S.op("dve", lambda e: e.reciprocal(out=dd[:, 16:24], in_=dd[:, 8:16]), [ddb], [ddb])
        self.TT("dve", dd[:, 16:24], dd[:, 16:24], self.avec_l[slot][:, c, :], ALU.mult, [ddb, self.gv_bl[slot]], [ddb])
        for bi, (b, n_) in enumerate(((5, 3), (6, 3), (7, 2))):
            pv_ = self.ps[b][:, 0:n_ * NV].rearrange("p (c e) -> p c e", e=NV)
            self.TT("dve", nd[:, bi * 3:bi * 3 + n_, :], pv_[:, :, 0:128],
                    dd[:, 16 + bi * 3:16 + bi * 3 + n_].unsqueeze(2).broadcast_to([128, n_, 128]), ALU.mult,
                    [self.psb[b], ddb], [ndb])
        yield
        self.l0_kprime(slot, c, par)
        yield
        self.l0_state_update(slot, c, 0, par, self.Fm, self.F_b, self.Fbf, self.Fbf_b)
        self.fdone = jc + 1
        yield
        self.TT("dve", hs[:], nd[:, 0:4, :], nd[:, 4:8, :], ALU.add, [ndb], [hsb])
        self.TT("dve", hs[:], hs[:], self.obs[slot][:, c, :].rearrange("p (h e) -> p h e", h=4), ALU.mult,
                [hsb, self.obs_b[slot]], [hsb])
        for h in range(4):
            self.S.op("dve", lambda e, h=h: e.bn_stats(out=hstat[:, h, :], in_=hs[:, h, :]), [hsb], [hstb])
        for h in range(4):
            self.S.op("dve", lambda e, h=h: e.bn_aggr(out=hmv[:, h, :], in_=hstat[:, h, :]), [hstb], [hstb])
        yield
        yield
        self.ACT(hsc[:, 0, :], hmv[:, :, 1], AF.Ln, [hstb], [hstb], bias=self.epsc[:, 0:1])
        self.ACT(hsc[:, 1, :], hsc[:, 0, :], AF.Exp, [hstb], [hstb], scale=-0.5)
        yield
        self.TT("dve", hs2[:], hs[:], hmv[:, :, 0].unsqueeze(2).broadcast_to([128, 4, 128]), ALU.subtract, [hsb, hstb], [hs2b])
        self.TT("dve", hs2[:], hs2[:], hsc[:, 1, :].unsqueeze(2).broadcast_to([128, 4, 128]), ALU.mult, [hs2b, hstb], [hs2b])
        yield
        for h in range(4):
            self.TR(self.ps[0][:, h * 128:(h + 1) * 128], hs2[:, h, :], self.ident_f[:], [hs2b, self.cb], [self.psb[0]])
        pv = self.ps[0][:].rearrange("p (h e) -> p h e", h=4)
        self.TT("dve", self.obT[:], pv, self.mhg[:].unsqueeze(2).broadcast_to([128, 4, 128]), ALU.mult, [self.psb[0], self.cb], [self.obT_b])
        self.TT("dve", self.obT[:], self.obT[:], self.skx_l[slot][:, :, cs], ALU.add, [self.obT_b, self.skx_bl[slot]], [self.obT_b])
        self.TT("dve", self.outbT[par][:], self.obT[:], self.zbs[slot][:, :, cs], ALU.mult, [self.obT_b, self.zbs_b[slot]], [self.outbT_b[par]])
        yield
        for g in range(4):
            blks = []
            if jc > 0:
                blks.append((jc - 1, 0))
            blks.append((jc, 1 if jc == 0 else (3 if jc == nch - 1 else 2)))
            if jc < nch - 1:
                blks.append((jc + 1, 4))
            for n_, (j2, blk) in enumerate(blks):
                i2, c2 = j2 // TC, j2 % TC
                s3 = i2 % 3
                self.MM(self.ps[1][:, g * 128:(g + 1) * 128], self.xa_tm[s3][:, c2, g * 128:(g + 1) * 128], self.PB[:, g, blk, :],
                        n_ == 0, n_ == len(blks) - 1, [self.xa_b[s3], self.cb], [self.psb[1]])
        self.CP("act", self.pooledT[par][:], self.ps[1][:].rearrange("p (h e) -> p h e", h=4), [self.psb[1]], [self.pooledT_b[par]])
        self.pool_done = max(self.pool_done, jc + 1)
        yield
        for g in range(4):
            self.MM(self.ps[7][:, g * 128:(g + 1) * 128], self.w_small[:, 0, g, :], self.pooledT[par][:, g, :], True, True,
                    [self.wbuf, self.wb_e, self.pooledT_b[par]], [self.psb[7]])
        self.TT("dve", self.oat[:], self.ps[7][:].rearrange("p (h e) -> p h e", h=4),
                self.pscale[:].unsqueeze(2).broadcast_to([128, 4, 128]), ALU.mult, [self.psb[7], self.cb], [self.oat_b])
        self.TT("dve", self.outaT[par][:], self.oat[:], self.zas[slot][:, :, cs], ALU.mult, [self.oat_b, self.zas_b[slot]], [self.outaT_b[par]])
        yield
        for hf in range(2):
            b = 2 + hf
            for f in range(8):
                lt = self.outaT[par][:, f, :] if f < 4 else self.outbT[par][:, f - 4, :]
                self.MM(self.ps[b][:], lt, self.w_out[:, f, hf * 512:(hf + 1) * 512], f == 0, f == 7,
                        [self.outaT_b[par], self.outbT_b[par], self.wbuf], [self.psb[b]])
            self.resid_half(rs, hf, b)
        yield
        dst = self.x1_d if 1 in self.layers else self.y_d
        dstb = self.x1_bufs if 1 in self.layers else None
        for _ in self.resid_finish_gen(rs, dst, off + jc * 128, dstb):
            yield

    def faster(self, g, r):
        while True:
            for _ in range(r):
                try:
                    next(g)
                except StopIteration:
                    return
            yield

    def run_gens(self, gens):
        gens = list(gens)
        while gens:
            for g in list(gens):
                try:
                    next(g)
                except StopIteration:
                    gens.remove(g)

    def l0_tile_gen(self, off, T, i, slot):
        if self.prologue_done_for != (off, i):
            for _ in self.l0_prologue_gen(slot):
                yield
        act = [self.l0_chunk_gen(off, T, i, slot, c) for c in range(self.TC)]
        while act:
            for g in list(act):
                try:
                    next(g)
                except StopIteration:
                    act.remove(g)
            yield

    def alloc_l1(self):
        if hasattr(self, "KT"):
            return
        TC, W = self.TC, self.W
        mc = self.maxch
        self.KT = self.sb([128, 2, mc * 128], BF16, "KT")
        self.KT_b = Buf()
        self.VA = self.sb([128, mc, 2, NV], BF16, "VA")
        self.VA_b = Buf()
        self.cosT = self.sb([128, mc, 2, 32], F32, "cosT")
        self.sinT = self.sb([128, mc, 2, 32], F32, "sinT")
        self.tab_b = Buf()
        self.ssq = self.sb([128, 16], F32, "ssq")
        self.ssq_b = Buf()
        self.junk = self.sb([128, 128], F32, "junk")
        self.junk_b = Buf()
        self.qn = self.sb([128, 8, 128], F32, "qn")
        self.qn_b = Buf()
        self.rt = [self.sb([128, 8, 2, 32], F32, "rt%d" % i) for i in range(2)]
        self.rt_b = Buf()
        self.qr = [self.sb([128, 8, 128], BF16, "qr%d" % i) for i in range(2)]
        self.qr_b = [Buf() for _ in range(2)]
        self.qrn = 0
        self.QT = [self.sb([128, 8, W], BF16, "QT%d" % i) for i in range(2)]
        self.QT_b = [Buf() for _ in range(2)]
        self.zs = [self.sb([128, TC, D], F32, "zs%d" % i) for i in range(2)]
        self.zs_b = [Buf() for _ in range(2)]
        self.PT = [self.sb([128, 512], BF16, "PT%d" % i) for i in range(3)]
        self.PT_b = [Buf() for _ in range(3)]
        self.og = [self.sb([128, TC, D], BF16, "og%d" % i) for i in range(2)]
        self.og_b = [Buf(), Buf()]
        self.ogT = self.sb([128, 8, 128], BF16, "ogT")
        self.ogT_b = Buf()
        self.rden = self.sb([128, 8], F32, "rden")
        self.rden_b = Buf()
        self.ptn = 0
        self.ktb = []
        for i in range(2):
            self.ktb.append((self.sb([128, 16], F32, "ssqk%d" % i), Buf(), self.sb([128, 2, 128], F32, "qnk%d" % i), Buf(),
                             [self.sb([128, 2, 2, 32], F32, "rtk%d_%d" % (i, j)) for j in range(2)], Buf(),
                             self.sb([128, 128], F32, "junkk%d" % i), Buf()))
        self.build_rope_tables()

    def build_rope_tables(self):
        S = self.S
        mc = self.maxch
        tb = self.tab_b
        R, Wr = [tb], [tb]
        A = self.cosT
        pidx = self.sb([128, 4], F32, "pidx")
        inv = self.sb([128, 32], F32, "inv")
        prow = self.sb([128, mc], F32, "prow")
        ne = mc * 64
        R = [tb, self.zs_b[0], self.zs_b[1], self.KT_b]
        Wr = R

        class _V:
            def __init__(self, ap):
                self.ap = ap

            def __getitem__(self, k):
                if isinstance(k, slice):
                    return self.ap
                return self.ap[k]
        ang = _V(self.zs[0][:].rearrange("p c d -> p (c d)")[:, 0:ne].rearrange("p (m a f) -> p m a f", a=2, f=32))
        kf = _V(self.zs[1][:].rearrange("p c d -> p (c d)")[:, 0:ne].rearrange("p (m a f) -> p m a f", a=2, f=32))
        ki = _V(self.KT[:].rearrange("p k t -> p (k t)").bitcast(I32)[:, 0:ne].rearrange("p (m a f) -> p m a f", a=2, f=32))
        S.op("pool", lambda e: e.iota(pidx[:, 0:1], pattern=[[0, 1]], base=0, channel_multiplier=1,
                                      allow_small_or_imprecise_dtypes=True), R, Wr)
        self.TS("dve", pidx[:, 1:2], pidx[:, 0:1], 64.0, None, ALU.is_ge, None, R, Wr)
        self.STT(pidx[:, 2:3], pidx[:, 1:2], -64.0, pidx[:, 0:1], ALU.mult, ALU.add, R, Wr)
        S.op("pool", lambda e: e.iota(inv[:], pattern=[[1, 32]], base=0, channel_multiplier=0,
                                      allow_small_or_imprecise_dtypes=True), R, Wr)
        self.ACT(inv[:], inv[:], AF.Exp, R, Wr, scale=-math.log(10000.0) / 32.0)
        S.op("pool", lambda e: e.iota(prow[:], pattern=[[2, mc]], base=0, channel_multiplier=0,
                                      allow_small_or_imprecise_dtypes=True), R, Wr)
        self.TS("dve", prow[:], prow[:], pidx[:, 1:2], None, ALU.add, None, R, Wr)
        self.TT("dve", ang[:, :, 0, :], prow[:].unsqueeze(2).broadcast_to([128, mc, 32]),
                inv[:].unsqueeze(1).broadcast_to([128, mc, 32]), ALU.mult, R, Wr)
        self.TS("dve", ang[:, :, 1, :], inv[:].unsqueeze(1).broadcast_to([128, mc, 32]), pidx[:, 2:3], None, ALU.mult, None, R, Wr)
        TWO_PI = 2.0 * math.pi
        for tab, shift in ((self.sinT, 0.0), (self.cosT, math.pi / 2.0)):
            self.TS("dve", tab[:], ang[:], shift, None, ALU.add, None, R, Wr)
            self.TS("dve", kf[:], tab[:], 1.0 / TWO_PI, None, ALU.mult, None, R, Wr)
            self.CP("dve", ki[:], kf[:], R, Wr)
            self.CP("dve", kf[:], ki[:], R, Wr)
            self.STT(tab[:], kf[:], -TWO_PI, tab[:], ALU.mult, ALU.add, R, Wr)
            self.TS("dve", kf[:], tab[:], math.pi, None, ALU.is_gt, None, R, Wr)
            self.STT(tab[:], kf[:], -TWO_PI, tab[:], ALU.mult, ALU.add, R, Wr)
            self.TS("dve", kf[:], tab[:], -math.pi, None, ALU.is_lt, None, R, Wr)
            self.STT(tab[:], kf[:], TWO_PI, tab[:], ALU.mult, ALU.add, R, Wr)
            self.TS("dve", tab[:], tab[:], math.pi, -math.pi, ALU.min, ALU.max, R, Wr)
            self.ACT(tab[:], tab[:], AF.Sin, R, Wr)

    def l1_norm_rope_gen(self, psrc_banks, nh, gain, jc, dst, dstb, tb=None):
        if tb is None:
            tb = (self.ssq, self.ssq_b, self.qn, self.qn_b, self.rt, self.rt_b, self.junk, self.junk_b)
        ssq, ssq_b, qn, qn_b, rt, rt_b, junk, junk_b = tb
        hh = 0
        for (b, n_) in psrc_banks:
            for j in range(n_):
                self.ACT(junk[:], self.ps[b][:, j * 128:(j + 1) * 128], AF.Square, [self.psb[b]], [junk_b, ssq_b],
                         accum=ssq[:, hh + j:hh + j + 1])
            hh += n_
        self.ACT(ssq[:, 8:8 + nh], ssq[:, 0:nh], AF.Ln, [ssq_b], [ssq_b], bias=self.epsc[:, 1:2], scale=1.0 / 128.0)
        self.ACT(ssq[:, 8:8 + nh], ssq[:, 8:8 + nh], AF.Exp, [ssq_b], [ssq_b], scale=-0.5)
        yield
        yield
        hh = 0
        for (b, n_) in psrc_banks:
            for j in range(n_):
                self.STT(qn[:, hh + j, :], self.ps[b][:, j * 128:(j + 1) * 128], ssq[:, 8 + hh + j:9 + hh + j], gain[:],
                         ALU.mult, ALU.mult, [self.psb[b], ssq_b, self.cb], [qn_b])
            hh += n_
        qv = qn[:, 0:nh, :].rearrange("p h (a t f) -> p h a t f", a=2, t=2)
        dv = dst.rearrange("p h (a t f) -> p h a t f", a=2, t=2)
        x1, x2 = qv[:, :, :, 0, :], qv[:, :, :, 1, :]
        cs = self.cosT[:, jc, :, :].unsqueeze(1).broadcast_to([128, nh, 2, 32])
        sn = self.sinT[:, jc, :, :].unsqueeze(1).broadcast_to([128, nh, 2, 32])
        t = [r[:, 0:nh, :, :] for r in rt]
        Rq = [qn_b, self.tab_b]
        self.TT("dve", t[0], x1, cs, ALU.mult, Rq, [rt_b])
        self.TT("dve", t[1], x2, sn, ALU.mult, Rq, [rt_b])
        self.TT("dve", dv[:, :, :, 0, :], t[0], t[1], ALU.subtract, [rt_b], [dstb])
        self.TT("dve", t[0], x2, cs, ALU.mult, Rq + [rt_b], [rt_b])
        self.TT("dve", t[1], x1, sn, ALU.mult, Rq + [rt_b], [rt_b])
        self.TT("dve", dv[:, :, :, 1, :], t[0], t[1], ALU.add, [rt_b], [dstb])

    def l1_norm_rope(self, psrc_banks, nh, gain, jc, dst, dstb):
        for _ in self.l1_norm_rope_gen(psrc_banks, nh, gain, jc, dst, dstb):
            pass

    def l1_q_gen(self, src, srcb, off, i, slot, qs):
        TC, W = self.TC, self.W
        wb = self.wbuf
        for c in range(TC):
            jc = i * TC + c
            q_ = self.qrn % 2
            self.qrn += 1
            pst = self.ps[2][:].bitcast(BF16)
            for k in range(8):
                self.TR(pst[:, k * 128:(k + 1) * 128], self.xbf[slot][:, c, k * 128:(k + 1) * 128], self.ident_b[:],
                        [self.xbf_c[slot][c], self.cb], [self.psb[2]])
            yield
            self.CP("dve", self.xT[:, :, c * 128:(c + 1) * 128], pst.rearrange("p (k t) -> p k t", k=8), [self.psb[2]], [self.xT_b])
            yield
            yield
            for hf in range(2):
                for k in range(8):
                    self.MM(self.ps[2 + hf][:], self.xT[:, k, c * 128:(c + 1) * 128], self.w_in[:, k, hf * 512:(hf + 1) * 512],
                            k == 0, k == 7, [self.xT_b, wb, self.wb_e], [self.psb[2 + hf]])
                    if k % 4 == 3:
                        yield
            yield
            for _ in self.l1_norm_rope_gen([(2, 4), (3, 4)], 8, self.qng, jc, self.qr[q_][:, 0:8, :], self.qr_b[q_]):
                yield
            yield
            for hf in range(2):
                for k in range(8):
                    self.MM(self.ps[2 + hf][:], self.xT[:, k, c * 128:(c + 1) * 128],
                            self.w_in[:, k, 1536 + hf * 512:1536 + (hf + 1) * 512], k == 0, k == 7, [self.xT_b, wb, self.wb_e], [self.psb[2 + hf]])
                    if k % 4 == 3:
                        yield
            yield
            for hf in range(2):
                self.CP("dve", self.zs[qs][:, c, hf * 512:(hf + 1) * 512], self.ps[2 + hf][:], [self.psb[2 + hf]], [self.zs_b[qs]])
            yield
            yield
            pst = self.ps[2][:].bitcast(BF16)
            for h in range(8):
                self.TR(pst[:, h * 128:(h + 1) * 128], self.qr[q_][:, h, :], self.ident_b[:], [self.qr_b[q_], self.cb], [self.psb[2]])
            yield
            yield
            self.CP("dve", self.QT[qs][:, :, c * 128:(c + 1) * 128], pst.rearrange("p (k t) -> p k t", k=8), [self.psb[2]], [self.QT_b[qs]])
            yield
            yield
        zall = self.zs[qs][:].rearrange("p c d -> p (c d)")
        self.ACT(zall, zall, AF.Silu, [self.zs_b[qs]], [self.zs_b[qs]])
        yield

    def l1_xload_gen(self, src, srcb, off, i, slot):
        self.load_x(src, off + i * self.W, slot, srcb)
        for _ in range(16):
            yield

    def l1_epi_gen(self, src, srcb, off, i, os_):
        TC = self.TC
        wb = self.wbuf
        for c in range(TC):
            jc = i * TC + c
            tok = off + jc * 128
            rs = self.resid_begin(src, tok, srcb)
            pst = self.ps[0][:].bitcast(BF16)
            for f in range(8):
                self.TR(pst[:, f * 128:(f + 1) * 128], self.og[os_][:, c, f * 128:(f + 1) * 128], self.ident_b[:],
                        [self.og_b[os_], self.cb], [self.psb[0]])
            yield
            yield
            self.CP("dve", self.ogT[:], pst.rearrange("p (k t) -> p k t", k=8), [self.psb[0]], [self.ogT_b])
            yield
            yield
            for hf in range(2):
                for f in range(8):
                    self.MM(self.ps[0][:], self.ogT[:, f, :], self.w_out[:, f, hf * 512:(hf + 1) * 512], f == 0, f == 7,
                            [self.ogT_b, wb], [self.psb[0]])
                yield
                yield
                yield
                yield
                self.resid_half(rs, hf, 0)
                yield
                yield
            for _ in self.resid_finish_gen(rs, self.y_d, tok, None):
                yield

    def l1_epi_steps(self, src, srcb, off, i, os_):
        return [self.l1_epi_gen(src, srcb, off, i, os_)]

    def l1_q_steps(self, src, srcb, off, i, slot, qs):
        return [self.l1_xload_gen(src, srcb, off, i, slot), self.l1_q_gen(src, srcb, off, i, slot, qs)]

    def l1_att_stage(self, nch, qs, os_, sideA, sideB):
        TC, W = self.TC, self.W
        KB = 512 // W
        nkb = nch // KB
        scale = DH ** -0.5
        iters = [(h, kb) for h in range(8) for kb in range(nkb)]
        SB = (6, 7, 1)
        kA = -(-90 // max(1, len(iters) - 2))
        kB = -(-60 // max(1, len(iters) - 2))

        def emit_S(n):
            h, kb = iters[n]
            kv = h // 4
            sb_ = SB[n % 3]
            for j in range(KB):
                kc = kb * KB + j
                self.MM(self.ps[sb_][:, j * W:(j + 1) * W], self.KT[:, kv, kc * 128:(kc + 1) * 128], self.QT[qs][:, h, :], True, True,
                        [self.KT_b, self.QT_b[qs]], [self.psb[sb_]])
        emit_S(0)
        if len(iters) > 1:
            emit_S(1)
        for n, (h, kb) in enumerate(iters):
            kv = h // 4
            if n + 2 < len(iters):
                emit_S(n + 2)
            sb_ = SB[n % 3]
            pi = self.ptn % 3
            self.ptn += 1
            self.ACT(self.PT[pi][:], self.ps[sb_][:], AF.Exp, [self.psb[sb_]], [self.PT_b[pi]], scale=scale)
            ob = 4 + (h % 2)
            for j in range(KB):
                kc = kb * KB + j
                for qc in range(TC):
                    self.MM(self.ps[ob][:, qc * NV:(qc + 1) * NV], self.PT[pi][:, j * W + qc * 128:j * W + (qc + 1) * 128],
                            self.VA[:, kc, kv, :], kc == 0 and qc == 0, kc == nch - 1, [self.PT_b[pi], self.VA_b], [self.psb[ob]], skip=True)
            if kb == nkb - 1:
                for qc in range(TC):
                    rc = (h % 2) * TC + qc
                    self.S.op("dve", lambda e, ob=ob, rc=rc, qc=qc: e.reciprocal(out=self.rden[:, rc:rc + 1],
                                                                              in_=self.ps[ob][:, qc * NV + 128:qc * NV + 129]),
                              [self.psb[ob]], [self.rden_b])
                    self.STT(self.og[os_][:, qc, h * 128:(h + 1) * 128], self.ps[ob][:, qc * NV:qc * NV + 128], self.rden[:, rc:rc + 1],
                             self.zs[qs][:, qc, h * 128:(h + 1) * 128], ALU.mult, ALU.mult, [self.psb[ob], self.rden_b, self.zs_b[qs]],
                             [self.og_b[os_]])
            for (lst, k_) in ((sideA, kA), (sideB, kB)):
                for _k in range(k_):
                    if lst:
                        try:
                            next(lst[0])
                        except StopIteration:
                            lst.pop(0)
        for lst in (sideB, sideA):
            while lst:
                for _ in lst.pop(0):
                    pass

    def layer1_seq(self, off, T):
        self.alloc_l1()
        TC, W = self.TC, self.W
        nt = T // W
        nch = T // 128
        wb = self.wbuf
        src = self.x1_d if 0 in self.layers else self.d["x"]
        srcb = self.x1_bufs if 0 in self.layers else None
        self.MS("pool", self.VA[:, :, :, 128:129], 1.0, [], [self.VA_b])
        def kv_chunk_gen(i, c):
            jc = i * TC + c
            par = jc % 2
            b = 2 + par
            for k in range(8):
                self.MM(self.ps[b][:], self.xT[:, k, c * 128:(c + 1) * 128], self.w_in[:, k, 1024:1536], k == 0, k == 7,
                        [self.xT_b, self.wb_e], [self.psb[b]])
            self.CP("act", self.VA[:, jc, :, 0:128], self.ps[b][:, 256:512].rearrange("p (h e) -> p h e", h=2),
                    [self.psb[b]], [self.VA_b])
            for _ in self.l1_norm_rope_gen([(b, 2)], 2, self.kng, jc, self.qr[par][:, 0:2, :], self.qr_b[par], self.ktb[par]):
                yield
            yield
            pst = self.ps[par][:].bitcast(BF16)
            for kv in range(2):
                self.TR(pst[:, kv * 128:(kv + 1) * 128], self.qr[par][:, kv, :], self.ident_b[:], [self.qr_b[par], self.cb], [self.psb[par]])
            self.CP("act", self.KT[:, :, jc * 128:(jc + 1) * 128], pst[:, 0:256].rearrange("p (k t) -> p k t", k=2),
                    [self.psb[par]], [self.KT_b])
            yield
        self.load_x(src, off, 0, srcb)
        for i in range(nt):
            slot = i % 2
            for c in range(TC):
                self.xT_chunk(c, c % 2, slot)
            if i + 1 < nt:
                self.load_x(src, off + (i + 1) * W, 1 - slot, srcb)
            self.run_gens([kv_chunk_gen(i, c) for c in range(TC)])
        for g_ in self.l1_q_steps(src, srcb, off, 0, 0, 0):
            for _ in g_:
                pass
        for i in range(nt):
            sideA, sideB = [], []
            if i + 1 < nt:
                sideA.append(self.l1_xload_gen(src, srcb, off, i + 1, (i + 1) % 2))
                sideA.append(self.l1_q_gen(src, srcb, off, i + 1, (i + 1) % 2, (i + 1) % 2))
            if i >= 1:
                sideB += self.l1_epi_steps(src, srcb, off, i - 1, (i - 1) % 2)
            self.l1_att_stage(nch, i % 2, i % 2, sideA, sideB)
        for g_ in self.l1_epi_steps(src, srcb, off, nt - 1, (nt - 1) % 2):
            for _ in g_:
                pass


def build_program(seq_lens, TC=2, layers=(0, 1)):
    k = K(seq_lens, TC=TC, layers=layers)
    return k.build()


_INPUT_NAMES = ["w_in_even", "w_pool", "pool_scale", "conv_w", "conv_b", "w_q_m", "w_k_m", "b_gate_i", "b_gate_f",
                "mh_norm_g", "skip", "w_out_even", "w_in_odd", "q_norm_g", "k_norm_g", "w_out_odd", "ln_g", "ln_b"]
_SHAPES = {"w_in_even": (D, IN0), "w_pool": (4, 128, 128), "pool_scale": (512,), "conv_w": (3, 512), "conv_b": (512,),
           "w_q_m": (4, 128, 128), "w_k_m": (4, 128, 128), "b_gate_i": (8,), "b_gate_f": (8,), "mh_norm_g": (512,),
           "skip": (512,), "w_out_even": (D, D), "w_in_odd": (D, IN1), "q_norm_g": (128,), "k_norm_g": (128,),
           "w_out_odd": (D, D), "ln_g": (2, D), "ln_b": (2, D)}


def weight_map(inputs):
    m = {}
    for nm in _INPUT_NAMES:
        m[nm] = np.ascontiguousarray(np.asarray(inputs[nm], dtype=np.float32).reshape(_SHAPES[nm]))
    return m


def kernel(**inputs):
    xp = np.asarray(inputs["x_prompt"], dtype=np.float32)
    xs = np.asarray(inputs["x_sample"], dtype=np.float32)
    n = 8
    nc = build_program([4096, 2048, 2048], TC=2, layers=(0, 1))
    wm = weight_map(inputs)
    in_maps = []
    for c in range(n):
        xcat = np.concatenate([xp[c], xs[2 * c], xs[2 * c + 1]], axis=0)
        m = dict(wm)
        m["x"] = np.ascontiguousarray(xcat)
        in_maps.append(m)
    res = run_bass_kernel_spmd(nc, in_maps, core_ids=list(range(n)))
    yp = np.empty_like(xp)
    ys = np.empty_like(xs)
    for c in range(n):
        y = res.results[c]["y"]
        yp[c] = y[0:4096]
        ys[2 * c] = y[4096:6144]
        ys[2 * c + 1] = y[6144:8192]
    return (yp, ys)
```
